# Optimizing a Trainium2 kernel written in Bass

```python
import jax, jax.numpy as jnp
from jax import lax
import numpy as np

D_MODEL = 1024
BATCH = 2
SEQ = 16384
DEPTH = 1
DEC_BATCH = 16
DEC_SEQ = 4096
PAST_LEN = 128

GRID_W = 64
NA_HEADS = 8
NA_HEAD_DIM = 64
NA_WIDTH = NA_HEADS * NA_HEAD_DIM
NA_KH_MAX = 8
NA_KW = 16
NA_SLAB_W = 2 * NA_KW
MLA_HEADS = 8
MLA_NOPE_DIM = 64
MLA_ROPE_DIM = 32
MLA_QK_DIM = MLA_NOPE_DIM + MLA_ROPE_DIM
MLA_V_DIM = 64
MLA_Q_RANK = 384
MLA_KV_RANK = 256
MLA_WIDTH = MLA_HEADS * MLA_V_DIM
ROPE_BASE = 10000.0
Q_BLOCK = 128
D_FF = 2816
LN_EPS = 1e-5
RMS_EPS = 1e-6
IN_COLS = 3 * NA_WIDTH + MLA_Q_RANK + MLA_KV_RANK + MLA_ROPE_DIM + 2 * D_MODEL

kernel_name = "hybrid_na_mla_macaron_deepnorm_encoder"


def layer_norm(x, g, b):
    xf = x.astype(jnp.float32)
    mu = jnp.mean(xf, axis=-1, keepdims=True)
    var = jnp.mean(jnp.square(xf - mu), axis=-1, keepdims=True)
    return ((xf - mu) * lax.rsqrt(var + LN_EPS) * g + b).astype(x.dtype)


def rms_norm(x, g):
    xf = x.astype(jnp.float32)
    return (xf * lax.rsqrt(jnp.mean(jnp.square(xf), axis=-1, keepdims=True) + RMS_EPS) * g).astype(x.dtype)


def swiglu_ffn(x, w_in, w_out):
    a, u = jnp.split(x @ w_in, 2, axis=-1)
    return (jax.nn.silu(a) * u) @ w_out


def rope_tables(n):
    inv = 1.0 / (ROPE_BASE ** (jnp.arange(0, MLA_ROPE_DIM, 2, dtype=jnp.float32) / MLA_ROPE_DIM))
    ang = jnp.arange(n, dtype=jnp.float32)[:, None] * inv[None, :]
    return jnp.cos(ang), jnp.sin(ang)


def apply_rope(x, cos, sin):
    x1, x2 = jnp.split(x.astype(jnp.float32), 2, axis=-1)
    return jnp.concatenate([x1 * cos - x2 * sin, x1 * sin + x2 * cos], axis=-1).astype(x.dtype)


def neighbourhood_attention(q, k, v, rpb):
    b, n = q.shape[0], q.shape[1]
    rows = n // GRID_W
    kh = min(NA_KH_MAX, rows)
    ncb = GRID_W // NA_KW
    qg = q.reshape(b, rows, ncb, NA_KW, NA_HEADS, NA_HEAD_DIM)
    kg = k.reshape(b, rows, GRID_W, NA_HEADS, NA_HEAD_DIM)
    vg = v.reshape(b, rows, GRID_W, NA_HEADS, NA_HEAD_DIM)
    r = jnp.arange(rows)
    row_start = jnp.clip(r - kh // 2, 0, rows - kh)
    key_rows = row_start[:, None] + jnp.arange(kh)[None, :]
    j = jnp.arange(ncb)
    slab_start = jnp.clip(j * NA_KW - NA_KW // 2, 0, GRID_W - NA_SLAB_W)
    key_cols = slab_start[:, None] + jnp.arange(NA_SLAB_W)[None, :]
    ridx = key_rows[:, None, :, None]
    cidx = key_cols[None, :, None, :]
    k_slab = kg[:, ridx, cidx]
    v_slab = vg[:, ridx, cidx]
    q_cols = j[:, None] * NA_KW + jnp.arange(NA_KW)[None, :]
    win_start = jnp.clip(q_cols - NA_KW // 2, 0, GRID_W - NA_KW)
    kc = key_cols[:, None, :]
    in_win = (kc >= win_start[..., None]) & (kc < win_start[..., None] + NA_KW)
    dc = jnp.clip(kc - q_cols[:, :, None] + NA_KW - 1, 0, 2 * NA_KW - 2)
    dr = key_rows - r[:, None] + NA_KH_MAX - 1
    bias = rpb[:, dr[:, None, None, :, None], dc[None, :, :, None, :]]
    s = jnp.einsum('brjqhd,brjkwhd->bhrjqkw', qg, k_slab,
                   preferred_element_type=jnp.float32) * (NA_HEAD_DIM ** -0.5)
    s = s + bias[None].astype(jnp.float32)
    s = jnp.where(in_win[:, :, None, :], s, -jnp.inf)
    p = jax.nn.softmax(s, axis=(-2, -1))
    out = jnp.einsum('bhrjqkw,brjkwhd->brjqhd', p.astype(v.dtype), v_slab)
    return out.reshape(b, n, NA_WIDTH)


def latent_attention(c_q, c_kv, k_rope, q_norm_g, kv_norm_g, w_uq, w_ukv):
    b, n = c_q.shape[0], c_q.shape[1]
    q = (rms_norm(c_q, q_norm_g) @ w_uq).reshape(b, n, MLA_HEADS, MLA_QK_DIM)
    kv = (rms_norm(c_kv, kv_norm_g) @ w_ukv).reshape(b, n, MLA_HEADS, MLA_NOPE_DIM + MLA_V_DIM)
    q_nope, q_rope = q[..., :MLA_NOPE_DIM], q[..., MLA_NOPE_DIM:]
    k_nope, v = kv[..., :MLA_NOPE_DIM], kv[..., MLA_NOPE_DIM:]
    cos, sin = rope_tables(n)
    q_rope = apply_rope(q_rope, cos[:, None, :], sin[:, None, :])
    k_rope = apply_rope(k_rope, cos, sin)
    nb = n // Q_BLOCK
    qn_blocks = q_nope.reshape(b, nb, Q_BLOCK, MLA_HEADS, MLA_NOPE_DIM).transpose(1, 0, 2, 3, 4)
    qr_blocks = q_rope.reshape(b, nb, Q_BLOCK, MLA_HEADS, MLA_ROPE_DIM).transpose(1, 0, 2, 3, 4)
    scale = MLA_QK_DIM ** -0.5

    def attend(blk):
        qn, qr = blk
        s = (jnp.einsum('bqhd,bkhd->bhqk', qn, k_nope, preferred_element_type=jnp.float32)
             + jnp.einsum('bqhr,bkr->bhqk', qr, k_rope, preferred_element_type=jnp.float32)) * scale
        p = jax.nn.softmax(s, axis=-1)
        return jnp.einsum('bhqk,bkhd->bqhd', p.astype(v.dtype), v)

    out = lax.map(attend, (qn_blocks, qr_blocks))
    return out.transpose(1, 0, 2, 3, 4).reshape(b, n, MLA_WIDTH)


def token_mixer(x, w_in, b_gate, na_rpb, q_norm_g, kv_norm_g, w_uq, w_ukv, w_na_o, w_mla_o, w_out):
    offs = np.cumsum([NA_WIDTH, NA_WIDTH, NA_WIDTH, MLA_Q_RANK, MLA_KV_RANK, MLA_ROPE_DIM]).tolist()
    h = x @ w_in
    na_q, na_k, na_v, c_q, c_kv, k_rope, gate_logits = jnp.split(h, offs, axis=-1)
    b, n = x.shape[0], x.shape[1]
    hs = (b, n, NA_HEADS, NA_HEAD_DIM)
    y_a = neighbourhood_attention(na_q.reshape(hs), na_k.reshape(hs), na_v.reshape(hs), na_rpb) @ w_na_o
    y_b = latent_attention(c_q, c_kv, k_rope, q_norm_g, kv_norm_g, w_uq, w_ukv) @ w_mla_o
    g = jax.nn.sigmoid((gate_logits + b_gate).astype(jnp.float32)).astype(x.dtype)
    g_a, g_b = g[..., :D_MODEL], g[..., D_MODEL:]
    return (g_a * y_a + g_b * y_b) @ w_out


def encoder_layer(x, ffn1_w_in, ffn1_w_out, ln1_g, ln1_b, w_in, b_gate, na_rpb, q_norm_g, kv_norm_g,
                  w_uq, w_ukv, w_na_o, w_mla_o, w_out, ln2_g, ln2_b, ffn2_w_in, ffn2_w_out, ln3_g, ln3_b):
    alpha = (2.0 * DEPTH) ** 0.25
    x = layer_norm(alpha * x + 0.5 * swiglu_ffn(x, ffn1_w_in, ffn1_w_out), ln1_g, ln1_b)
    x = layer_norm(alpha * x + token_mixer(x, w_in, b_gate, na_rpb, q_norm_g, kv_norm_g, w_uq, w_ukv,
                                           w_na_o, w_mla_o, w_out), ln2_g, ln2_b)
    x = layer_norm(alpha * x + 0.5 * swiglu_ffn(x, ffn2_w_in, ffn2_w_out), ln3_g, ln3_b)
    return x


def run_trunk(x, ffn1_w_in, ffn1_w_out, ln1_g, ln1_b, w_in, b_gate, na_rpb, q_norm_g, kv_norm_g,
              w_uq, w_ukv, w_na_o, w_mla_o, w_out, ln2_g, ln2_b, ffn2_w_in, ffn2_w_out, ln3_g, ln3_b):
    for l in range(DEPTH):
        x = encoder_layer(x, ffn1_w_in[l], ffn1_w_out[l], ln1_g[l], ln1_b[l], w_in[l], b_gate[l], na_rpb[l],
                          q_norm_g[l], kv_norm_g[l], w_uq[l], w_ukv[l], w_na_o[l], w_mla_o[l], w_out[l],
                          ln2_g[l], ln2_b[l], ffn2_w_in[l], ffn2_w_out[l], ln3_g[l], ln3_b[l])
    return x


def setup_inputs(seed: int = 0) -> dict:
    key = jax.random.key(seed)
    ks = jax.random.split(key, 24)
    beta = (8.0 * DEPTH) ** -0.25
    f32 = jnp.float32

    def dense(k, shape, fan_in, scale=1.0):
        return jax.random.normal(k, shape, f32) * (fan_in ** -0.5) * scale

    def gain(k, shape):
        return 1.0 + 0.05 * jax.random.normal(k, shape, f32)

    def small(k, shape, s):
        return s * jax.random.normal(k, shape, f32)

    in_col_scale = jnp.concatenate([jnp.ones((2 * NA_WIDTH,), f32), jnp.full((NA_WIDTH,), beta, f32),
                                    jnp.ones((IN_COLS - 3 * NA_WIDTH,), f32)])
    ukv_col_scale = jnp.tile(jnp.concatenate([jnp.ones((MLA_NOPE_DIM,), f32),
                                              jnp.full((MLA_V_DIM,), beta, f32)]), MLA_HEADS)
    return {
        "x_prompt": jax.random.normal(ks[0], (BATCH, SEQ, D_MODEL), f32),
        "x_sample": jax.random.normal(ks[1], (DEC_BATCH, DEC_SEQ, D_MODEL), f32),
        "ffn1_w_in": dense(ks[2], (DEPTH, D_MODEL, 2 * D_FF), D_MODEL),
        "ffn1_w_out": dense(ks[3], (DEPTH, D_FF, D_MODEL), D_FF, beta),
        "ln1_g": gain(ks[4], (DEPTH, D_MODEL)),
        "ln1_b": small(ks[5], (DEPTH, D_MODEL), 0.02),
        "w_in": dense(ks[6], (DEPTH, D_MODEL, IN_COLS), D_MODEL) * in_col_scale,
        "b_gate": small(ks[7], (DEPTH, 2 * D_MODEL), 0.1),
        "na_rpb": small(ks[8], (DEPTH, NA_HEADS, 2 * NA_KH_MAX - 1, 2 * NA_KW - 1), 0.02),
        "q_norm_g": gain(ks[9], (DEPTH, MLA_Q_RANK)),
        "kv_norm_g": gain(ks[10], (DEPTH, MLA_KV_RANK)),
        "w_uq": dense(ks[11], (DEPTH, MLA_Q_RANK, MLA_HEADS * MLA_QK_DIM), MLA_Q_RANK),
        "w_ukv": dense(ks[12], (DEPTH, MLA_KV_RANK, MLA_HEADS * (MLA_NOPE_DIM + MLA_V_DIM)), MLA_KV_RANK) * ukv_col_scale,
        "w_na_o": dense(ks[13], (DEPTH, NA_WIDTH, D_MODEL), NA_WIDTH),
        "w_mla_o": dense(ks[14], (DEPTH, MLA_WIDTH, D_MODEL), MLA_WIDTH),
        "w_out": dense(ks[15], (DEPTH, D_MODEL, D_MODEL), D_MODEL, beta),
        "ln2_g": gain(ks[16], (DEPTH, D_MODEL)),
        "ln2_b": small(ks[17], (DEPTH, D_MODEL), 0.02),
        "ffn2_w_in": dense(ks[18], (DEPTH, D_MODEL, 2 * D_FF), D_MODEL),
        "ffn2_w_out": dense(ks[19], (DEPTH, D_FF, D_MODEL), D_FF, beta),
        "ln3_g": gain(ks[20], (DEPTH, D_MODEL)),
        "ln3_b": small(ks[21], (DEPTH, D_MODEL), 0.02),
    }


def reference(x_prompt, x_sample, ffn1_w_in, ffn1_w_out, ln1_g, ln1_b, w_in, b_gate, na_rpb, q_norm_g,
              kv_norm_g, w_uq, w_ukv, w_na_o, w_mla_o, w_out, ln2_g, ln2_b, ffn2_w_in, ffn2_w_out,
              ln3_g, ln3_b):
    y_prompt = run_trunk(x_prompt, ffn1_w_in, ffn1_w_out, ln1_g, ln1_b, w_in, b_gate, na_rpb, q_norm_g,
                         kv_norm_g, w_uq, w_ukv, w_na_o, w_mla_o, w_out, ln2_g, ln2_b, ffn2_w_in,
                         ffn2_w_out, ln3_g, ln3_b)
    y_sample = run_trunk(x_sample, ffn1_w_in, ffn1_w_out, ln1_g, ln1_b, w_in, b_gate, na_rpb, q_norm_g,
                         kv_norm_g, w_uq, w_ukv, w_na_o, w_mla_o, w_out, ln2_g, ln2_b, ffn2_w_in,
                         ffn2_w_out, ln3_g, ln3_b)
    return (y_prompt, y_sample)
```

```python
import bisect
import contextlib
import numpy as np
import concourse.bass as bass
import concourse.mybir as mybir
from concourse.bass_utils import run_bass_kernel_spmd

F32 = mybir.dt.float32
BF16 = mybir.dt.bfloat16
AF = mybir.ActivationFunctionType
ALU = mybir.AluOpType

D = 1024
DFF = 2816
G = 512
NCORES = 8
ALPHA = 2.0 ** 0.25
LN_EPS = 1e-5
RMS_EPS = 1e-6
NEG = -30000.0
IN_COLS = 4256
COMPUTE = ("pe", "act", "dve", "pool")


class Op:
    __slots__ = ("eng", "fn", "deps", "dma_key", "dma_cnt", "needs_inc", "inc_cnt", "idx", "wdeps", "gpos", "throttle")


class Prog:
    def __init__(self, nc):
        self.nc = nc
        self.ops = []
        self.last_w = {}
        self.readers = {}
        self.dma_total = {}
        self.eng_ops = {e: [] for e in ("pe", "act", "dve", "pool", "sp")}
        self.bar = {}
        self.phase = 0

    def barrier(self):
        bar = {}
        for e, lst in self.eng_ops.items():
            for o in reversed(lst):
                if o.dma_key is None:
                    bar[o.idx] = True
                    break
        last_dma = {}
        for o in self.ops:
            if o.dma_key is not None:
                last_dma[o.dma_key] = o.idx
        for i in last_dma.values():
            bar[i] = True
        self.bar = bar
        self.last_w = {}
        self.readers = {}
        self.phase += 1

    def op(self, eng, fn, reads=(), writes=(), dma_key=None, queue=None, group=False):
        o = Op()
        o.idx = len(self.ops)
        if dma_key is not None:
            dma_key = (self.phase, dma_key)
        o.eng = eng if dma_key is None else (queue or "sp")
        o.fn = fn
        o.dma_key = dma_key
        o.needs_inc = False
        o.inc_cnt = 0
        o.gpos = 0
        o.throttle = 0
        deps = dict(self.bar)
        for r in reads:
            w = self.last_w.get(r)
            if w is not None:
                deps[w] = True
        o.wdeps = {}
        for r in writes:
            w = self.last_w.get(r)
            dr = {}
            if w is not None:
                ow = self.ops[w]
                if group and dma_key is not None and ow.dma_key == dma_key and r in ow.wdeps:
                    dr.update(ow.wdeps[r])
                    o.gpos = ow.gpos + 1
                    if o.gpos >= 4:
                        o.throttle = self.dma_total[dma_key] - 16 * 4
                else:
                    dr[w] = False
            for rd in self.readers.get(r, ()):
                dr.setdefault(rd, False)
            o.wdeps[r] = dr
            for k, v in dr.items():
                deps.setdefault(k, v)
        o.deps = deps
        for r in reads:
            self.readers.setdefault(r, []).append(o.idx)
        for r in writes:
            self.last_w[r] = o.idx
            self.readers[r] = []
        if dma_key is not None:
            self.dma_total[dma_key] = self.dma_total.get(dma_key, 0) + 16
            o.dma_cnt = self.dma_total[dma_key]
        else:
            o.dma_cnt = 0
        self.ops.append(o)
        self.eng_ops[o.eng].append(o)
        return o

    def emit(self):
        nc = self.nc
        ops = self.ops
        for o in ops:
            real = []
            for d, is_raw in o.deps.items():
                od = ops[d]
                if od.dma_key is None:
                    if od.eng == o.eng and o.dma_key is None:
                        if o.eng == "pe" or not is_raw:
                            continue
                    od.needs_inc = True
                real.append(d)
            o.deps = real
        cnt = {e: 0 for e in self.eng_ops}
        for e, lst in self.eng_ops.items():
            for o in lst:
                if o.dma_key is None and o.needs_inc:
                    cnt[e] += 1
                o.inc_cnt = cnt[e]
        key_hist = {}
        for o in ops:
            if o.dma_key is not None:
                key_hist.setdefault(o.dma_key, []).append((o.idx, o.dma_cnt))
        key_idx = {k: [a for a, _ in v] for k, v in key_hist.items()}

        with contextlib.ExitStack() as st:
            esem = {e: st.enter_context(nc.semaphore("s_" + e)) for e in COMPUTE}
            dsem = {}
            for i, k in enumerate(key_hist):
                dsem[k] = st.enter_context(nc.semaphore("d%d" % i))
            block = st.enter_context(nc.Block())

            def run(ename, handle):
                seen = {}
                for o in self.eng_ops[ename]:
                    waits = {}
                    for d in o.deps:
                        od = ops[d]
                        if od.dma_key is not None:
                            k = od.dma_key
                            pos = bisect.bisect_left(key_idx[k], o.idx) - 1
                            c = key_hist[k][pos][1]
                            sk = ("d", k)
                        else:
                            c = od.inc_cnt
                            sk = ("e", od.eng)
                        if c > waits.get(sk, 0):
                            waits[sk] = c
                    if o.throttle > 0:
                        sk = ("d", o.dma_key)
                        if o.throttle > waits.get(sk, 0):
                            waits[sk] = o.throttle
                    for sk, c in waits.items():
                        if seen.get(sk, 0) >= c:
                            continue
                        seen[sk] = c
                        sem = dsem[sk[1]] if sk[0] == "d" else esem[sk[1]]
                        handle.wait_ge(sem, c)
                    ins = o.fn(handle)
                    if o.dma_key is not None:
                        ins.then_inc(dsem[o.dma_key], 16)
                    elif o.needs_inc:
                        ins.then_inc(esem[o.eng], 1)
                if ename == "sp":
                    for k, tot in self.dma_total.items():
                        handle.wait_ge(dsem[k], tot)
                    for e in COMPUTE:
                        if cnt[e] > 0:
                            handle.wait_ge(esem[e], cnt[e])

            @block.sync
            def _(e):
                run("sp", e)

            @block.tensor
            def _(e):
                run("pe", e)

            @block.scalar
            def _(e):
                run("act", e)

            @block.vector
            def _(e):
                run("dve", e)

            @block.gpsimd
            def _(e):
                run("pool", e)


class Arena:
    def __init__(self, nc):
        self.nc = nc
        self.base = (nc.sbuf_base + 63) // 64 * 64
        self.limit = nc.sbuf_top
        self.cur = self.base
        self.n = 0

    def pin(self):
        self.base = self.cur

    def reset(self):
        self.cur = self.base

    def alloc(self, name, shape, dt):
        esz = 4 if dt == F32 else 2
        nb = esz
        for s in shape[1:]:
            nb *= s
        off = self.cur
        self.cur += (nb + 63) // 64 * 64
        assert self.cur <= self.limit, (name, self.cur, self.limit)
        self.n += 1
        return self.nc.alloc_sbuf_tensor_at("%s_%d" % (name, self.n), list(shape), dt, offset=off)


class Cfg:
    def __init__(self, SP=16384, SS=4096, NS=2):
        self.SP = SP
        self.SS = SS
        self.NS = NS
        self.QP = SP // 4
        self.NPG = SP // G
        self.NOG = self.QP // G
        self.NSG = SS // G
        self.RP = self.QP // 64
        self.RS = SS // 64
        self.TOWN = self.QP + NS * SS
        self.TALL = SP + NS * SS


def build_program(cfg, debug=False, phases=("P1", "P2", "P3", "P4", "P5", "P6")):
    nc = bass.Bass("TRN2", target_bir_lowering=False)
    c = cfg
    SP, SS, NS, QP = c.SP, c.SS, c.NS, c.QP
    TOWN, TALL = c.TOWN, c.TALL

    def din(name, shape, dt=F32):
        return nc.dram_tensor(name, list(shape), dt, kind="ExternalInput").ap()

    def dout(name, shape, dt=F32):
        return nc.dram_tensor(name, list(shape), dt, kind="ExternalOutput").ap()

    def dscr(name, shape, dt):
        if debug:
            return nc.dram_tensor(name, list(shape), dt, kind="ExternalOutput").ap()
        return nc.dram_tensor(name, list(shape), dt).ap()

    xp = din("xp", [SP, D])
    xs = din("xs", [NS * SS, D])
    w_ffn1_in = din("ffn1_w_in", [D, 2 * DFF])
    w_ffn1_out = din("ffn1_w_out", [DFF, D])
    w_ffn2_in = din("ffn2_w_in", [D, 2 * DFF])
    w_ffn2_out = din("ffn2_w_out", [DFF, D])
    lnp = {k: din(k, [1, D]) for k in ("ln1_g", "ln1_b", "ln2_g", "ln2_b", "ln3_g", "ln3_b")}
    w_in = din("w_in", [D, IN_COLS])
    b_gate = din("b_gate", [2 * D])
    q_norm_g = din("q_norm_g", [384])
    kv_norm_g = din("kv_norm_g", [256])
    w_uq = din("w_uq", [384, 768])
    w_ukv = din("w_ukv", [256, 1024])
    w_na_o = din("w_na_o", [512, D])
    w_mla_o = din("w_mla_o", [512, D])
    w_out = din("w_out", [D, D])
    tz = din("tz", [8, 15, 64, 64])
    mint = din("mint", [128, 5, 128])
    mcol = din("mcol", [128, 128])
    rmask = din("rmask", [2, 3 * 4 * 7 * 128 + 128])
    ropep = din("ropep", [2, 32, SP])
    ropes = din("ropes", [2, 32, SS])

    yp = dout("yp", [QP, D])
    ys = dout("ys", [NS * SS, D])

    x1_d = dscr("x1_d", [TALL, D], F32)
    x2_d = dscr("x2_d", [TOWN, D], F32)
    NAKP = QP + 512
    naq_d = dscr("naq_d", [128, 4, TOWN], BF16)
    nakp_d = dscr("nakp_d", [128, 4, NAKP], BF16)
    naks_d = dscr("naks_d", [128, 4, NS * SS], BF16)
    navp_d = dscr("navp_d", [NAKP, 512], BF16)
    navs_d = dscr("navs_d", [NS * SS, 512], BF16)
    qt_d = dscr("qt_d", [8, 96, TOWN], BF16)
    kt_d = dscr("kt_d", [8, 64, TALL], BF16)
    kr_d = dscr("kr_d", [32, TALL], BF16)
    v_d = dscr("v_d", [TALL, 512], BF16)
    gt_d = dscr("gt_d", [128, 16, TOWN], BF16)
    nao_d = dscr("nao_d", [128, 4, TOWN], BF16)
    mlao_d = dscr("mlao_d", [128, 4, TOWN], BF16)

    P = Prog(nc)
    A = Arena(nc)
    ps = nc.alloc_psum_tensor("ps", [128, 8, 512], F32)
    psb = ps.bitcast(BF16)

    def bank(i):
        return ("B", i)

    ident = A.alloc("ident", [128, 128], F32)
    ident_b = A.alloc("identb", [128, 128], BF16)
    ones_f = A.alloc("onesf", [128, 128], F32)
    ones_b = A.alloc("onesb", [128, 128], BF16)
    epsln = A.alloc("epsln", [128, 1], F32)
    A.pin()
    P.op("pool", lambda e: e.memset(ident[:], 0.0), writes=["ident"])
    P.op("pool", lambda e: e.affine_select(out=ident[:], in_=ident[:], pattern=[[-1, 128]], compare_op=ALU.not_equal,
                                           fill=1.0, base=0, channel_multiplier=1), reads=["ident"], writes=["ident"])
    P.op("pool", lambda e: e.tensor_copy(out=ident_b[:], in_=ident[:]), reads=["ident"], writes=["identb"])
    P.op("pool", lambda e: e.memset(ones_f[:], 1.0), writes=["onesf"])
    P.op("pool", lambda e: e.memset(ones_b[:], 1.0), writes=["onesb"])
    CONSTS = ["ident", "identb", "onesf", "onesb"]

    all_groups = []
    for g in range(c.NPG):
        all_groups.append(("p", g))
    for g in range(NS * c.NSG):
        all_groups.append(("s", g))

    def x_src(kind, g):
        return (xp if kind == "p" else xs)[g * G:(g + 1) * G, :]

    def tall_off(kind, g):
        return g * G if kind == "p" else SP + g * G

    def own_off(kind, g):
        if kind == "p":
            return g * G if g < c.NOG else None
        return QP + g * G


    def load_weight_cast(dst, src, key, nsplit=1):
        K = dst.shape[1]
        v = src.rearrange("(k p) n -> p k n", p=128)
        step = (K + nsplit - 1) // nsplit
        for k0 in range(0, K, step):
            k1 = min(K, k0 + step)
            P.op("pool", lambda e, k0=k0, k1=k1: e.dma_start(out=dst[:, k0:k1, :], in_=v[:, k0:k1, :]),
                 writes=[key], dma_key=key + "_ld", queue="pool", group=True)

    def load_transposed(src_rows, xin, xT, slot, bank0, xbf):
        for t in range(4):
            s = t % 2
            P.op("sp", lambda e, t=t, s=s: e.dma_start(out=xin[:, s, :], in_=src_rows[t * 128:(t + 1) * 128, :]),
                 writes=[("xin", s)], dma_key="xin%d" % s)
            P.op("pool", lambda e, s=s: e.tensor_copy(out=xbf[:, s, :], in_=xin[:, s, :]), reads=[("xin", s)], writes=[("xbf", s)])
            for half in range(2):
                b = bank0 + (t % 2) * 2 + half
                for j in range(4):
                    k = half * 4 + j
                    P.op("pe", lambda e, s=s, k=k, b=b, j=j: e.transpose(out=psb[:, b, j * 128:(j + 1) * 128],
                                                                         in_=xbf[:, s, k * 128:(k + 1) * 128], identity=ident_b[:]),
                         reads=[("xbf", s), "identb"], writes=[bank(b)])
                eng = "act" if half == 0 else "dve"
                dst = xT[:, slot, half * 4:(half + 1) * 4, t * 128:(t + 1) * 128]
                src = psb[:, b, 0:512].rearrange("p (j n) -> p j n", n=128)
                if eng == "act":
                    P.op("act", lambda e, dst=dst, src=src: e.activation(out=dst, in_=src, func=AF.Copy),
                         reads=[bank(b)], writes=[("xT", slot)])
                else:
                    P.op("dve", lambda e, dst=dst, src=src: e.tensor_copy(out=dst, in_=src),
                         reads=[bank(b)], writes=[("xT", slot)])

    def layer_norm_tile(r, slot, gb, st6, mv, key):
        rv = r[:, slot, :]
        for cc in range(2):
            P.op("dve", lambda e, cc=cc: e.bn_stats(out=st6[:, slot, cc, :], in_=r[:, slot, cc * 512:(cc + 1) * 512]),
                 reads=[key], writes=[("st6", slot)])
        P.op("dve", lambda e: e.bn_aggr(out=mv[:, slot, 0:2], in_=st6[:, slot, :, :]), reads=[("st6", slot)], writes=[("mv", slot)])
        P.op("dve", lambda e: e.tensor_scalar(out=mv[:, slot, 2:3], in0=mv[:, slot, 1:2], scalar1=LN_EPS, scalar2=None, op0=ALU.add),
             reads=[("mv", slot)], writes=[("mv", slot)])
        P.op("act", lambda e: e.activation(out=mv[:, slot, 3:4], in_=mv[:, slot, 2:3], func=AF.Sqrt),
             reads=[("mv", slot)], writes=[("mv", slot)])
        P.op("dve", lambda e: e.reciprocal(out=mv[:, slot, 4:5], in_=mv[:, slot, 3:4]), reads=[("mv", slot)], writes=[("mv", slot)])
        P.op("dve", lambda e: e.tensor_scalar(out=mv[:, slot, 5:6], in0=mv[:, slot, 0:1], scalar1=mv[:, slot, 4:5], scalar2=-1.0,
                                              op0=ALU.mult, op1=ALU.mult), reads=[("mv", slot)], writes=[("mv", slot)])
        P.op("act", lambda e: e.activation(out=rv, in_=rv, func=AF.Identity, scale=mv[:, slot, 4:5], bias=mv[:, slot, 5:6]),
             reads=[key, ("mv", slot)], writes=[key])
        P.op("pool", lambda e: e.tensor_tensor(out=rv, in0=rv, in1=gb[:, 0, :], op=ALU.mult), reads=[key, "gb"], writes=[key])
        P.op("pool", lambda e: e.tensor_tensor(out=rv, in0=rv, in1=gb[:, 1, :], op=ALU.add), reads=[key, "gb"], writes=[key])

    def load_gb(gb, gname, bname):
        P.op("sp", lambda e: e.dma_start(out=gb[:, 0, :], in_=lnp[gname].partition_broadcast(128)), writes=["gb"], dma_key="gb")
        P.op("sp", lambda e: e.dma_start(out=gb[:, 1, :], in_=lnp[bname].partition_broadcast(128)), writes=["gb"], dma_key="gb")

    def ffn_phase(groups, w1_d, w2_d, gname, bname):
        A.reset()
        W1 = A.alloc("W1", [128, 8, 2 * DFF], BF16)
        W2 = A.alloc("W2", [128, 22, D], BF16)
        xin = A.alloc("xin", [128, 2, D], F32)
        xbf = A.alloc("xbf", [128, 2, D], BF16)
        xT = A.alloc("xT", [128, 2, 8, G], BF16)
        hT = A.alloc("hT", [128, 22, G], BF16)
        sil = A.alloc("sil", [128, 2, G], F32)
        rr = A.alloc("rr", [128, 2, D], F32)
        gb = A.alloc("gb", [128, 2, D], F32)
        st6 = A.alloc("st6", [128, 2, 2, 6], F32)
        mv = A.alloc("mv", [128, 2, 8], F32)
        load_weight_cast(W1, w1_d, "W1", nsplit=8)
        load_weight_cast(W2, w2_d, "W2", nsplit=4)
        load_gb(gb, gname, bname)
        ng = len(groups)

        def stage_T(g):
            load_transposed(groups[g][0], xin, xT, g % 2, 0, xbf)

        def stage_A(g):
            slot = g % 2
            for hc in range(22):
                ba = (hc % 2) * 2
                bu = ba + 1
                for k in range(8):
                    P.op("pe", lambda e, k=k, hc=hc, ba=ba: e.matmul(ps[:, ba, :], lhsT=W1[:, k, hc * 128:(hc + 1) * 128],
                                                                     rhs=xT[:, slot, k, :], start=(k == 0), stop=(k == 7)),
                         reads=["W1", ("xT", slot)], writes=[bank(ba)])
                for k in range(8):
                    P.op("pe", lambda e, k=k, hc=hc, bu=bu: e.matmul(ps[:, bu, :], lhsT=W1[:, k, DFF + hc * 128:DFF + (hc + 1) * 128],
                                                                     rhs=xT[:, slot, k, :], start=(k == 0), stop=(k == 7)),
                         reads=["W1", ("xT", slot)], writes=[bank(bu)])
                ss = hc % 2
                P.op("act", lambda e, ss=ss, ba=ba: e.activation(out=sil[:, ss, :], in_=ps[:, ba, :], func=AF.Silu),
                     reads=[bank(ba)], writes=[("sil", ss)])
                P.op("dve", lambda e, ss=ss, bu=bu, hc=hc: e.scalar_tensor_tensor(out=hT[:, hc, :], in0=sil[:, ss, :], scalar=0.5,
                                                                                  in1=ps[:, bu, :], op0=ALU.mult, op1=ALU.mult),
                     reads=[("sil", ss), bank(bu)], writes=["hT"])

        def stage_B(g):
            src, dst = groups[g]
            for t in range(4):
                rs = t % 2
                b0 = 4 + (t % 2) * 2
                P.op("sp", lambda e, t=t, rs=rs: e.dma_start(out=rr[:, rs, :], in_=src[t * 128:(t + 1) * 128, :]),
                     writes=[("rr", rs)], dma_key="rr%d" % rs)
                for half in range(2):
                    for k in range(22):
                        P.op("pe", lambda e, k=k, t=t, half=half, b0=b0: e.matmul(ps[:, b0 + half, :], lhsT=hT[:, k, t * 128:(t + 1) * 128],
                                                                                  rhs=W2[:, k, half * 512:(half + 1) * 512],
                                                                                  start=(k == 0), stop=(k == 21)),
                             reads=["hT", "W2"], writes=[bank(b0 + half)])
                psy = ps[:, b0:b0 + 2, :].rearrange("p a b -> p (a b)")
                P.op("dve", lambda e, rs=rs, psy=psy: e.scalar_tensor_tensor(out=rr[:, rs, :], in0=rr[:, rs, :], scalar=ALPHA, in1=psy,
                                                                            op0=ALU.mult, op1=ALU.add),
                     reads=[("rr", rs), bank(b0), bank(b0 + 1)], writes=[("rr", rs)])
                layer_norm_tile(rr, rs, gb, st6, mv, ("rr", rs))
                P.op("pool", lambda e, t=t, rs=rs: e.dma_start(out=dst[t * 128:(t + 1) * 128, :], in_=rr[:, rs, :]),
                     reads=[("rr", rs)], dma_key="rrst%d" % rs, queue="pool")

        stage_T(0)
        for g in range(ng):
            stage_A(g)
            if g + 1 < ng:
                stage_T(g + 1)
            stage_B(g)
        P.barrier()


    def proj_phase():
        A.reset()
        Wi = A.alloc("Wi", [128, 8, IN_COLS], BF16)
        wuq = A.alloc("wuq", [128, 3, 768], BF16)
        wuq_rh = A.alloc("wuqrh", [128, 3, 256], BF16)
        wuk = A.alloc("wuk", [128, 2, 512], BF16)
        wuv = A.alloc("wuv", [128, 2, 512], BF16)
        wkr_rh = A.alloc("wkrrh", [128, 8, 32], BF16)
        qg = A.alloc("qg", [128, 4], F32)
        kvg = A.alloc("kvg", [128, 2], F32)
        bg = A.alloc("bg", [128, 16], F32)
        xin = A.alloc("xin", [128, 2, D], F32)
        xbf = A.alloc("xbf", [128, 2, D], BF16)
        xT = A.alloc("xT", [128, 2, 8, G], BF16)
        cq = A.alloc("cq", [128, 3, G], F32)
        sq = A.alloc("sq", [128, 3, G], F32)
        ckr = A.alloc("ckr", [128, 2, G], F32)
        skv = A.alloc("skv", [128, 2, G], F32)
        cqn = A.alloc("cqn", [128, 2, 3, G], BF16)
        ckvn = A.alloc("ckvn", [128, 2, 2, G], BF16)
        rstd = A.alloc("rstd", [128, 2, G], F32)
        cs = A.alloc("cs", [128, 2, 2, G], F32)
        t1 = A.alloc("t1", [128, 2, G], F32)
        t2 = A.alloc("t2", [128, 2, G], F32)
        naqs = A.alloc("naqs", [128, 4, G], BF16)
        naks = A.alloc("naks", [128, 4, G], BF16)
        navs = A.alloc("navs", [128, 4, 512], BF16)
        off_qst = A.cur
        qst = A.alloc("qst", [128, 8, G], BF16)
        kst = A.alloc("kst", [128, 4, G], BF16)
        vst = A.alloc("vst", [128, 4, 512], BF16)
        krs = A.alloc("krs", [128, G], BF16)
        off_gst = A.cur
        gst = A.alloc("gst", [128, 16, G], BF16)
        stg_q = nc.alloc_sbuf_tensor_at("stgq_alias", [128, 3, 768], F32, offset=off_gst)
        stg_kv = nc.alloc_sbuf_tensor_at("stgkv_alias", [128, 2, 1024], F32, offset=off_qst)

        def col_load(dst, src1d, n, key):
            for k in range(n):
                P.op("sp", lambda e, k=k: e.dma_start(out=dst[:, k:k + 1], in_=src1d[k * 128:(k + 1) * 128].rearrange("(p o) -> p o", o=1)),
                     writes=[key], dma_key=key + "_ld")

        load_weight_cast(Wi, w_in, "Wi", nsplit=8)
        col_load(qg, q_norm_g, 3, "qg")
        col_load(kvg, kv_norm_g, 2, "kvg")
        col_load(bg, b_gate, 16, "bg")
        P.op("sp", lambda e: e.dma_start(out=stg_q[:], in_=w_uq.rearrange("(k p) n -> p k n", p=128)), writes=["gst"], dma_key="stgq")
        P.op("sp", lambda e: e.dma_start(out=stg_kv[:], in_=w_ukv.rearrange("(k p) n -> p k n", p=128)), writes=["qst"], dma_key="stgkv")
        for k in range(3):
            P.op("dve", lambda e, k=k: e.tensor_scalar(out=wuq[:, k, :], in0=stg_q[:, k, :], scalar1=qg[:, k:k + 1], scalar2=None, op0=ALU.mult),
                 reads=["gst", "qg"], writes=["wuq"])
            v = wuq[:, k, :].rearrange("p (h t) -> p h t", t=96)
            o = wuq_rh[:, k, :].rearrange("p (h t) -> p h t", t=32)
            P.op("act", lambda e, v=v, o=o: e.mul(out=o[:, :, 0:16], in_=v[:, :, 80:96], mul=-1.0), reads=["wuq"], writes=["wuqrh"])
            P.op("act", lambda e, v=v, o=o: e.copy(out=o[:, :, 16:32], in_=v[:, :, 64:80]), reads=["wuq"], writes=["wuqrh"])
        for k in range(2):
            sv = stg_kv[:, k, :].rearrange("p (h t) -> p h t", t=128)
            P.op("dve", lambda e, k=k, sv=sv: e.tensor_scalar(out=wuk[:, k, :].rearrange("p (h d) -> p h d", d=64), in0=sv[:, :, 0:64],
                                                              scalar1=kvg[:, k:k + 1], scalar2=None, op0=ALU.mult),
                 reads=["qst", "kvg"], writes=["wuk"])
            P.op("dve", lambda e, k=k, sv=sv: e.tensor_scalar(out=wuv[:, k, :].rearrange("p (h d) -> p h d", d=64), in0=sv[:, :, 64:128],
                                                              scalar1=kvg[:, k:k + 1], scalar2=None, op0=ALU.mult),
                 reads=["qst", "kvg"], writes=["wuv"])
        KR0 = 2176
        P.op("act", lambda e: e.mul(out=wkr_rh[:, :, 0:16], in_=Wi[:, :, KR0 + 16:KR0 + 32], mul=-1.0), reads=["Wi"], writes=["wkrrh"])
        P.op("act", lambda e: e.copy(out=wkr_rh[:, :, 16:32], in_=Wi[:, :, KR0:KR0 + 16]), reads=["Wi"], writes=["wkrrh"])

        rot = [4]

        def nb():
            b = rot[0]
            rot[0] = 4 + (rot[0] - 3) % 4
            return b

        evt = [0]

        def evac(dst, src, rd, wr, scale=None, eng=None):
            if eng is None:
                eng = "act" if evt[0] % 2 == 0 else "dve"
                evt[0] += 1
            if eng == "act":
                if scale is None:
                    P.op("act", lambda e: e.activation(out=dst, in_=src, func=AF.Copy), reads=rd, writes=wr)
                else:
                    P.op("act", lambda e: e.activation(out=dst, in_=src, func=AF.Copy, scale=scale), reads=rd, writes=wr)
            else:
                assert scale is None
                P.op("dve", lambda e: e.tensor_copy(out=dst, in_=src), reads=rd, writes=wr)

        def mm8(out, c0, c1, slot, b):
            for k in range(8):
                P.op("pe", lambda e, k=k: e.matmul(out, lhsT=Wi[:, k, c0:c1], rhs=xT[:, slot, k, :], start=(k == 0), stop=(k == 7)),
                     reads=["Wi", ("xT", slot)], writes=[bank(b)])

        def rms_mm(c0, nch, slot, raw, sqb, rkey, skey):
            for ci in range(nch):
                b = nb()
                mm8(ps[:, b, :], c0 + ci * 128, c0 + (ci + 1) * 128, slot, b)
                P.op("act", lambda e, ci=ci, b=b: e.activation(out=raw[:, ci, :], in_=ps[:, b, :], func=AF.Copy), reads=[bank(b)], writes=[rkey])
                P.op("act", lambda e, ci=ci, b=b: e.activation(out=sqb[:, ci, :], in_=ps[:, b, :], func=AF.Square), reads=[bank(b)], writes=[skey])

        def rms_fin(nch, dim, rslot, raw, sqb, rkey, skey, dstn, dkey):
            b = nb()
            for ci in range(nch):
                P.op("pe", lambda e, ci=ci, b=b: e.matmul(ps[:, b, :], lhsT=ones_f[:], rhs=sqb[:, ci, :], start=(ci == 0), stop=(ci == nch - 1)),
                     reads=["onesf", skey], writes=[bank(b)])
            rk = ("rstd", rslot)
            P.op("dve", lambda e, b=b: e.tensor_scalar(out=rstd[:, rslot, :], in0=ps[:, b, :], scalar1=1.0 / dim, scalar2=RMS_EPS,
                                                       op0=ALU.mult, op1=ALU.add), reads=[bank(b)], writes=[rk])
            P.op("act", lambda e: e.activation(out=rstd[:, rslot, :], in_=rstd[:, rslot, :], func=AF.Sqrt), reads=[rk], writes=[rk])
            P.op("dve", lambda e: e.reciprocal(out=rstd[:, rslot, :], in_=rstd[:, rslot, :]), reads=[rk], writes=[rk])
            for ci in range(nch):
                P.op("dve" if ci % 2 == 0 else "pool", lambda e, ci=ci: e.tensor_tensor(out=dstn[:, ci, :], in0=raw[:, ci, :], in1=rstd[:, rslot, :], op=ALU.mult),
                     reads=[rkey, rk], writes=[dkey])

        def st(dst, src, rd, key):
            P.op("pool", lambda e: e.dma_start(out=dst, in_=src), reads=rd, dma_key=key, queue="pool")

        def info(gi):
            kind, g = all_groups[gi]
            return kind, g, own_off(kind, g), tall_off(kind, g), gi % 2

        def stage_T(gi):
            kind, g, own, toff, sl = info(gi)
            load_transposed(x1_d[toff:toff + G, :], xin, xT, sl, 0, xbf)
            rope_src = ropep[:, :, g * G:(g + 1) * G] if kind == "p" else ropes[:, :, (g % c.NSG) * G:(g % c.NSG + 1) * G]
            for i in range(2):
                P.op("sp", lambda e, i=i: e.dma_start(out=cs[64:96, sl, i, :], in_=rope_src[i]), writes=[("cs", sl)], dma_key="cs%d" % sl)

        def stage_Amm(gi):
            kind, g, own, toff, sl = info(gi)
            if own is not None:
                rms_mm(1536, 3, sl, cq, sq, "cq", "sq")
            rms_mm(1920, 2, sl, ckr, skv, "ckr", "skv")
            b1 = nb()
            mm8(ps[64:96, b1, :], KR0, KR0 + 32, sl, b1)
            b2 = nb()
            for k in range(8):
                P.op("pe", lambda e, k=k, b2=b2: e.matmul(ps[64:96, b2, :], lhsT=wkr_rh[:, k, :], rhs=xT[:, sl, k, :], start=(k == 0), stop=(k == 7)),
                     reads=["wkrrh", ("xT", sl)], writes=[bank(b2)])
            P.op("dve", lambda e, b1=b1: e.tensor_tensor(out=t1[64:96, 0, :], in0=ps[64:96, b1, :], in1=cs[64:96, sl, 0, :], op=ALU.mult),
                 reads=[bank(b1), ("cs", sl)], writes=[("t1", 0)])
            P.op("dve", lambda e, b2=b2: e.tensor_tensor(out=t2[64:96, 0, :], in0=ps[64:96, b2, :], in1=cs[64:96, sl, 1, :], op=ALU.mult),
                 reads=[bank(b2), ("cs", sl)], writes=[("t2", 0)])
            P.op("pool", lambda e: e.tensor_tensor(out=krs[64:96, :], in0=t1[64:96, 0, :], in1=t2[64:96, 0, :], op=ALU.add),
                 reads=[("t1", 0), ("t2", 0)], writes=["krs"])
            st(kr_d[:, toff:toff + G], krs[64:96, :], ["krs"], "krs_st")

        def stage_Afin(gi):
            kind, g, own, toff, sl = info(gi)
            if own is not None:
                rms_fin(3, 384.0, 0, cq, sq, "cq", "sq", cqn[:, sl], ("cqn", sl))
            rms_fin(2, 256.0, 1, ckr, skv, "ckr", "skv", ckvn[:, sl], ("ckvn", sl))

        def stage_B1(gi):
            kind, g, own, toff, sl = info(gi)
            halo_a = kind == "p" and g == c.NOG
            halo_b = kind == "p" and g == c.NPG - 1
            if own is not None:
                for ci in range(4):
                    b = nb()
                    mm8(ps[:, b, :], ci * 128, (ci + 1) * 128, sl, b)
                    evac(naqs[:, ci, :], ps[:, b, :], [bank(b)], ["naqs"], scale=0.125, eng="act")
                st(naq_d[:, :, own:own + G], naqs[:], ["naqs"], "naqs_st")
            if own is not None or halo_a or halo_b:
                for ci in range(4):
                    b = nb()
                    mm8(ps[:, b, :], 512 + ci * 128, 512 + (ci + 1) * 128, sl, b)
                    evac(naks[:, ci, :], ps[:, b, :], [bank(b)], ["naks"])
                for t in range(4):
                    b = nb()
                    for k in range(8):
                        P.op("pe", lambda e, k=k, t=t, b=b: e.matmul(ps[:, b, :], lhsT=xT[:, sl, k, t * 128:(t + 1) * 128], rhs=Wi[:, k, 1024:1536],
                                                                     start=(k == 0), stop=(k == 7)),
                             reads=["Wi", ("xT", sl)], writes=[bank(b)])
                    evac(navs[:, t, :], ps[:, b, :], [bank(b)], ["navs"])
                if kind == "s":
                    st(naks_d[:, :, g * G:(g + 1) * G], naks[:], ["naks"], "naks_st")
                    st(navs_d[g * G:(g + 1) * G, :].rearrange("(t p) d -> p t d", p=128), navs[:], ["navs"], "navs_st")
                elif own is not None:
                    st(nakp_d[:, :, 256 + own:256 + own + G], naks[:], ["naks"], "naks_st")
                    st(navp_d[256 + own:256 + own + G, :].rearrange("(t p) d -> p t d", p=128), navs[:], ["navs"], "navs_st")
                elif halo_a:
                    st(nakp_d[:, :, 256 + QP:256 + QP + 256], naks[:, :, 0:256], ["naks"], "naks_st")
                    st(navp_d[256 + QP:256 + QP + 256, :].rearrange("(t p) d -> p t d", p=128), navs[:, 0:2, :], ["navs"], "navs_st")
                else:
                    st(nakp_d[:, :, 0:256], naks[:, :, 256:512], ["naks"], "naks_st")
                    st(navp_d[0:256, :].rearrange("(t p) d -> p t d", p=128), navs[:, 2:4, :], ["navs"], "navs_st")
            for hp in range(4):
                b = nb()
                for hh in range(2):
                    h = 2 * hp + hh
                    for k in range(2):
                        P.op("pe", lambda e, k=k, h=h, hh=hh, b=b: e.matmul(ps[hh * 64:(hh + 1) * 64, b, :], lhsT=wuk[:, k, h * 64:(h + 1) * 64],
                                                                            rhs=ckvn[:, sl, k, :], start=(k == 0), stop=(k == 1)),
                             reads=["wuk", ("ckvn", sl)], writes=[bank(b)])
                evac(kst[:, hp, :], ps[:, b, :], [bank(b)], ["kst"])
            st(kt_d[:, :, toff:toff + G].rearrange("(c a) d t -> (a d) c t", a=2), kst[:], ["kst"], "kst_st")

        def stage_B2(gi):
            kind, g, own, toff, sl = info(gi)
            for t in range(4):
                b = nb()
                for k in range(2):
                    P.op("pe", lambda e, k=k, t=t, b=b: e.matmul(ps[:, b, :], lhsT=ckvn[:, sl, k, t * 128:(t + 1) * 128], rhs=wuv[:, k, :],
                                                                 start=(k == 0), stop=(k == 1)),
                         reads=["wuv", ("ckvn", sl)], writes=[bank(b)])
                evac(vst[:, t, :], ps[:, b, :], [bank(b)], ["vst"])
            st(v_d[toff:toff + G, :].rearrange("(t p) d -> p t d", p=128), vst[:], ["vst"], "vst_st")
            if own is None:
                return
            for h in range(8):
                b = nb()
                for k in range(3):
                    P.op("pe", lambda e, k=k, h=h, b=b: e.matmul(ps[0:96, b, :], lhsT=wuq[:, k, h * 96:(h + 1) * 96], rhs=cqn[:, sl, k, :],
                                                                 start=(k == 0), stop=(k == 2)),
                         reads=["wuq", ("cqn", sl)], writes=[bank(b)])
                b2 = nb()
                for k in range(3):
                    P.op("pe", lambda e, k=k, h=h, b2=b2: e.matmul(ps[64:96, b2, :], lhsT=wuq_rh[:, k, h * 32:(h + 1) * 32], rhs=cqn[:, sl, k, :],
                                                                   start=(k == 0), stop=(k == 2)),
                         reads=["wuqrh", ("cqn", sl)], writes=[bank(b2)])
                ts = h % 2
                P.op("act", lambda e, h=h, b=b: e.activation(out=qst[0:64, h, :], in_=ps[0:64, b, :], func=AF.Copy), reads=[bank(b)], writes=["qst"])
                P.op("dve", lambda e, b=b, ts=ts: e.tensor_tensor(out=t1[64:96, ts, :], in0=ps[64:96, b, :], in1=cs[64:96, sl, 0, :], op=ALU.mult),
                     reads=[bank(b), ("cs", sl)], writes=[("t1", ts)])
                P.op("dve", lambda e, b2=b2, ts=ts: e.tensor_tensor(out=t2[64:96, ts, :], in0=ps[64:96, b2, :], in1=cs[64:96, sl, 1, :], op=ALU.mult),
                     reads=[bank(b2), ("cs", sl)], writes=[("t2", ts)])
                P.op("pool", lambda e, h=h, ts=ts: e.tensor_tensor(out=qst[64:96, h, :], in0=t1[64:96, ts, :], in1=t2[64:96, ts, :], op=ALU.add),
                     reads=[("t1", ts), ("t2", ts)], writes=["qst"])
            st(qt_d[:, :, own:own + G].rearrange("h d t -> d h t"), qst[0:96, :, :], ["qst"], "qst_st")
            for ci in range(16):
                b = nb()
                mm8(ps[:, b, :], 2208 + ci * 128, 2208 + (ci + 1) * 128, sl, b)
                P.op("act", lambda e, ci=ci, b=b: e.activation(out=gst[:, ci, :], in_=ps[:, b, :], func=AF.Sigmoid, bias=bg[:, ci:ci + 1]),
                     reads=[bank(b), "bg"], writes=["gst"])
            st(gt_d[:, :, own:own + G], gst[:], ["gst"], "gst_st")

        ng = len(all_groups)
        stage_T(0)
        stage_Amm(0)
        stage_Afin(0)
        for gi in range(ng):
            if gi + 1 < ng:
                stage_T(gi + 1)
            stage_B1(gi)
            if gi + 1 < ng:
                stage_Amm(gi + 1)
            stage_B2(gi)
            if gi + 1 < ng:
                stage_Afin(gi + 1)
        P.barrier()

    def na_phase():
        A.reset()
        NKMAX = max(QP + 512, SS)
        NQMAX = max(QP, SS)
        TTi = A.alloc("TTi", [128, 8, 5, 128], BF16)
        TTg = A.alloc("TTg", [128, 8, 7, 128], BF16)
        tzs = A.alloc("tzs", [128, 2, 7, 128], F32)
        mi = A.alloc("mi", [128, 5, 128], F32)
        mc = A.alloc("mc", [128, 128], F32)
        NRM = 3 * 4 * 7 * 128
        rm = A.alloc("rm", [2, NRM + 128], BF16)
        nak = A.alloc("nak", [128, 4, NKMAX], BF16)
        nav1 = A.alloc("nav1", [128, NKMAX // 128, 8, 128], BF16)
        naq = A.alloc("naq", [128, 4, NQMAX], BF16)
        nao = A.alloc("nao", [128, 2, 4, 128], BF16)
        PTn = A.alloc("PTn", [128, 2, 896], BF16)
        recn = A.alloc("recn", [128, 2, 128], F32)
        nv5 = nav1[:].rearrange("p c (hp two) d -> p c hp two d", two=2)
        P.op("pool", lambda e: e.memset(nv5[:, :, :, 0, 64:128], 1.0), writes=["nav1"])
        P.op("pool", lambda e: e.memset(nv5[:, :, :, 1, 0:64], 1.0), writes=["nav1"])
        P.op("sp", lambda e: e.dma_start(out=mi[:], in_=mint[:, :, :]), writes=["mi"], dma_key="mi")
        P.op("sp", lambda e: e.dma_start(out=mc[:], in_=mcol[:, :]), writes=["mc"], dma_key="mc")
        P.op("pool", lambda e: e.dma_start(out=rm[:], in_=rmask[:, :]), writes=["rm"], dma_key="rm", queue="pool")
        for h in range(8):
            sl = h % 2
            for a in range(2):
                for b in range(2):
                    dr0 = 1 + a - b
                    for dd in range(7):
                        P.op("sp", lambda e, h=h, a=a, b=b, dd=dd, dr0=dr0, sl=sl: e.dma_start(
                            out=tzs[a * 64:(a + 1) * 64, sl, dd, b * 64:(b + 1) * 64], in_=tz[h, dr0 + 2 * dd, :, :]),
                            writes=[("tzs", sl)], dma_key="tzs%d" % sl, group=True)
            for dd in range(7):
                P.op("dve", lambda e, h=h, dd=dd, sl=sl: e.tensor_tensor(out=TTg[:, h, dd, :], in0=tzs[:, sl, dd, :], in1=mc[:], op=ALU.add),
                     reads=[("tzs", sl), "mc"], writes=["TTg"])
            for di in range(5):
                P.op("dve", lambda e, h=h, di=di, sl=sl: e.tensor_tensor(out=TTi[:, h, di, :], in0=tzs[:, sl, di + 1, :], in1=mi[:, di, :], op=ALU.add),
                     reads=[("tzs", sl), "mi"], writes=["TTi"])

        blocks = [(0, "p", c.RP, 0, QP, nakp_d, navp_d, 0, QP + 512, 2)]
        for i in range(NS):
            blocks.append((1 + i, "s", c.RS, QP + i * SS, SS, naks_d, navs_d, i * SS, SS, 0))
        pcount = [0]
        for (blk, kind, R, q_off, NQ, kd, vd, koff, NK, joff) in blocks:
            NP = R // 2
            for ci in range(4):
                P.op("sp", lambda e, ci=ci, kd=kd, koff=koff, NK=NK: e.dma_start(out=nak[:, ci, 0:NK], in_=kd[:, ci, koff:koff + NK]),
                     writes=["nak"], dma_key="nak")
                P.op("sp", lambda e, ci=ci, q_off=q_off, NQ=NQ: e.dma_start(out=naq[:, ci, 0:NQ], in_=naq_d[:, ci, q_off:q_off + NQ]),
                     writes=["naq"], dma_key="naq")
            nch = NK // 128
            vsrc = vd[koff:koff + NK, :].rearrange("(c p) (hp two d) -> p c hp two d", p=128, two=2, d=64)
            for c0 in range(nch):
                for two in range(2):
                    P.op("sp", lambda e, c0=c0, two=two, vsrc=vsrc: e.dma_start(
                        out=nv5[:, c0, :, two, two * 64:two * 64 + 64], in_=vsrc[:, c0, :, two, :]),
                        writes=["nav1"], dma_key="nav", group=True)
            eps_ = edge_pairs(NP)

            def do_pair(p, blk, kind, R, joff, eps_, q_off):
                edge = p in eps_
                dls = delta_list(kind, p, R) if edge else [-2, -1, 0, 1, 2]
                n = len(dls)
                assert dls == list(range(dls[0], dls[0] + n))
                pi = eps_.index(p) if edge else -1
                osl = pcount[0] % 2
                pcount[0] += 1
                pieces = [(0, min(n, 4))] + ([(4, n)] if n > 4 else [])

                def S(h):
                    sb = (h % 2) * 2
                    hp = h % 2
                    for (i0, i1) in pieces:
                        out = ps[:, sb + i0 // 4, 0:(i1 - i0) * 128]
                        wr = [bank(sb + i0 // 4)]
                        if edge:
                            d0 = dls[0] + 3 + i0
                            rhs = TTg[:, h, d0:d0 + (i1 - i0), :].rearrange("p a b -> p (a b)")
                            P.op("pe", lambda e, out=out, rhs=rhs: e.matmul(out, lhsT=ident_b[:], rhs=rhs, start=True, stop=False),
                                 reads=["identb", "TTg"], writes=wr)
                            idx = ((blk * 4 + pi) * 7 + d0) * 128
                            P.op("pe", lambda e, out=out, idx=idx, i0=i0, i1=i1: e.matmul(out, lhsT=rm[0:2, NRM:NRM + 128],
                                                                                         rhs=rm[0:2, idx:idx + (i1 - i0) * 128], start=False, stop=False),
                                 reads=["rm"], writes=wr)
                        else:
                            d0 = dls[0] + 2 + i0
                            rhs = TTi[:, h, d0:d0 + (i1 - i0), :].rearrange("p a b -> p (a b)")
                            P.op("pe", lambda e, out=out, rhs=rhs: e.matmul(out, lhsT=ident_b[:], rhs=rhs, start=True, stop=False),
                                 reads=["identb", "TTi"], writes=wr)
                    for i, dl in enumerate(dls):
                        tok0 = (p + dl + joff) * 128
                        out = ps[:, sb + i // 4, (i % 4) * 128:(i % 4 + 1) * 128]
                        P.op("pe", lambda e, out=out, tok0=tok0, i=i: e.matmul(
                            out, lhsT=nak[hp * 64:(hp + 1) * 64, h // 2, tok0:tok0 + 128],
                            rhs=naq[hp * 64:(hp + 1) * 64, h // 2, p * 128:(p + 1) * 128], start=False, stop=(i == n - 1 or i == 3)),
                            reads=["nak", "naq"], writes=[bank(sb + i // 4)])
                    if n <= 4:
                        src = ps[:, sb, 0:n * 128]
                        rd = [bank(sb)]
                    else:
                        src = ps[:, sb:sb + 2, :].rearrange("p a b -> p (a b)")[:, 0:n * 128]
                        rd = [bank(sb), bank(sb + 1)]
                    P.op("act", lambda e, src=src: e.activation(out=PTn[:, h % 2, 0:n * 128], in_=src, func=AF.Exp),
                         reads=rd, writes=[("PTn", h % 2)])

                def PV(h):
                    ob = 4 + h % 4
                    oslot = (h // 4) + 2 * (p % 2)
                    cols = slice(oslot * 128, (oslot + 1) * 128)
                    okey = bank(ob)
                    nh = slice((h % 2) * 64, (h % 2) * 64 + 64)
                    dh = slice((1 - h % 2) * 64, (1 - h % 2) * 64 + 64)
                    for i, dl in enumerate(dls):
                        ch = p + dl + joff
                        P.op("pe", lambda e, i=i, ch=ch: e.matmul(ps[:, ob, cols], lhsT=nav1[:, ch, h, :],
                                                                 rhs=PTn[:, h % 2, i * 128:(i + 1) * 128], start=(i == 0), stop=(i == n - 1)),
                             reads=["nav1", ("PTn", h % 2)], writes=[okey])
                    P.op("act", lambda e: e.activation(out=recn[nh, h % 2, :], in_=ps[dh, ob, cols], func=AF.Ln), reads=[okey], writes=[("recn", h % 2)])
                    P.op("act", lambda e: e.activation(out=recn[nh, h % 2, :], in_=recn[nh, h % 2, :], func=AF.Exp, scale=-1.0),
                         reads=[("recn", h % 2)], writes=[("recn", h % 2)])
                    P.op("dve", lambda e: e.tensor_tensor(out=nao[nh, osl, h // 2, :], in0=ps[nh, ob, cols], in1=recn[nh, h % 2, :],
                                                          op=ALU.mult), reads=[okey, ("recn", h % 2)], writes=[("nao", osl)])

                S(0)
                for h in range(8):
                    if h + 1 < 8:
                        S(h + 1)
                    PV(h)
                P.op("pool", lambda e: e.dma_start(out=nao_d[:, :, q_off + p * 128:q_off + (p + 1) * 128], in_=nao[:, osl, :, :]),
                     reads=[("nao", osl)], dma_key="nao_st%d" % osl, queue="pool")

            for p in range(NP):
                do_pair(p, blk, kind, R, joff, eps_, q_off)
        P.barrier()

    def mla_phase():
        A.reset()
        NKMAX = max(SP, SS)
        NQMAX = max(QP, SS)
        KT = A.alloc("KT", [128, 2, NKMAX], BF16)
        V1 = A.alloc("V1", [128, 2, NKMAX // 128, 128], BF16)
        QT = A.alloc("QT", [128, 2, G], BF16)
        PT = A.alloc("PT", [128, 3, 2 * G], BF16)
        mlao = A.alloc("mlao", [128, 4, NQMAX], BF16)
        rec = A.alloc("rec", [128, 2, G], F32)
        P.op("pool", lambda e: e.memset(V1[:, 0, :, 64:128], 1.0), writes=[("V1", 0)])
        P.op("pool", lambda e: e.memset(V1[:, 1, :, 0:64], 1.0), writes=[("V1", 1)])
        seqs = [(0, QP, 0, SP)] + [(QP + i * SS, SS, SP + i * SS, SS) for i in range(NS)]
        scale = 96.0 ** -0.5
        cnt = [0]
        for (q_off, NQ, k_off, NK) in seqs:
            nkc = NK // 128
            for sl in range(2):
                P.op("sp", lambda e, sl=sl, k_off=k_off, NK=NK: e.dma_start(out=KT[64:96, sl, 0:NK], in_=kr_d[:, k_off:k_off + NK]),
                     writes=[("KTr", sl)], dma_key="KTr%d" % sl)
            for h in range(8):
                sl = h % 2
                vc = 0 if sl == 0 else 64
                nsp = 4 if NK >= 2048 else 1
                stp = NK // nsp
                for i in range(nsp):
                    P.op("sp", lambda e, i=i, h=h, sl=sl, stp=stp, k_off=k_off: e.dma_start(
                        out=KT[0:64, sl, i * stp:(i + 1) * stp], in_=kt_d[h, :, k_off + i * stp:k_off + (i + 1) * stp]),
                        writes=[("KT", sl)], dma_key="KT%d" % sl)
                    P.op("sp", lambda e, i=i, h=h, sl=sl, stp=stp, k_off=k_off, vc=vc: e.dma_start(
                        out=V1[:, sl, i * stp // 128:(i + 1) * stp // 128, vc:vc + 64],
                        in_=v_d[k_off + i * stp:k_off + (i + 1) * stp, h * 64:(h + 1) * 64].rearrange("(c p) d -> p c d", p=128)),
                        writes=[("V1", sl)], dma_key="V1%d" % sl)
                def do_qg(h, sl, qg, q_off, nkc):
                    nh = slice(sl * 64, sl * 64 + 64)
                    dh = slice((1 - sl) * 64, (1 - sl) * 64 + 64)
                    qs = cnt[0] % 2
                    ob = 6 + cnt[0] % 2
                    cnt[0] += 1
                    P.op("sp", lambda e, h=h, qs=qs, qg=qg, q_off=q_off: e.dma_start(
                        out=QT[0:96, qs, :], in_=qt_d[h, :, q_off + qg * G:q_off + (qg + 1) * G]), writes=[("QT", qs)], dma_key="QT%d" % qs)

                    def QK2(kp):
                        sbp = (kp % 3) * 2
                        for i in range(2):
                            kc = 2 * kp + i
                            P.op("pe", lambda e, kc=kc, i=i: e.matmul(ps[:, sbp + i, :], lhsT=KT[0:96, sl, kc * 128:(kc + 1) * 128], rhs=QT[0:96, qs, :],
                                                                      start=True, stop=True),
                                 reads=[("KT", sl), ("KTr", sl), ("QT", qs)], writes=[bank(sbp + i)])
                        P.op("act", lambda e: e.activation(out=PT[:, kp % 3, :], in_=ps[:, sbp:sbp + 2, :].rearrange("p a b -> p (a b)"),
                                                           func=AF.Exp, scale=scale),
                             reads=[bank(sbp), bank(sbp + 1)], writes=[("PT", kp % 3)])

                    def PV2(kp):
                        for i in range(2):
                            kc = 2 * kp + i
                            P.op("pe", lambda e, kc=kc, i=i: e.matmul(ps[:, ob, :], lhsT=V1[:, sl, kc, :], rhs=PT[:, kp % 3, i * G:(i + 1) * G],
                                                                      start=(kc == 0), stop=(kc == nkc - 1)),
                                 reads=[("V1", sl), ("PT", kp % 3)], writes=[bank(ob)])

                    nkp = nkc // 2
                    QK2(0)
                    if nkp > 1:
                        QK2(1)
                    for kp in range(nkp):
                        if kp + 2 < nkp:
                            QK2(kp + 2)
                        PV2(kp)
                    rs = qs
                    P.op("dve", lambda e, rs=rs, ob=ob: e.reciprocal(out=rec[nh, rs, :], in_=ps[dh, ob, :]), reads=[bank(ob)], writes=[("rec", rs)])
                    P.op("dve", lambda e, rs=rs, ob=ob, h=h, qg=qg: e.tensor_tensor(out=mlao[nh, h // 2, qg * G:(qg + 1) * G], in0=ps[nh, ob, :],
                                                                                   in1=rec[nh, rs, :], op=ALU.mult),
                         reads=[bank(ob), ("rec", rs)], writes=["mlao"])

                for qg in range(NQ // G):
                    do_qg(h, sl, qg, q_off, nkc)
            P.op("pool", lambda e, q_off=q_off, NQ=NQ: e.dma_start(out=mlao_d[:, :, q_off:q_off + NQ], in_=mlao[:, :, 0:NQ]),
                 reads=["mlao"], dma_key="mlao_st", queue="pool")
        P.barrier()

    def mix_phase():
        A.reset()
        wna = A.alloc("wna", [128, 4, D], BF16)
        wml = A.alloc("wml", [128, 4, D], BF16)
        wo = A.alloc("wo", [128, 8, D], BF16)
        gb = A.alloc("gb", [128, 2, D], F32)
        nat = A.alloc("nat", [128, 2, 4, G], BF16)
        mlt = A.alloc("mlt", [128, 2, 4, G], BF16)
        gt = A.alloc("gt", [128, 2, 16, G], BF16)
        mT = A.alloc("mT", [128, 2, 8, G], BF16)
        ta = A.alloc("ta", [128, 2, G], F32)
        tb = A.alloc("tb", [128, 2, G], F32)
        rr = A.alloc("rr", [128, 4, D], F32)
        st6 = A.alloc("st6", [128, 4, 2, 6], F32)
        mv = A.alloc("mv", [128, 4, 8], F32)
        load_weight_cast(wna, w_na_o, "wna")
        load_weight_cast(wml, w_mla_o, "wml")
        load_weight_cast(wo, w_out, "wo", nsplit=2)
        load_gb(gb, "ln2_g", "ln2_b")
        own_groups = [(kind, g) for (kind, g) in all_groups if own_off(kind, g) is not None]

        def stage_C(i):
            kind, g = own_groups[i]
            o = own_off(kind, g)
            sg = i % 2
            P.op("sp", lambda e, o=o, sg=sg: e.dma_start(out=nat[:, sg], in_=nao_d[:, :, o:o + G]), writes=[("nat", sg)], dma_key="nat%d" % sg)
            P.op("sp", lambda e, o=o, sg=sg: e.dma_start(out=mlt[:, sg], in_=mlao_d[:, :, o:o + G]), writes=[("mlt", sg)], dma_key="mlt%d" % sg)
            P.op("sp", lambda e, o=o, sg=sg: e.dma_start(out=gt[:, sg], in_=gt_d[:, :, o:o + G]), writes=[("gt", sg)], dma_key="gt%d" % sg)
            for c8 in range(8):
                ba = (c8 % 2) * 2
                bb = ba + 1
                s2 = c8 % 2
                for k in range(4):
                    P.op("pe", lambda e, k=k, c8=c8, ba=ba, sg=sg: e.matmul(ps[:, ba, :], lhsT=wna[:, k, c8 * 128:(c8 + 1) * 128], rhs=nat[:, sg, k, :],
                                                                            start=(k == 0), stop=(k == 3)), reads=["wna", ("nat", sg)], writes=[bank(ba)])
                for k in range(4):
                    P.op("pe", lambda e, k=k, c8=c8, bb=bb, sg=sg: e.matmul(ps[:, bb, :], lhsT=wml[:, k, c8 * 128:(c8 + 1) * 128], rhs=mlt[:, sg, k, :],
                                                                            start=(k == 0), stop=(k == 3)), reads=["wml", ("mlt", sg)], writes=[bank(bb)])
                P.op("dve", lambda e, c8=c8, ba=ba, s2=s2, sg=sg: e.tensor_tensor(out=ta[:, s2, :], in0=ps[:, ba, :], in1=gt[:, sg, c8, :], op=ALU.mult),
                     reads=[bank(ba), ("gt", sg)], writes=[("ta", s2)])
                P.op("dve", lambda e, c8=c8, bb=bb, s2=s2, sg=sg: e.tensor_tensor(out=tb[:, s2, :], in0=ps[:, bb, :], in1=gt[:, sg, 8 + c8, :], op=ALU.mult),
                     reads=[bank(bb), ("gt", sg)], writes=[("tb", s2)])
                P.op("pool" if c8 % 2 == 0 else "dve", lambda e, c8=c8, s2=s2, sg=sg: e.tensor_tensor(out=mT[:, sg, c8, :], in0=ta[:, s2, :], in1=tb[:, s2, :], op=ALU.add),
                     reads=[("ta", s2), ("tb", s2)], writes=[("mT", sg)])

        def stage_O(i):
            kind, g = own_groups[i]
            o = own_off(kind, g)
            xo = tall_off(kind, g)
            sg = i % 2
            for t in range(4):
                rs = t
                b0 = 4 + (t % 2) * 2
                P.op("sp", lambda e, t=t, rs=rs, xo=xo: e.dma_start(out=rr[:, rs, :], in_=x1_d[xo + t * 128:xo + (t + 1) * 128, :]),
                     writes=[("rr", rs)], dma_key="rr%d" % rs)
                for half in range(2):
                    for k in range(8):
                        P.op("pe", lambda e, k=k, t=t, half=half, b0=b0, sg=sg: e.matmul(ps[:, b0 + half, :], lhsT=mT[:, sg, k, t * 128:(t + 1) * 128],
                                                                                         rhs=wo[:, k, half * 512:(half + 1) * 512],
                                                                                         start=(k == 0), stop=(k == 7)),
                             reads=[("mT", sg), "wo"], writes=[bank(b0 + half)])
                psy = ps[:, b0:b0 + 2, :].rearrange("p a b -> p (a b)")
                P.op("dve", lambda e, rs=rs, psy=psy: e.scalar_tensor_tensor(out=rr[:, rs, :], in0=rr[:, rs, :], scalar=ALPHA, in1=psy,
                                                                            op0=ALU.mult, op1=ALU.add),
                     reads=[("rr", rs), bank(b0), bank(b0 + 1)], writes=[("rr", rs)])
                layer_norm_tile(rr, rs, gb, st6, mv, ("rr", rs))
                P.op("pool", lambda e, t=t, rs=rs, o=o: e.dma_start(out=x2_d[o + t * 128:o + (t + 1) * 128, :], in_=rr[:, rs, :]),
                     reads=[("rr", rs)], dma_key="rrst%d" % rs, queue="pool")

        nog = len(own_groups)
        stage_C(0)
        for i in range(nog):
            if i + 1 < nog:
                stage_C(i + 1)
            stage_O(i)
        P.barrier()

    if "P1" in phases:
        groups = [(x_src(k, g), x1_d[tall_off(k, g):tall_off(k, g) + G, :]) for (k, g) in all_groups]
        ffn_phase(groups, w_ffn1_in, w_ffn1_out, "ln1_g", "ln1_b")
    if "P2" in phases:
        proj_phase()
    if "P3" in phases:
        na_phase()
    if "P4" in phases:
        mla_phase()
    if "P5" in phases:
        mix_phase()
    if "P6" in phases:
        groups = []
        for (k, g) in all_groups:
            o = own_off(k, g)
            if o is None:
                continue
            dst = yp[o:o + G, :] if k == "p" else ys[o - QP:o - QP + G, :]
            groups.append((x2_d[o:o + G, :], dst))
        ffn_phase(groups, w_ffn2_in, w_ffn2_out, "ln3_g", "ln3_b")

    P.emit()
    return nc


def window_valid(qr, kr, R, top, bottom):
    ws = qr - 4
    if top:
        ws = max(ws, 0)
    if bottom:
        ws = min(ws, R - 8)
    return ws <= kr < ws + 8


def edge_pairs(NP):
    return [0, 1, NP - 2, NP - 1]


def delta_list(kind, p, R):
    NP = R // 2
    if kind == "p":
        variants = [(False, False), (True, False), (False, True)]
        jmin, jmax = -2, NP + 1
    else:
        variants = [(True, True)]
        jmin, jmax = 0, NP - 1
    out = []
    for dl in range(-3, 4):
        j = p + dl
        if j < jmin or j > jmax:
            continue
        ok = False
        for (top, bottom) in variants:
            for a in range(2):
                for b in range(2):
                    kr, qr = 2 * j + a, 2 * p + b
                    if top and kr < 0:
                        continue
                    if bottom and kr >= R:
                        continue
                    if window_valid(qr, kr, R, top, bottom):
                        ok = True
        if ok:
            out.append(dl)
    return out


def host_constants(cfg, core):
    c = cfg
    qtr = core % 4
    kc = np.arange(64)[:, None]
    cc = np.arange(64)[None, :]
    ws = np.clip(cc - 8, 0, 48)
    colv = (kc >= ws) & (kc < ws + 16)
    mcol64 = np.where(colv, 0.0, NEG).astype(np.float32)
    mcol = np.tile(mcol64, (2, 2))
    mint = np.zeros((128, 5, 128), np.float32)
    for di in range(5):
        dl = di - 2
        for a in range(2):
            for b in range(2):
                rv = -4 <= 2 * dl + a - b <= 3
                blk = mcol64 if rv else np.full((64, 64), NEG, np.float32)
                mint[a * 64:(a + 1) * 64, di, b * 64:(b + 1) * 64] = blk
    rm = np.zeros((2, 3, 4, 7, 128), np.float32)
    for blk in range(3):
        if blk == 0:
            R, top, bottom, kind = c.RP, qtr == 0, qtr == 3, "p"
        else:
            R, top, bottom, kind = c.RS, True, True, "s"
        NP = R // 2
        for pi, p in enumerate(edge_pairs(NP)):
            for di in range(7):
                dl = di - 3
                for a in range(2):
                    for b in range(2):
                        kr, qr = 2 * (p + dl) + a, 2 * p + b
                        v = window_valid(qr, kr, R, top, bottom)
                        rm[a, blk, pi, di, b * 64:(b + 1) * 64] = 0.0 if v else NEG
    inv = (1.0 / (10000.0 ** (np.arange(0, 32, 2, dtype=np.float32) / np.float32(32)))).astype(np.float32)

    def rope(pos):
        ang = pos.astype(np.float32)[:, None] * inv[None, :]
        cs = np.cos(ang).astype(np.float32).T
        sn = np.sin(ang).astype(np.float32).T
        return np.stack([np.concatenate([cs, cs], 0), np.concatenate([sn, sn], 0)], 0)

    posp = (np.arange(c.SP) + qtr * c.QP) % c.SP
    return {
        "mcol": mcol, "mint": mint,
        "rmask": np.ascontiguousarray(np.concatenate([rm.reshape(2, -1), np.kron(np.eye(2, dtype=np.float32), np.ones((1, 64), np.float32))], 1)),
        "ropep": np.ascontiguousarray(rope(posp)), "ropes": np.ascontiguousarray(rope(np.arange(c.SS))),
    }


def make_in_maps(inputs, cfg, used=None):
    c = cfg
    x_prompt = np.asarray(inputs["x_prompt"], np.float32)
    x_sample = np.asarray(inputs["x_sample"], np.float32)
    rpb = np.asarray(inputs["na_rpb"], np.float32)[0]
    kc = np.arange(64)[:, None]
    cc = np.arange(64)[None, :]
    tz = np.ascontiguousarray(rpb[:, :, np.clip(kc - cc + 15, 0, 30)])
    shared = {
        "ffn1_w_in": inputs["ffn1_w_in"][0], "ffn1_w_out": inputs["ffn1_w_out"][0],
        "ffn2_w_in": inputs["ffn2_w_in"][0], "ffn2_w_out": inputs["ffn2_w_out"][0],
        "ln1_g": inputs["ln1_g"], "ln1_b": inputs["ln1_b"], "ln2_g": inputs["ln2_g"], "ln2_b": inputs["ln2_b"],
        "ln3_g": inputs["ln3_g"], "ln3_b": inputs["ln3_b"],
        "w_in": inputs["w_in"][0], "b_gate": inputs["b_gate"][0], "q_norm_g": inputs["q_norm_g"][0],
        "kv_norm_g": inputs["kv_norm_g"][0], "w_uq": inputs["w_uq"][0], "w_ukv": inputs["w_ukv"][0],
        "w_na_o": inputs["w_na_o"][0], "w_mla_o": inputs["w_mla_o"][0], "w_out": inputs["w_out"][0], "tz": tz,
    }
    shared = {k: np.ascontiguousarray(np.asarray(v, np.float32)) for k, v in shared.items()}
    maps = []
    for core in range(NCORES):
        b, qtr = core // 4, core % 4
        m = dict(shared)
        m["xp"] = np.ascontiguousarray(np.roll(x_prompt[b], -qtr * c.QP, axis=0))
        m["xs"] = np.ascontiguousarray(x_sample[c.NS * core:c.NS * (core + 1)].reshape(c.NS * c.SS, D))
        m.update(host_constants(c, core))
        if used is not None:
            m = {k: v for k, v in m.items() if k in used}
        maps.append(m)
    return maps


_CACHE = {}


def kernel(**inputs):
    cfg = Cfg()
    if "nc" not in _CACHE:
        _CACHE["nc"] = build_program(cfg)
    nc = _CACHE["nc"]
    maps = make_in_maps(inputs, cfg)
    res = run_bass_kernel_spmd(nc, maps, core_ids=list(range(NCORES)))
    y_prompt = np.empty((2, cfg.SP, D), np.float32)
    y_sample = np.empty((NCORES * cfg.NS, cfg.SS, D), np.float32)
    for core in range(NCORES):
        b, qtr = core // 4, core % 4
        r = res.results[core]
        y_prompt[b, qtr * cfg.QP:(qtr + 1) * cfg.QP] = np.asarray(r["yp"], np.float32)
        y_sample[cfg.NS * core:cfg.NS * (core + 1)] = np.asarray(r["ys"], np.float32).reshape(cfg.NS, cfg.SS, D)
    return (y_prompt, y_sample)
```

```python
import bisect
import contextlib
import numpy as np
import concourse.bass as bass
import concourse.mybir as mybir
from concourse.bass_utils import run_bass_kernel_spmd

F32 = mybir.dt.float32
BF16 = mybir.dt.bfloat16
AF = mybir.ActivationFunctionType
ALU = mybir.AluOpType

D = 1024
DFF = 2816
G = 512
NCORES = 8
ALPHA = 2.0 ** 0.25
LN_EPS = 1e-5
RMS_EPS = 1e-6
NEG = -30000.0
IN_COLS = 4256
COMPUTE = ("pe", "act", "dve", "pool")


class Op:
    __slots__ = ("eng", "fn", "deps", "dma_key", "dma_cnt", "needs_inc", "inc_cnt", "idx", "wdeps", "gpos", "throttle")


class Prog:
    def __init__(self, nc):
        self.nc = nc
        self.ops = []
        self.last_w = {}
        self.readers = {}
        self.dma_total = {}
        self.eng_ops = {e: [] for e in ("pe", "act", "dve", "pool", "sp")}
        self.bar = {}
        self.phase = 0

    def barrier(self):
        bar = {}
        for e, lst in self.eng_ops.items():
            for o in reversed(lst):
                if o.dma_key is None:
                    bar[o.idx] = True
                    break
        last_dma = {}
        for o in self.ops:
            if o.dma_key is not None:
                last_dma[o.dma_key] = o.idx
        for i in last_dma.values():
            bar[i] = True
        self.bar = bar
        self.last_w = {}
        self.readers = {}
        self.phase += 1

    def op(self, eng, fn, reads=(), writes=(), dma_key=None, queue=None, group=False):
        o = Op()
        o.idx = len(self.ops)
        if dma_key is not None:
            dma_key = (self.phase, dma_key)
        o.eng = eng if dma_key is None else (queue or "sp")
        o.fn = fn
        o.dma_key = dma_key
        o.needs_inc = False
        o.inc_cnt = 0
        o.gpos = 0
        o.throttle = 0
        deps = dict(self.bar)
        for r in reads:
            w = self.last_w.get(r)
            if w is not None:
                deps[w] = True
        o.wdeps = {}
        for r in writes:
            w = self.last_w.get(r)
            dr = {}
            if w is not None:
                ow = self.ops[w]
                if group and dma_key is not None and ow.dma_key == dma_key and r in ow.wdeps:
                    dr.update(ow.wdeps[r])
                    o.gpos = ow.gpos + 1
                    if o.gpos >= 4:
                        o.throttle = self.dma_total[dma_key] - 16 * 4
                else:
                    dr[w] = False
            for rd in self.readers.get(r, ()):
                dr.setdefault(rd, False)
            o.wdeps[r] = dr
            for k, v in dr.items():
                deps.setdefault(k, v)
        o.deps = deps
        for r in reads:
            self.readers.setdefault(r, []).append(o.idx)
        for r in writes:
            self.last_w[r] = o.idx
            self.readers[r] = []
        if dma_key is not None:
            self.dma_total[dma_key] = self.dma_total.get(dma_key, 0) + 16
            o.dma_cnt = self.dma_total[dma_key]
        else:
            o.dma_cnt = 0
        self.ops.append(o)
        self.eng_ops[o.eng].append(o)
        return o

    def emit(self):
        nc = self.nc
        ops = self.ops
        for o in ops:
            real = []
            for d, is_raw in o.deps.items():
                od = ops[d]
                if od.dma_key is None:
                    if od.eng == o.eng and o.dma_key is None:
                        if o.eng == "pe" or not is_raw:
                            continue
                    od.needs_inc = True
                real.append(d)
            o.deps = real
        cnt = {e: 0 for e in self.eng_ops}
        for e, lst in self.eng_ops.items():
            for o in lst:
                if o.dma_key is None and o.needs_inc:
                    cnt[e] += 1
                o.inc_cnt = cnt[e]
        key_hist = {}
        for o in ops:
            if o.dma_key is not None:
                key_hist.setdefault(o.dma_key, []).append((o.idx, o.dma_cnt))
        key_idx = {k: [a for a, _ in v] for k, v in key_hist.items()}

        with contextlib.ExitStack() as st:
            esem = {e: st.enter_context(nc.semaphore("s_" + e)) for e in COMPUTE}
            dsem = {}
            for i, k in enumerate(key_hist):
                dsem[k] = st.enter_context(nc.semaphore("d%d" % i))
            block = st.enter_context(nc.Block())

            def run(ename, handle):
                seen = {}
                for o in self.eng_ops[ename]:
                    waits = {}
                    for d in o.deps:
                        od = ops[d]
                        if od.dma_key is not None:
                            k = od.dma_key
                            pos = bisect.bisect_left(key_idx[k], o.idx) - 1
                            c = key_hist[k][pos][1]
                            sk = ("d", k)
                        else:
                            c = od.inc_cnt
                            sk = ("e", od.eng)
                        if c > waits.get(sk, 0):
                            waits[sk] = c
                    if o.throttle > 0:
                        sk = ("d", o.dma_key)
                        if o.throttle > waits.get(sk, 0):
                            waits[sk] = o.throttle
                    for sk, c in waits.items():
                        if seen.get(sk, 0) >= c:
                            continue
                        seen[sk] = c
                        sem = dsem[sk[1]] if sk[0] == "d" else esem[sk[1]]
                        handle.wait_ge(sem, c)
                    ins = o.fn(handle)
                    if o.dma_key is not None:
                        ins.then_inc(dsem[o.dma_key], 16)
                    elif o.needs_inc:
                        ins.then_inc(esem[o.eng], 1)
                if ename == "sp":
                    for k, tot in self.dma_total.items():
                        handle.wait_ge(dsem[k], tot)
                    for e in COMPUTE:
                        if cnt[e] > 0:
                            handle.wait_ge(esem[e], cnt[e])

            @block.sync
            def _(e):
                run("sp", e)

            @block.tensor
            def _(e):
                run("pe", e)

            @block.scalar
            def _(e):
                run("act", e)

            @block.vector
            def _(e):
                run("dve", e)

            @block.gpsimd
            def _(e):
                run("pool", e)


class Arena:
    def __init__(self, nc):
        self.nc = nc
        self.base = (nc.sbuf_base + 63) // 64 * 64
        self.limit = nc.sbuf_top
        self.cur = self.base
        self.n = 0

    def pin(self):
        self.base = self.cur

    def reset(self):
        self.cur = self.base

    def alloc(self, name, shape, dt):
        esz = 4 if dt == F32 else 2
        nb = esz
        for s in shape[1:]:
            nb *= s
        off = self.cur
        self.cur += (nb + 63) // 64 * 64
        assert self.cur <= self.limit, (name, self.cur, self.limit)
        self.n += 1
        return self.nc.alloc_sbuf_tensor_at("%s_%d" % (name, self.n), list(shape), dt, offset=off)


class Cfg:
    def __init__(self, SP=16384, SS=4096, NS=2):
        self.SP = SP
        self.SS = SS
        self.NS = NS
        self.QP = SP // 4
        self.NPG = SP // G
        self.NOG = self.QP // G
        self.NSG = SS // G
        self.RP = self.QP // 64
        self.RS = SS // 64
        self.TOWN = self.QP + NS * SS
        self.TALL = SP + NS * SS


def build_program(cfg, debug=False, phases=("P1", "P2", "P3", "P4", "P5", "P6")):
    nc = bass.Bass("TRN2", target_bir_lowering=False)
    c = cfg
    SP, SS, NS, QP = c.SP, c.SS, c.NS, c.QP
    TOWN, TALL = c.TOWN, c.TALL

    def din(name, shape, dt=F32):
        return nc.dram_tensor(name, list(shape), dt, kind="ExternalInput").ap()

    def dout(name, shape, dt=F32):
        return nc.dram_tensor(name, list(shape), dt, kind="ExternalOutput").ap()

    def dscr(name, shape, dt):
        if debug:
            return nc.dram_tensor(name, list(shape), dt, kind="ExternalOutput").ap()
        return nc.dram_tensor(name, list(shape), dt).ap()

    xp = din("xp", [SP, D])
    xs = din("xs", [NS * SS, D])
    w_ffn1_in = din("ffn1_w_in", [D, 2 * DFF])
    w_ffn1_out = din("ffn1_w_out", [DFF, D])
    w_ffn2_in = din("ffn2_w_in", [D, 2 * DFF])
    w_ffn2_out = din("ffn2_w_out", [DFF, D])
    lnp = {k: din(k, [1, D]) for k in ("ln1_g", "ln1_b", "ln2_g", "ln2_b", "ln3_g", "ln3_b")}
    w_in = din("w_in", [D, IN_COLS])
    b_gate = din("b_gate", [2 * D])
    q_norm_g = din("q_norm_g", [384])
    kv_norm_g = din("kv_norm_g", [256])
    w_uq = din("w_uq", [384, 768])
    w_ukv = din("w_ukv", [256, 1024])
    w_na_o = din("w_na_o", [512, D])
    w_mla_o = din("w_mla_o", [512, D])
    w_out = din("w_out", [D, D])
    tz = din("tz", [8, 15, 64, 64])
    mint = din("mint", [128, 5, 128])
    mcol = din("mcol", [128, 128])
    rmask = din("rmask", [2, 3 * 4 * 7 * 128 + 128])
    ropep = din("ropep", [2, 32, SP])
    ropes = din("ropes", [2, 32, SS])

    yp = dout("yp", [QP, D])
    ys = dout("ys", [NS * SS, D])

    x1_d = dscr("x1_d", [TALL, D], F32)
    x2_d = dscr("x2_d", [TOWN, D], F32)
    NAKP = QP + 512
    naq_d = dscr("naq_d", [128, 4, TOWN], BF16)
    nakp_d = dscr("nakp_d", [128, 4, NAKP], BF16)
    naks_d = dscr("naks_d", [128, 4, NS * SS], BF16)
    navp_d = dscr("navp_d", [NAKP, 512], BF16)
    navs_d = dscr("navs_d", [NS * SS, 512], BF16)
    qt_d = dscr("qt_d", [8, 96, TOWN], BF16)
    kt_d = dscr("kt_d", [8, 64, TALL], BF16)
    kr_d = dscr("kr_d", [32, TALL], BF16)
    v_d = dscr("v_d", [TALL, 512], BF16)
    gt_d = dscr("gt_d", [128, 16, TOWN], BF16)
    nao_d = dscr("nao_d", [128, 4, TOWN], BF16)
    mlao_d = dscr("mlao_d", [128, 4, TOWN], BF16)

    P = Prog(nc)
    A = Arena(nc)
    ps = nc.alloc_psum_tensor("ps", [128, 8, 512], F32)
    psb = ps.bitcast(BF16)

    def bank(i):
        return ("B", i)

    ident = A.alloc("ident", [128, 128], F32)
    ident_b = A.alloc("identb", [128, 128], BF16)
    ones_f = A.alloc("onesf", [128, 128], F32)
    ones_b = A.alloc("onesb", [128, 128], BF16)
    epsln = A.alloc("epsln", [128, 1], F32)
    A.pin()
    P.op("pool", lambda e: e.memset(ident[:], 0.0), writes=["ident"])
    P.op("pool", lambda e: e.affine_select(out=ident[:], in_=ident[:], pattern=[[-1, 128]], compare_op=ALU.not_equal,
                                           fill=1.0, base=0, channel_multiplier=1), reads=["ident"], writes=["ident"])
    P.op("pool", lambda e: e.tensor_copy(out=ident_b[:], in_=ident[:]), reads=["ident"], writes=["identb"])
    P.op("pool", lambda e: e.memset(ones_f[:], 1.0), writes=["onesf"])
    P.op("pool", lambda e: e.memset(ones_b[:], 1.0), writes=["onesb"])
    CONSTS = ["ident", "identb", "onesf", "onesb"]

    all_groups = []
    for g in range(c.NPG):
        all_groups.append(("p", g))
    for g in range(NS * c.NSG):
        all_groups.append(("s", g))

    def x_src(kind, g):
        return (xp if kind == "p" else xs)[g * G:(g + 1) * G, :]

    def tall_off(kind, g):
        return g * G if kind == "p" else SP + g * G

    def own_off(kind, g):
        if kind == "p":
            return g * G if g < c.NOG else None
        return QP + g * G


    def load_weight_cast(dst, src, key, nsplit=1):
        K = dst.shape[1]
        v = src.rearrange("(k p) n -> p k n", p=128)
        step = (K + nsplit - 1) // nsplit
        for k0 in range(0, K, step):
            k1 = min(K, k0 + step)
            P.op("pool", lambda e, k0=k0, k1=k1: e.dma_start(out=dst[:, k0:k1, :], in_=v[:, k0:k1, :]),
                 writes=[key], dma_key=key + "_ld", queue="pool", group=True)

    xin_ctr = [0]

    def load_transposed(src_rows, xin, xT, slot, bank0, xbf=None):
        nx = xin.shape[1]
        for t in range(4):
            s = xin_ctr[0] % nx
            xin_ctr[0] += 1
            P.op("sp", lambda e, t=t, s=s: e.dma_start(out=xin[:, s, :], in_=src_rows[t * 128:(t + 1) * 128, :]),
                 writes=[("xin", s)], dma_key="xin%d" % s)
            for half in range(2):
                b = bank0 + (t % 2) * 2 + half
                for j in range(4):
                    k = half * 4 + j
                    P.op("pe", lambda e, s=s, k=k, b=b, j=j: e.transpose(out=ps[:, b, j * 128:(j + 1) * 128],
                                                                         in_=xin[:, s, k * 128:(k + 1) * 128], identity=ident[:]),
                         reads=[("xin", s), "ident"], writes=[bank(b)])
                eng = "act" if half == 0 else "dve"
                dst = xT[:, slot, half * 4:(half + 1) * 4, t * 128:(t + 1) * 128]
                src = ps[:, b, :].rearrange("p (j n) -> p j n", n=128)
                if eng == "act":
                    P.op("act", lambda e, dst=dst, src=src: e.activation(out=dst, in_=src, func=AF.Copy),
                         reads=[bank(b)], writes=[("xT", slot)])
                else:
                    P.op("dve", lambda e, dst=dst, src=src: e.tensor_copy(out=dst, in_=src),
                         reads=[bank(b)], writes=[("xT", slot)])

    def layer_norm_tile(r, slot, gb, st6, mv, key):
        rv = r[:, slot, :]
        for cc in range(2):
            P.op("dve", lambda e, cc=cc: e.bn_stats(out=st6[:, slot, cc, :], in_=r[:, slot, cc * 512:(cc + 1) * 512]),
                 reads=[key], writes=[("st6", slot)])
        P.op("dve", lambda e: e.bn_aggr(out=mv[:, slot, 0:2], in_=st6[:, slot, :, :]), reads=[("st6", slot)], writes=[("mv", slot)])
        P.op("dve", lambda e: e.tensor_scalar(out=mv[:, slot, 2:3], in0=mv[:, slot, 1:2], scalar1=LN_EPS, scalar2=None, op0=ALU.add),
             reads=[("mv", slot)], writes=[("mv", slot)])
        P.op("act", lambda e: e.activation(out=mv[:, slot, 3:4], in_=mv[:, slot, 2:3], func=AF.Sqrt),
             reads=[("mv", slot)], writes=[("mv", slot)])
        P.op("dve", lambda e: e.reciprocal(out=mv[:, slot, 4:5], in_=mv[:, slot, 3:4]), reads=[("mv", slot)], writes=[("mv", slot)])
        P.op("dve", lambda e: e.tensor_scalar(out=mv[:, slot, 5:6], in0=mv[:, slot, 0:1], scalar1=mv[:, slot, 4:5], scalar2=-1.0,
                                              op0=ALU.mult, op1=ALU.mult), reads=[("mv", slot)], writes=[("mv", slot)])
        P.op("act", lambda e: e.activation(out=rv, in_=rv, func=AF.Identity, scale=mv[:, slot, 4:5], bias=mv[:, slot, 5:6]),
             reads=[key, ("mv", slot)], writes=[key])
        P.op("pool", lambda e: e.tensor_tensor(out=rv, in0=rv, in1=gb[:, 0, :], op=ALU.mult), reads=[key, "gb"], writes=[key])
        P.op("pool", lambda e: e.tensor_tensor(out=rv, in0=rv, in1=gb[:, 1, :], op=ALU.add), reads=[key, "gb"], writes=[key])

    def load_gb(gb, gname, bname):
        P.op("sp", lambda e: e.dma_start(out=gb[:, 0, :], in_=lnp[gname].partition_broadcast(128)), writes=["gb"], dma_key="gb")
        P.op("sp", lambda e: e.dma_start(out=gb[:, 1, :], in_=lnp[bname].partition_broadcast(128)), writes=["gb"], dma_key="gb")

    def ffn_phase(groups, w1_d, w2_d, gname, bname):
        A.reset()
        W1 = A.alloc("W1", [128, 8, 2 * DFF], BF16)
        W2 = A.alloc("W2", [128, 22, D], BF16)
        xin = A.alloc("xin", [128, 3, D], F32)
        xT = A.alloc("xT", [128, 2, 8, G], BF16)
        hT = A.alloc("hT", [128, 22, G], BF16)
        sil = A.alloc("sil", [128, 2, G], F32)
        rr = A.alloc("rr", [128, 2, D], F32)
        gb = A.alloc("gb", [128, 2, D], F32)
        st6 = A.alloc("st6", [128, 2, 2, 6], F32)
        mv = A.alloc("mv", [128, 2, 8], F32)
        load_weight_cast(W1, w1_d, "W1", nsplit=8)
        load_weight_cast(W2, w2_d, "W2", nsplit=4)
        load_gb(gb, gname, bname)
        ng = len(groups)

        def stage_T(g):
            load_transposed(groups[g][0], xin, xT, g % 2, 0)

        def stage_A(g):
            slot = g % 2
            for hc in range(22):
                ba = (hc % 2) * 2
                bu = ba + 1
                for k in range(8):
                    P.op("pe", lambda e, k=k, hc=hc, ba=ba: e.matmul(ps[:, ba, :], lhsT=W1[:, k, hc * 128:(hc + 1) * 128],
                                                                     rhs=xT[:, slot, k, :], start=(k == 0), stop=(k == 7)),
                         reads=["W1", ("xT", slot)], writes=[bank(ba)])
                for k in range(8):
                    P.op("pe", lambda e, k=k, hc=hc, bu=bu: e.matmul(ps[:, bu, :], lhsT=W1[:, k, DFF + hc * 128:DFF + (hc + 1) * 128],
                                                                     rhs=xT[:, slot, k, :], start=(k == 0), stop=(k == 7)),
                         reads=["W1", ("xT", slot)], writes=[bank(bu)])
                ss = hc % 2
                P.op("act", lambda e, ss=ss, ba=ba: e.activation(out=sil[:, ss, :], in_=ps[:, ba, :], func=AF.Silu),
                     reads=[bank(ba)], writes=[("sil", ss)])
                P.op("dve", lambda e, ss=ss, bu=bu, hc=hc: e.scalar_tensor_tensor(out=hT[:, hc, :], in0=sil[:, ss, :], scalar=0.5,
                                                                                  in1=ps[:, bu, :], op0=ALU.mult, op1=ALU.mult),
                     reads=[("sil", ss), bank(bu)], writes=["hT"])

        def stage_B(g):
            src, dst = groups[g]
            for t in range(4):
                rs = t % 2
                b0 = 4 + (t % 2) * 2
                P.op("sp", lambda e, t=t, rs=rs: e.dma_start(out=rr[:, rs, :], in_=src[t * 128:(t + 1) * 128, :]),
                     writes=[("rr", rs)], dma_key="rr%d" % rs)
                for half in range(2):
                    for k in range(22):
                        P.op("pe", lambda e, k=k, t=t, half=half, b0=b0: e.matmul(ps[:, b0 + half, :], lhsT=hT[:, k, t * 128:(t + 1) * 128],
                                                                                  rhs=W2[:, k, half * 512:(half + 1) * 512],
                                                                                  start=(k == 0), stop=(k == 21)),
                             reads=["hT", "W2"], writes=[bank(b0 + half)])
                psy = ps[:, b0:b0 + 2, :].rearrange("p a b -> p (a b)")
                P.op("dve", lambda e, rs=rs, psy=psy: e.scalar_tensor_tensor(out=rr[:, rs, :], in0=rr[:, rs, :], scalar=ALPHA, in1=psy,
                                                                            op0=ALU.mult, op1=ALU.add),
                     reads=[("rr", rs), bank(b0), bank(b0 + 1)], writes=[("rr", rs)])
                layer_norm_tile(rr, rs, gb, st6, mv, ("rr", rs))
                P.op("pool", lambda e, t=t, rs=rs: e.dma_start(out=dst[t * 128:(t + 1) * 128, :], in_=rr[:, rs, :]),
                     reads=[("rr", rs)], dma_key="rrst%d" % rs, queue="pool")

        stage_T(0)
        for g in range(ng):
            stage_A(g)
            if g + 1 < ng:
                stage_T(g + 1)
            stage_B(g)
        P.barrier()


    def proj_phase():
        A.reset()
        Wi = A.alloc("Wi", [128, 8, IN_COLS], BF16)
        wuq = A.alloc("wuq", [128, 3, 768], BF16)
        wuq_rh = A.alloc("wuqrh", [128, 3, 256], BF16)
        wuk = A.alloc("wuk", [128, 2, 512], BF16)
        wuv = A.alloc("wuv", [128, 2, 512], BF16)
        wkr_rh = A.alloc("wkrrh", [128, 8, 32], BF16)
        qg = A.alloc("qg", [128, 4], F32)
        kvg = A.alloc("kvg", [128, 2], F32)
        bg = A.alloc("bg", [128, 16], F32)
        xin = A.alloc("xin", [128, 4, D], F32)
        epsc = A.alloc("epsc", [128, 1], F32)
        xT = A.alloc("xT", [128, 2, 8, G], BF16)
        cq = A.alloc("cq", [128, 3, G], F32)
        sq = A.alloc("sq", [128, 3, G], F32)
        ckr = A.alloc("ckr", [128, 2, G], F32)
        skv = A.alloc("skv", [128, 2, G], F32)
        cqn = A.alloc("cqn", [128, 2, 3, G], BF16)
        ckvn = A.alloc("ckvn", [128, 2, 2, G], BF16)
        rstd = A.alloc("rstd", [128, 2, G], F32)
        cs = A.alloc("cs", [128, 2, 2, G], F32)
        t1 = A.alloc("t1", [128, 2, G], F32)
        t2 = A.alloc("t2", [128, 2, G], F32)
        naqs = A.alloc("naqs", [128, 4, G], BF16)
        naks = A.alloc("naks", [128, 4, G], BF16)
        navs = A.alloc("navs", [128, 4, 512], BF16)
        off_qst = A.cur
        qst = A.alloc("qst", [128, 8, G], BF16)
        kst = A.alloc("kst", [128, 4, G], BF16)
        vst = A.alloc("vst", [128, 4, 512], BF16)
        krs = A.alloc("krs", [128, G], BF16)
        off_gst = A.cur
        gst = A.alloc("gst", [128, 16, G], BF16)
        stg_q = nc.alloc_sbuf_tensor_at("stgq_alias", [128, 3, 768], F32, offset=off_gst)
        stg_kv = nc.alloc_sbuf_tensor_at("stgkv_alias", [128, 2, 1024], F32, offset=off_qst)

        def col_load(dst, src1d, n, key):
            for k in range(n):
                P.op("sp", lambda e, k=k: e.dma_start(out=dst[:, k:k + 1], in_=src1d[k * 128:(k + 1) * 128].rearrange("(p o) -> p o", o=1)),
                     writes=[key], dma_key=key + "_ld")

        P.op("pool", lambda e: e.memset(epsc[:], RMS_EPS), writes=["epsc"])
        load_weight_cast(Wi, w_in, "Wi", nsplit=8)
        col_load(qg, q_norm_g, 3, "qg")
        col_load(kvg, kv_norm_g, 2, "kvg")
        col_load(bg, b_gate, 16, "bg")
        P.op("sp", lambda e: e.dma_start(out=stg_q[:], in_=w_uq.rearrange("(k p) n -> p k n", p=128)), writes=["gst"], dma_key="stgq")
        P.op("sp", lambda e: e.dma_start(out=stg_kv[:], in_=w_ukv.rearrange("(k p) n -> p k n", p=128)), writes=["qst"], dma_key="stgkv")
        for k in range(3):
            P.op("dve", lambda e, k=k: e.tensor_scalar(out=wuq[:, k, :], in0=stg_q[:, k, :], scalar1=qg[:, k:k + 1], scalar2=None, op0=ALU.mult),
                 reads=["gst", "qg"], writes=["wuq"])
            v = wuq[:, k, :].rearrange("p (h t) -> p h t", t=96)
            o = wuq_rh[:, k, :].rearrange("p (h t) -> p h t", t=32)
            P.op("act", lambda e, v=v, o=o: e.mul(out=o[:, :, 0:16], in_=v[:, :, 80:96], mul=-1.0), reads=["wuq"], writes=["wuqrh"])
            P.op("act", lambda e, v=v, o=o: e.copy(out=o[:, :, 16:32], in_=v[:, :, 64:80]), reads=["wuq"], writes=["wuqrh"])
        for k in range(2):
            sv = stg_kv[:, k, :].rearrange("p (h t) -> p h t", t=128)
            P.op("dve", lambda e, k=k, sv=sv: e.tensor_scalar(out=wuk[:, k, :].rearrange("p (h d) -> p h d", d=64), in0=sv[:, :, 0:64],
                                                              scalar1=kvg[:, k:k + 1], scalar2=None, op0=ALU.mult),
                 reads=["qst", "kvg"], writes=["wuk"])
            P.op("dve", lambda e, k=k, sv=sv: e.tensor_scalar(out=wuv[:, k, :].rearrange("p (h d) -> p h d", d=64), in0=sv[:, :, 64:128],
                                                              scalar1=kvg[:, k:k + 1], scalar2=None, op0=ALU.mult),
                 reads=["qst", "kvg"], writes=["wuv"])
        KR0 = 2176
        P.op("act", lambda e: e.mul(out=wkr_rh[:, :, 0:16], in_=Wi[:, :, KR0 + 16:KR0 + 32], mul=-1.0), reads=["Wi"], writes=["wkrrh"])
        P.op("act", lambda e: e.copy(out=wkr_rh[:, :, 16:32], in_=Wi[:, :, KR0:KR0 + 16]), reads=["Wi"], writes=["wkrrh"])

        rot = [4]

        def nb():
            b = rot[0]
            rot[0] = 4 + (rot[0] - 3) % 4
            return b

        evt = [0]

        def evac(dst, src, rd, wr, scale=None, eng=None):
            if eng is None:
                eng = "act" if evt[0] % 2 == 0 else "dve"
                evt[0] += 1
            if eng == "act":
                if scale is None:
                    P.op("act", lambda e: e.activation(out=dst, in_=src, func=AF.Copy), reads=rd, writes=wr)
                else:
                    P.op("act", lambda e: e.activation(out=dst, in_=src, func=AF.Copy, scale=scale), reads=rd, writes=wr)
            else:
                assert scale is None
                P.op("dve", lambda e: e.tensor_copy(out=dst, in_=src), reads=rd, writes=wr)

        def mm8(out, c0, c1, slot, b):
            for k in range(8):
                P.op("pe", lambda e, k=k: e.matmul(out, lhsT=Wi[:, k, c0:c1], rhs=xT[:, slot, k, :], start=(k == 0), stop=(k == 7)),
                     reads=["Wi", ("xT", slot)], writes=[bank(b)])

        def rms_mm(c0, nch, slot, raw, sqb, rkey, skey):
            for ci in range(nch):
                b = nb()
                mm8(ps[:, b, :], c0 + ci * 128, c0 + (ci + 1) * 128, slot, b)
                P.op("act", lambda e, ci=ci, b=b: e.activation(out=raw[:, ci, :], in_=ps[:, b, :], func=AF.Copy), reads=[bank(b)], writes=[rkey])
                P.op("act", lambda e, ci=ci, b=b: e.activation(out=sqb[:, ci, :], in_=ps[:, b, :], func=AF.Square), reads=[bank(b)], writes=[skey])

        def rms_fin(nch, dim, rslot, raw, sqb, rkey, skey, dstn, dkey):
            b = nb()
            for ci in range(nch):
                P.op("pe", lambda e, ci=ci, b=b: e.matmul(ps[:, b, :], lhsT=ones_f[:], rhs=sqb[:, ci, :], start=(ci == 0), stop=(ci == nch - 1)),
                     reads=["onesf", skey], writes=[bank(b)])
            rk = ("rstd", rslot)
            P.op("act", lambda e, b=b: e.activation(out=rstd[:, rslot, :], in_=ps[:, b, :], func=AF.Ln, scale=1.0 / dim, bias=epsc[:, 0:1]),
                 reads=[bank(b), "epsc"], writes=[rk])
            P.op("act", lambda e: e.activation(out=rstd[:, rslot, :], in_=rstd[:, rslot, :], func=AF.Exp, scale=-0.5), reads=[rk], writes=[rk])
            for ci in range(nch):
                P.op("dve" if ci % 2 == 0 else "pool", lambda e, ci=ci: e.tensor_tensor(out=dstn[:, ci, :], in0=raw[:, ci, :], in1=rstd[:, rslot, :], op=ALU.mult),
                     reads=[rkey, rk], writes=[dkey])

        def st(dst, src, rd, key):
            P.op("pool", lambda e: e.dma_start(out=dst, in_=src), reads=rd, dma_key=key, queue="pool")

        def info(gi):
            kind, g = all_groups[gi]
            return kind, g, own_off(kind, g), tall_off(kind, g), gi % 2

        def stage_T(gi):
            kind, g, own, toff, sl = info(gi)
            load_transposed(x1_d[toff:toff + G, :], xin, xT, sl, 0)
            rope_src = ropep[:, :, g * G:(g + 1) * G] if kind == "p" else ropes[:, :, (g % c.NSG) * G:(g % c.NSG + 1) * G]
            for i in range(2):
                P.op("sp", lambda e, i=i: e.dma_start(out=cs[64:96, sl, i, :], in_=rope_src[i]), writes=[("cs", sl)], dma_key="cs%d" % sl)

        def stage_Amm(gi):
            kind, g, own, toff, sl = info(gi)
            if own is not None:
                rms_mm(1536, 3, sl, cq, sq, "cq", "sq")
            rms_mm(1920, 2, sl, ckr, skv, "ckr", "skv")
            b1 = nb()
            mm8(ps[64:96, b1, :], KR0, KR0 + 32, sl, b1)
            b2 = nb()
            for k in range(8):
                P.op("pe", lambda e, k=k, b2=b2: e.matmul(ps[64:96, b2, :], lhsT=wkr_rh[:, k, :], rhs=xT[:, sl, k, :], start=(k == 0), stop=(k == 7)),
                     reads=["wkrrh", ("xT", sl)], writes=[bank(b2)])
            P.op("dve", lambda e, b1=b1: e.tensor_tensor(out=t1[64:96, 0, :], in0=ps[64:96, b1, :], in1=cs[64:96, sl, 0, :], op=ALU.mult),
                 reads=[bank(b1), ("cs", sl)], writes=[("t1", 0)])
            P.op("dve", lambda e, b2=b2: e.tensor_tensor(out=t2[64:96, 0, :], in0=ps[64:96, b2, :], in1=cs[64:96, sl, 1, :], op=ALU.mult),
                 reads=[bank(b2), ("cs", sl)], writes=[("t2", 0)])
            P.op("pool", lambda e: e.tensor_tensor(out=krs[64:96, :], in0=t1[64:96, 0, :], in1=t2[64:96, 0, :], op=ALU.add),
                 reads=[("t1", 0), ("t2", 0)], writes=["krs"])
            st(kr_d[:, toff:toff + G], krs[64:96, :], ["krs"], "krs_st")

        def stage_Afin(gi):
            kind, g, own, toff, sl = info(gi)
            if own is not None:
                rms_fin(3, 384.0, 0, cq, sq, "cq", "sq", cqn[:, sl], ("cqn", sl))
            rms_fin(2, 256.0, 1, ckr, skv, "ckr", "skv", ckvn[:, sl], ("ckvn", sl))

        def stage_B1(gi):
            kind, g, own, toff, sl = info(gi)
            halo_a = kind == "p" and g == c.NOG
            halo_b = kind == "p" and g == c.NPG - 1
            if own is not None:
                for ci in range(4):
                    b = nb()
                    mm8(ps[:, b, :], ci * 128, (ci + 1) * 128, sl, b)
                    evac(naqs[:, ci, :], ps[:, b, :], [bank(b)], ["naqs"], scale=0.125, eng="act")
                st(naq_d[:, :, own:own + G], naqs[:], ["naqs"], "naqs_st")
            if own is not None or halo_a or halo_b:
                for ci in range(4):
                    b = nb()
                    mm8(ps[:, b, :], 512 + ci * 128, 512 + (ci + 1) * 128, sl, b)
                    evac(naks[:, ci, :], ps[:, b, :], [bank(b)], ["naks"])
                for t in range(4):
                    b = nb()
                    for k in range(8):
                        P.op("pe", lambda e, k=k, t=t, b=b: e.matmul(ps[:, b, :], lhsT=xT[:, sl, k, t * 128:(t + 1) * 128], rhs=Wi[:, k, 1024:1536],
                                                                     start=(k == 0), stop=(k == 7)),
                             reads=["Wi", ("xT", sl)], writes=[bank(b)])
                    evac(navs[:, t, :], ps[:, b, :], [bank(b)], ["navs"])
                if kind == "s":
                    st(naks_d[:, :, g * G:(g + 1) * G], naks[:], ["naks"], "naks_st")
                    st(navs_d[g * G:(g + 1) * G, :].rearrange("(t p) d -> p t d", p=128), navs[:], ["navs"], "navs_st")
                elif own is not None:
                    st(nakp_d[:, :, 256 + own:256 + own + G], naks[:], ["naks"], "naks_st")
                    st(navp_d[256 + own:256 + own + G, :].rearrange("(t p) d -> p t d", p=128), navs[:], ["navs"], "navs_st")
                elif halo_a:
                    st(nakp_d[:, :, 256 + QP:256 + QP + 256], naks[:, :, 0:256], ["naks"], "naks_st")
                    st(navp_d[256 + QP:256 + QP + 256, :].rearrange("(t p) d -> p t d", p=128), navs[:, 0:2, :], ["navs"], "navs_st")
                else:
                    st(nakp_d[:, :, 0:256], naks[:, :, 256:512], ["naks"], "naks_st")
                    st(navp_d[0:256, :].rearrange("(t p) d -> p t d", p=128), navs[:, 2:4, :], ["navs"], "navs_st")
            for hp in range(4):
                b = nb()
                for hh in range(2):
                    h = 2 * hp + hh
                    for k in range(2):
                        P.op("pe", lambda e, k=k, h=h, hh=hh, b=b: e.matmul(ps[hh * 64:(hh + 1) * 64, b, :], lhsT=wuk[:, k, h * 64:(h + 1) * 64],
                                                                            rhs=ckvn[:, sl, k, :], start=(k == 0), stop=(k == 1)),
                             reads=["wuk", ("ckvn", sl)], writes=[bank(b)])
                evac(kst[:, hp, :], ps[:, b, :], [bank(b)], ["kst"])
            st(kt_d[:, :, toff:toff + G].rearrange("(c a) d t -> (a d) c t", a=2), kst[:], ["kst"], "kst_st")

        def stage_B2(gi):
            kind, g, own, toff, sl = info(gi)
            for t in range(4):
                b = nb()
                for k in range(2):
                    P.op("pe", lambda e, k=k, t=t, b=b: e.matmul(ps[:, b, :], lhsT=ckvn[:, sl, k, t * 128:(t + 1) * 128], rhs=wuv[:, k, :],
                                                                 start=(k == 0), stop=(k == 1)),
                         reads=["wuv", ("ckvn", sl)], writes=[bank(b)])
                evac(vst[:, t, :], ps[:, b, :], [bank(b)], ["vst"])
            st(v_d[toff:toff + G, :].rearrange("(t p) d -> p t d", p=128), vst[:], ["vst"], "vst_st")
            if own is None:
                return
            for h in range(8):
                b = nb()
                for k in range(3):
                    P.op("pe", lambda e, k=k, h=h, b=b: e.matmul(ps[0:96, b, :], lhsT=wuq[:, k, h * 96:(h + 1) * 96], rhs=cqn[:, sl, k, :],
                                                                 start=(k == 0), stop=(k == 2)),
                         reads=["wuq", ("cqn", sl)], writes=[bank(b)])
                b2 = nb()
                for k in range(3):
                    P.op("pe", lambda e, k=k, h=h, b2=b2: e.matmul(ps[64:96, b2, :], lhsT=wuq_rh[:, k, h * 32:(h + 1) * 32], rhs=cqn[:, sl, k, :],
                                                                   start=(k == 0), stop=(k == 2)),
                         reads=["wuqrh", ("cqn", sl)], writes=[bank(b2)])
                ts = h % 2
                P.op("act", lambda e, h=h, b=b: e.activation(out=qst[0:64, h, :], in_=ps[0:64, b, :], func=AF.Copy), reads=[bank(b)], writes=["qst"])
                P.op("dve", lambda e, b=b, ts=ts: e.tensor_tensor(out=t1[64:96, ts, :], in0=ps[64:96, b, :], in1=cs[64:96, sl, 0, :], op=ALU.mult),
                     reads=[bank(b), ("cs", sl)], writes=[("t1", ts)])
                P.op("dve", lambda e, b2=b2, ts=ts: e.tensor_tensor(out=t2[64:96, ts, :], in0=ps[64:96, b2, :], in1=cs[64:96, sl, 1, :], op=ALU.mult),
                     reads=[bank(b2), ("cs", sl)], writes=[("t2", ts)])
                P.op("pool", lambda e, h=h, ts=ts: e.tensor_tensor(out=qst[64:96, h, :], in0=t1[64:96, ts, :], in1=t2[64:96, ts, :], op=ALU.add),
                     reads=[("t1", ts), ("t2", ts)], writes=["qst"])
            st(qt_d[:, :, own:own + G].rearrange("h d t -> d h t"), qst[0:96, :, :], ["qst"], "qst_st")
            for ci in range(16):
                b = nb()
                mm8(ps[:, b, :], 2208 + ci * 128, 2208 + (ci + 1) * 128, sl, b)
                P.op("act", lambda e, ci=ci, b=b: e.activation(out=gst[:, ci, :], in_=ps[:, b, :], func=AF.Sigmoid, bias=bg[:, ci:ci + 1]),
                     reads=[bank(b), "bg"], writes=["gst"])
            st(gt_d[:, :, own:own + G], gst[:], ["gst"], "gst_st")

        ng = len(all_groups)
        stage_T(0)
        stage_Amm(0)
        stage_Afin(0)
        for gi in range(ng):
            if gi + 1 < ng:
                stage_T(gi + 1)
            stage_B1(gi)
            if gi + 1 < ng:
                stage_Amm(gi + 1)
            stage_B2(gi)
            if gi + 1 < ng:
                stage_Afin(gi + 1)
        P.barrier()

    def na_phase():
        A.reset()
        NKMAX = max(QP + 512, SS)
        NQMAX = max(QP, SS)
        TTi = A.alloc("TTi", [128, 8, 5, 128], BF16)
        TTg = A.alloc("TTg", [128, 8, 7, 128], BF16)
        tzs = A.alloc("tzs", [128, 2, 7, 128], F32)
        mi = A.alloc("mi", [128, 5, 128], F32)
        mc = A.alloc("mc", [128, 128], F32)
        NRM = 3 * 4 * 7 * 128
        rm = A.alloc("rm", [2, NRM + 128], BF16)
        nak = A.alloc("nak", [128, 4, NKMAX], BF16)
        nav1 = A.alloc("nav1", [128, NKMAX // 128, 8, 128], BF16)
        naq = A.alloc("naq", [128, 4, NQMAX], BF16)
        nao = A.alloc("nao", [128, 2, 4, 128], BF16)
        PTn = A.alloc("PTn", [128, 2, 896], BF16)
        recn = A.alloc("recn", [128, 2, 128], F32)
        nv5 = nav1[:].rearrange("p c (hp two) d -> p c hp two d", two=2)
        P.op("pool", lambda e: e.memset(nv5[:, :, :, 0, 64:128], 1.0), writes=["nav1"])
        P.op("pool", lambda e: e.memset(nv5[:, :, :, 1, 0:64], 1.0), writes=["nav1"])
        P.op("sp", lambda e: e.dma_start(out=mi[:], in_=mint[:, :, :]), writes=["mi"], dma_key="mi")
        P.op("sp", lambda e: e.dma_start(out=mc[:], in_=mcol[:, :]), writes=["mc"], dma_key="mc")
        P.op("pool", lambda e: e.dma_start(out=rm[:], in_=rmask[:, :]), writes=["rm"], dma_key="rm", queue="pool")
        for h in range(8):
            sl = h % 2
            for a in range(2):
                for b in range(2):
                    dr0 = 1 + a - b
                    for dd in range(7):
                        P.op("sp", lambda e, h=h, a=a, b=b, dd=dd, dr0=dr0, sl=sl: e.dma_start(
                            out=tzs[a * 64:(a + 1) * 64, sl, dd, b * 64:(b + 1) * 64], in_=tz[h, dr0 + 2 * dd, :, :]),
                            writes=[("tzs", sl)], dma_key="tzs%d" % sl, group=True)
            for dd in range(7):
                P.op("dve", lambda e, h=h, dd=dd, sl=sl: e.tensor_tensor(out=TTg[:, h, dd, :], in0=tzs[:, sl, dd, :], in1=mc[:], op=ALU.add),
                     reads=[("tzs", sl), "mc"], writes=["TTg"])
            for di in range(5):
                P.op("dve", lambda e, h=h, di=di, sl=sl: e.tensor_tensor(out=TTi[:, h, di, :], in0=tzs[:, sl, di + 1, :], in1=mi[:, di, :], op=ALU.add),
                     reads=[("tzs", sl), "mi"], writes=["TTi"])

        blocks = [(0, "p", c.RP, 0, QP, nakp_d, navp_d, 0, QP + 512, 2)]
        for i in range(NS):
            blocks.append((1 + i, "s", c.RS, QP + i * SS, SS, naks_d, navs_d, i * SS, SS, 0))
        pcount = [0]
        for (blk, kind, R, q_off, NQ, kd, vd, koff, NK, joff) in blocks:
            NP = R // 2
            for ci in range(4):
                P.op("sp", lambda e, ci=ci, kd=kd, koff=koff, NK=NK: e.dma_start(out=nak[:, ci, 0:NK], in_=kd[:, ci, koff:koff + NK]),
                     writes=["nak"], dma_key="nak")
                P.op("sp", lambda e, ci=ci, q_off=q_off, NQ=NQ: e.dma_start(out=naq[:, ci, 0:NQ], in_=naq_d[:, ci, q_off:q_off + NQ]),
                     writes=["naq"], dma_key="naq")
            nch = NK // 128
            vsrc = vd[koff:koff + NK, :].rearrange("(c p) (hp two d) -> p c hp two d", p=128, two=2, d=64)
            for c0 in range(nch):
                for two in range(2):
                    P.op("sp", lambda e, c0=c0, two=two, vsrc=vsrc: e.dma_start(
                        out=nv5[:, c0, :, two, two * 64:two * 64 + 64], in_=vsrc[:, c0, :, two, :]),
                        writes=["nav1"], dma_key="nav", group=True)
            eps_ = edge_pairs(NP)

            def do_pair(p, blk, kind, R, joff, eps_, q_off):
                edge = p in eps_
                dls = delta_list(kind, p, R) if edge else [-2, -1, 0, 1, 2]
                n = len(dls)
                assert dls == list(range(dls[0], dls[0] + n))
                pi = eps_.index(p) if edge else -1
                osl = pcount[0] % 2
                pcount[0] += 1
                pieces = [(0, min(n, 4))] + ([(4, n)] if n > 4 else [])

                def S(h):
                    sb = (h % 2) * 2
                    hp = h % 2
                    for (i0, i1) in pieces:
                        out = ps[:, sb + i0 // 4, 0:(i1 - i0) * 128]
                        wr = [bank(sb + i0 // 4)]
                        if edge:
                            d0 = dls[0] + 3 + i0
                            rhs = TTg[:, h, d0:d0 + (i1 - i0), :].rearrange("p a b -> p (a b)")
                            P.op("pe", lambda e, out=out, rhs=rhs: e.matmul(out, lhsT=ident_b[:], rhs=rhs, start=True, stop=False),
                                 reads=["identb", "TTg"], writes=wr)
                            idx = ((blk * 4 + pi) * 7 + d0) * 128
                            P.op("pe", lambda e, out=out, idx=idx, i0=i0, i1=i1: e.matmul(out, lhsT=rm[0:2, NRM:NRM + 128],
                                                                                         rhs=rm[0:2, idx:idx + (i1 - i0) * 128], start=False, stop=False),
                                 reads=["rm"], writes=wr)
                        else:
                            d0 = dls[0] + 2 + i0
                            rhs = TTi[:, h, d0:d0 + (i1 - i0), :].rearrange("p a b -> p (a b)")
                            P.op("pe", lambda e, out=out, rhs=rhs: e.matmul(out, lhsT=ident_b[:], rhs=rhs, start=True, stop=False),
                                 reads=["identb", "TTi"], writes=wr)
                    for i, dl in enumerate(dls):
                        tok0 = (p + dl + joff) * 128
                        out = ps[:, sb + i // 4, (i % 4) * 128:(i % 4 + 1) * 128]
                        P.op("pe", lambda e, out=out, tok0=tok0, i=i: e.matmul(
                            out, lhsT=nak[hp * 64:(hp + 1) * 64, h // 2, tok0:tok0 + 128],
                            rhs=naq[hp * 64:(hp + 1) * 64, h // 2, p * 128:(p + 1) * 128], start=False, stop=(i == n - 1 or i == 3)),
                            reads=["nak", "naq"], writes=[bank(sb + i // 4)])
                    if n <= 4:
                        src = ps[:, sb, 0:n * 128]
                        rd = [bank(sb)]
                    else:
                        src = ps[:, sb:sb + 2, :].rearrange("p a b -> p (a b)")[:, 0:n * 128]
                        rd = [bank(sb), bank(sb + 1)]
                    P.op("act", lambda e, src=src: e.activation(out=PTn[:, h % 2, 0:n * 128], in_=src, func=AF.Exp),
                         reads=rd, writes=[("PTn", h % 2)])

                def PV(h):
                    ob = 4 + h % 4
                    oslot = (h // 4) + 2 * (p % 2)
                    cols = slice(oslot * 128, (oslot + 1) * 128)
                    okey = bank(ob)
                    nh = slice((h % 2) * 64, (h % 2) * 64 + 64)
                    dh = slice((1 - h % 2) * 64, (1 - h % 2) * 64 + 64)
                    for i, dl in enumerate(dls):
                        ch = p + dl + joff
                        P.op("pe", lambda e, i=i, ch=ch: e.matmul(ps[:, ob, cols], lhsT=nav1[:, ch, h, :],
                                                                 rhs=PTn[:, h % 2, i * 128:(i + 1) * 128], start=(i == 0), stop=(i == n - 1)),
                             reads=["nav1", ("PTn", h % 2)], writes=[okey])
                    if h % 2 == 0:
                        P.op("act", lambda e: e.activation(out=recn[nh, h % 2, :], in_=ps[dh, ob, cols], func=AF.Ln), reads=[okey], writes=[("recn", h % 2)])
                        P.op("act", lambda e: e.activation(out=recn[nh, h % 2, :], in_=recn[nh, h % 2, :], func=AF.Exp, scale=-1.0),
                             reads=[("recn", h % 2)], writes=[("recn", h % 2)])
                    else:
                        P.op("dve", lambda e: e.reciprocal(out=recn[nh, h % 2, :], in_=ps[dh, ob, cols]), reads=[okey], writes=[("recn", h % 2)])
                    P.op("dve", lambda e: e.tensor_tensor(out=nao[nh, osl, h // 2, :], in0=ps[nh, ob, cols], in1=recn[nh, h % 2, :],
                                                          op=ALU.mult), reads=[okey, ("recn", h % 2)], writes=[("nao", osl)])

                S(0)
                for h in range(8):
                    if h + 1 < 8:
                        S(h + 1)
                    PV(h)
                P.op("pool", lambda e: e.dma_start(out=nao_d[:, :, q_off + p * 128:q_off + (p + 1) * 128], in_=nao[:, osl, :, :]),
                     reads=[("nao", osl)], dma_key="nao_st%d" % osl, queue="pool")

            for p in range(NP):
                do_pair(p, blk, kind, R, joff, eps_, q_off)
        P.barrier()

    def mla_phase():
        A.reset()
        NKMAX = max(SP, SS)
        NQMAX = max(QP, SS)
        KT = A.alloc("KT", [128, 2, NKMAX], BF16)
        V1 = A.alloc("V1", [128, 2, NKMAX // 128, 128], BF16)
        QT = A.alloc("QT", [128, 2, G], BF16)
        PT = A.alloc("PT", [128, 3, 2 * G], BF16)
        mlao = A.alloc("mlao", [128, 4, NQMAX], BF16)
        rec = A.alloc("rec", [128, 2, G], F32)
        P.op("pool", lambda e: e.memset(V1[:, 0, :, 64:128], 1.0), writes=[("V1", 0)])
        P.op("pool", lambda e: e.memset(V1[:, 1, :, 0:64], 1.0), writes=[("V1", 1)])
        seqs = [(0, QP, 0, SP)] + [(QP + i * SS, SS, SP + i * SS, SS) for i in range(NS)]
        scale = 96.0 ** -0.5
        cnt = [0]
        for (q_off, NQ, k_off, NK) in seqs:
            nkc = NK // 128
            for sl in range(2):
                P.op("sp", lambda e, sl=sl, k_off=k_off, NK=NK: e.dma_start(out=KT[64:96, sl, 0:NK], in_=kr_d[:, k_off:k_off + NK]),
                     writes=[("KTr", sl)], dma_key="KTr%d" % sl)
            for h in range(8):
                sl = h % 2
                vc = 0 if sl == 0 else 64
                nsp = 4 if NK >= 2048 else 1
                stp = NK // nsp
                for i in range(nsp):
                    P.op("sp", lambda e, i=i, h=h, sl=sl, stp=stp, k_off=k_off: e.dma_start(
                        out=KT[0:64, sl, i * stp:(i + 1) * stp], in_=kt_d[h, :, k_off + i * stp:k_off + (i + 1) * stp]),
                        writes=[("KT", sl)], dma_key="KT%d" % sl)
                    P.op("sp", lambda e, i=i, h=h, sl=sl, stp=stp, k_off=k_off, vc=vc: e.dma_start(
                        out=V1[:, sl, i * stp // 128:(i + 1) * stp // 128, vc:vc + 64],
                        in_=v_d[k_off + i * stp:k_off + (i + 1) * stp, h * 64:(h + 1) * 64].rearrange("(c p) d -> p c d", p=128)),
                        writes=[("V1", sl)], dma_key="V1%d" % sl)
                def do_qg(h, sl, qg, q_off, nkc):
                    nh = slice(sl * 64, sl * 64 + 64)
                    dh = slice((1 - sl) * 64, (1 - sl) * 64 + 64)
                    qs = cnt[0] % 2
                    ob = 6 + cnt[0] % 2
                    cnt[0] += 1
                    P.op("sp", lambda e, h=h, qs=qs, qg=qg, q_off=q_off: e.dma_start(
                        out=QT[0:96, qs, :], in_=qt_d[h, :, q_off + qg * G:q_off + (qg + 1) * G]), writes=[("QT", qs)], dma_key="QT%d" % qs)

                    def QK2(kp):
                        sbp = (kp % 3) * 2
                        for i in range(2):
                            kc = 2 * kp + i
                            P.op("pe", lambda e, kc=kc, i=i: e.matmul(ps[:, sbp + i, :], lhsT=KT[0:96, sl, kc * 128:(kc + 1) * 128], rhs=QT[0:96, qs, :],
                                                                      start=True, stop=True),
                                 reads=[("KT", sl), ("KTr", sl), ("QT", qs)], writes=[bank(sbp + i)])
                        P.op("act", lambda e: e.activation(out=PT[:, kp % 3, :], in_=ps[:, sbp:sbp + 2, :].rearrange("p a b -> p (a b)"),
                                                           func=AF.Exp, scale=scale),
                             reads=[bank(sbp), bank(sbp + 1)], writes=[("PT", kp % 3)])

                    def PV2(kp):
                        for i in range(2):
                            kc = 2 * kp + i
                            P.op("pe", lambda e, kc=kc, i=i: e.matmul(ps[:, ob, :], lhsT=V1[:, sl, kc, :], rhs=PT[:, kp % 3, i * G:(i + 1) * G],
                                                                      start=(kc == 0), stop=(kc == nkc - 1)),
                                 reads=[("V1", sl), ("PT", kp % 3)], writes=[bank(ob)])

                    nkp = nkc // 2
                    QK2(0)
                    if nkp > 1:
                        QK2(1)
                    for kp in range(nkp):
                        if kp + 2 < nkp:
                            QK2(kp + 2)
                        PV2(kp)
                    rs = qs
                    P.op("dve", lambda e, rs=rs, ob=ob: e.reciprocal(out=rec[nh, rs, :], in_=ps[dh, ob, :]), reads=[bank(ob)], writes=[("rec", rs)])
                    P.op("dve", lambda e, rs=rs, ob=ob, h=h, qg=qg: e.tensor_tensor(out=mlao[nh, h // 2, qg * G:(qg + 1) * G], in0=ps[nh, ob, :],
                                                                                   in1=rec[nh, rs, :], op=ALU.mult),
                         reads=[bank(ob), ("rec", rs)], writes=["mlao"])

                for qg in range(NQ // G):
                    do_qg(h, sl, qg, q_off, nkc)
            P.op("pool", lambda e, q_off=q_off, NQ=NQ: e.dma_start(out=mlao_d[:, :, q_off:q_off + NQ], in_=mlao[:, :, 0:NQ]),
                 reads=["mlao"], dma_key="mlao_st", queue="pool")
        P.barrier()

    def mix_phase():
        A.reset()
        wna = A.alloc("wna", [128, 4, D], BF16)
        wml = A.alloc("wml", [128, 4, D], BF16)
        wo = A.alloc("wo", [128, 8, D], BF16)
        gb = A.alloc("gb", [128, 2, D], F32)
        nat = A.alloc("nat", [128, 2, 4, G], BF16)
        mlt = A.alloc("mlt", [128, 2, 4, G], BF16)
        gt = A.alloc("gt", [128, 2, 16, G], BF16)
        mT = A.alloc("mT", [128, 2, 8, G], BF16)
        ta = A.alloc("ta", [128, 2, G], F32)
        tb = A.alloc("tb", [128, 2, G], F32)
        rr = A.alloc("rr", [128, 4, D], F32)
        st6 = A.alloc("st6", [128, 4, 2, 6], F32)
        mv = A.alloc("mv", [128, 4, 8], F32)
        load_weight_cast(wna, w_na_o, "wna")
        load_weight_cast(wml, w_mla_o, "wml")
        load_weight_cast(wo, w_out, "wo", nsplit=2)
        load_gb(gb, "ln2_g", "ln2_b")
        own_groups = [(kind, g) for (kind, g) in all_groups if own_off(kind, g) is not None]

        def stage_C(i):
            kind, g = own_groups[i]
            o = own_off(kind, g)
            sg = i % 2
            P.op("sp", lambda e, o=o, sg=sg: e.dma_start(out=nat[:, sg], in_=nao_d[:, :, o:o + G]), writes=[("nat", sg)], dma_key="nat%d" % sg)
            P.op("sp", lambda e, o=o, sg=sg: e.dma_start(out=mlt[:, sg], in_=mlao_d[:, :, o:o + G]), writes=[("mlt", sg)], dma_key="mlt%d" % sg)
            P.op("sp", lambda e, o=o, sg=sg: e.dma_start(out=gt[:, sg], in_=gt_d[:, :, o:o + G]), writes=[("gt", sg)], dma_key="gt%d" % sg)
            for c8 in range(8):
                ba = (c8 % 2) * 2
                bb = ba + 1
                s2 = c8 % 2
                for k in range(4):
                    P.op("pe", lambda e, k=k, c8=c8, ba=ba, sg=sg: e.matmul(ps[:, ba, :], lhsT=wna[:, k, c8 * 128:(c8 + 1) * 128], rhs=nat[:, sg, k, :],
                                                                            start=(k == 0), stop=(k == 3)), reads=["wna", ("nat", sg)], writes=[bank(ba)])
                for k in range(4):
                    P.op("pe", lambda e, k=k, c8=c8, bb=bb, sg=sg: e.matmul(ps[:, bb, :], lhsT=wml[:, k, c8 * 128:(c8 + 1) * 128], rhs=mlt[:, sg, k, :],
                                                                            start=(k == 0), stop=(k == 3)), reads=["wml", ("mlt", sg)], writes=[bank(bb)])
                P.op("dve", lambda e, c8=c8, ba=ba, s2=s2, sg=sg: e.tensor_tensor(out=ta[:, s2, :], in0=ps[:, ba, :], in1=gt[:, sg, c8, :], op=ALU.mult),
                     reads=[bank(ba), ("gt", sg)], writes=[("ta", s2)])
                P.op("dve", lambda e, c8=c8, bb=bb, s2=s2, sg=sg: e.tensor_tensor(out=tb[:, s2, :], in0=ps[:, bb, :], in1=gt[:, sg, 8 + c8, :], op=ALU.mult),
                     reads=[bank(bb), ("gt", sg)], writes=[("tb", s2)])
                P.op("dve", lambda e, c8=c8, s2=s2, sg=sg: e.tensor_tensor(out=mT[:, sg, c8, :], in0=ta[:, s2, :], in1=tb[:, s2, :], op=ALU.add),
                     reads=[("ta", s2), ("tb", s2)], writes=[("mT", sg)])

        def stage_O(i):
            kind, g = own_groups[i]
            o = own_off(kind, g)
            xo = tall_off(kind, g)
            sg = i % 2
            for t in range(4):
                rs = t
                b0 = 4 + (t % 2) * 2
                P.op("sp", lambda e, t=t, rs=rs, xo=xo: e.dma_start(out=rr[:, rs, :], in_=x1_d[xo + t * 128:xo + (t + 1) * 128, :]),
                     writes=[("rr", rs)], dma_key="rr%d" % rs)
                for half in range(2):
                    for k in range(8):
                        P.op("pe", lambda e, k=k, t=t, half=half, b0=b0, sg=sg: e.matmul(ps[:, b0 + half, :], lhsT=mT[:, sg, k, t * 128:(t + 1) * 128],
                                                                                         rhs=wo[:, k, half * 512:(half + 1) * 512],
                                                                                         start=(k == 0), stop=(k == 7)),
                             reads=[("mT", sg), "wo"], writes=[bank(b0 + half)])
                psy = ps[:, b0:b0 + 2, :].rearrange("p a b -> p (a b)")
                P.op("dve", lambda e, rs=rs, psy=psy: e.scalar_tensor_tensor(out=rr[:, rs, :], in0=rr[:, rs, :], scalar=ALPHA, in1=psy,
                                                                            op0=ALU.mult, op1=ALU.add),
                     reads=[("rr", rs), bank(b0), bank(b0 + 1)], writes=[("rr", rs)])
                layer_norm_tile(rr, rs, gb, st6, mv, ("rr", rs))
                P.op("pool", lambda e, t=t, rs=rs, o=o: e.dma_start(out=x2_d[o + t * 128:o + (t + 1) * 128, :], in_=rr[:, rs, :]),
                     reads=[("rr", rs)], dma_key="rrst%d" % rs, queue="pool")

        nog = len(own_groups)
        stage_C(0)
        for i in range(nog):
            if i + 1 < nog:
                stage_C(i + 1)
            stage_O(i)
        P.barrier()

    if "P1" in phases:
        groups = [(x_src(k, g), x1_d[tall_off(k, g):tall_off(k, g) + G, :]) for (k, g) in all_groups]
        ffn_phase(groups, w_ffn1_in, w_ffn1_out, "ln1_g", "ln1_b")
    if "P2" in phases:
        proj_phase()
    if "P3" in phases:
        na_phase()
    if "P4" in phases:
        mla_phase()
    if "P5" in phases:
        mix_phase()
    if "P6" in phases:
        groups = []
        for (k, g) in all_groups:
            o = own_off(k, g)
            if o is None:
                continue
            dst = yp[o:o + G, :] if k == "p" else ys[o - QP:o - QP + G, :]
            groups.append((x2_d[o:o + G, :], dst))
        ffn_phase(groups, w_ffn2_in, w_ffn2_out, "ln3_g", "ln3_b")

    P.emit()
    return nc


def window_valid(qr, kr, R, top, bottom):
    ws = qr - 4
    if top:
        ws = max(ws, 0)
    if bottom:
        ws = min(ws, R - 8)
    return ws <= kr < ws + 8


def edge_pairs(NP):
    return [0, 1, NP - 2, NP - 1]


def delta_list(kind, p, R):
    NP = R // 2
    if kind == "p":
        variants = [(False, False), (True, False), (False, True)]
        jmin, jmax = -2, NP + 1
    else:
        variants = [(True, True)]
        jmin, jmax = 0, NP - 1
    out = []
    for dl in range(-3, 4):
        j = p + dl
        if j < jmin or j > jmax:
            continue
        ok = False
        for (top, bottom) in variants:
            for a in range(2):
                for b in range(2):
                    kr, qr = 2 * j + a, 2 * p + b
                    if top and kr < 0:
                        continue
                    if bottom and kr >= R:
                        continue
                    if window_valid(qr, kr, R, top, bottom):
                        ok = True
        if ok:
            out.append(dl)
    return out


def host_constants(cfg, core):
    c = cfg
    qtr = core % 4
    kc = np.arange(64)[:, None]
    cc = np.arange(64)[None, :]
    ws = np.clip(cc - 8, 0, 48)
    colv = (kc >= ws) & (kc < ws + 16)
    mcol64 = np.where(colv, 0.0, NEG).astype(np.float32)
    mcol = np.tile(mcol64, (2, 2))
    mint = np.zeros((128, 5, 128), np.float32)
    for di in range(5):
        dl = di - 2
        for a in range(2):
            for b in range(2):
                rv = -4 <= 2 * dl + a - b <= 3
                blk = mcol64 if rv else np.full((64, 64), NEG, np.float32)
                mint[a * 64:(a + 1) * 64, di, b * 64:(b + 1) * 64] = blk
    rm = np.zeros((2, 3, 4, 7, 128), np.float32)
    for blk in range(3):
        if blk == 0:
            R, top, bottom, kind = c.RP, qtr == 0, qtr == 3, "p"
        else:
            R, top, bottom, kind = c.RS, True, True, "s"
        NP = R // 2
        for pi, p in enumerate(edge_pairs(NP)):
            for di in range(7):
                dl = di - 3
                for a in range(2):
                    for b in range(2):
                        kr, qr = 2 * (p + dl) + a, 2 * p + b
                        v = window_valid(qr, kr, R, top, bottom)
                        rm[a, blk, pi, di, b * 64:(b + 1) * 64] = 0.0 if v else NEG
    inv = (1.0 / (10000.0 ** (np.arange(0, 32, 2, dtype=np.float32) / np.float32(32)))).astype(np.float32)

    def rope(pos):
        ang = pos.astype(np.float32)[:, None] * inv[None, :]
        cs = np.cos(ang).astype(np.float32).T
        sn = np.sin(ang).astype(np.float32).T
        return np.stack([np.concatenate([cs, cs], 0), np.concatenate([sn, sn], 0)], 0)

    posp = (np.arange(c.SP) + qtr * c.QP) % c.SP
    return {
        "mcol": mcol, "mint": mint,
        "rmask": np.ascontiguousarray(np.concatenate([rm.reshape(2, -1), np.kron(np.eye(2, dtype=np.float32), np.ones((1, 64), np.float32))], 1)),
        "ropep": np.ascontiguousarray(rope(posp)), "ropes": np.ascontiguousarray(rope(np.arange(c.SS))),
    }


def make_in_maps(inputs, cfg, used=None):
    c = cfg
    x_prompt = np.asarray(inputs["x_prompt"], np.float32)
    x_sample = np.asarray(inputs["x_sample"], np.float32)
    rpb = np.asarray(inputs["na_rpb"], np.float32)[0]
    kc = np.arange(64)[:, None]
    cc = np.arange(64)[None, :]
    tz = np.ascontiguousarray(rpb[:, :, np.clip(kc - cc + 15, 0, 30)])
    shared = {
        "ffn1_w_in": inputs["ffn1_w_in"][0], "ffn1_w_out": inputs["ffn1_w_out"][0],
        "ffn2_w_in": inputs["ffn2_w_in"][0], "ffn2_w_out": inputs["ffn2_w_out"][0],
        "ln1_g": inputs["ln1_g"], "ln1_b": inputs["ln1_b"], "ln2_g": inputs["ln2_g"], "ln2_b": inputs["ln2_b"],
        "ln3_g": inputs["ln3_g"], "ln3_b": inputs["ln3_b"],
        "w_in": inputs["w_in"][0], "b_gate": inputs["b_gate"][0], "q_norm_g": inputs["q_norm_g"][0],
        "kv_norm_g": inputs["kv_norm_g"][0], "w_uq": inputs["w_uq"][0], "w_ukv": inputs["w_ukv"][0],
        "w_na_o": inputs["w_na_o"][0], "w_mla_o": inputs["w_mla_o"][0], "w_out": inputs["w_out"][0], "tz": tz,
    }
    shared = {k: np.ascontiguousarray(np.asarray(v, np.float32)) for k, v in shared.items()}
    maps = []
    for core in range(NCORES):
        b, qtr = core // 4, core % 4
        m = dict(shared)
        m["xp"] = np.ascontiguousarray(np.roll(x_prompt[b], -qtr * c.QP, axis=0))
        m["xs"] = np.ascontiguousarray(x_sample[c.NS * core:c.NS * (core + 1)].reshape(c.NS * c.SS, D))
        m.update(host_constants(c, core))
        if used is not None:
            m = {k: v for k, v in m.items() if k in used}
        maps.append(m)
    return maps


_CACHE = {}


def kernel(**inputs):
    cfg = Cfg()
    if "nc" not in _CACHE:
        _CACHE["nc"] = build_program(cfg)
    nc = _CACHE["nc"]
    maps = make_in_maps(inputs, cfg)
    res = run_bass_kernel_spmd(nc, maps, core_ids=list(range(NCORES)))
    y_prompt = np.empty((2, cfg.SP, D), np.float32)
    y_sample = np.empty((NCORES * cfg.NS, cfg.SS, D), np.float32)
    for core in range(NCORES):
        b, qtr = core // 4, core % 4
        r = res.results[core]
        y_prompt[b, qtr * cfg.QP:(qtr + 1) * cfg.QP] = np.asarray(r["yp"], np.float32)
        y_sample[cfg.NS * core:cfg.NS * (core + 1)] = np.asarray(r["ys"], np.float32).reshape(cfg.NS, cfg.SS, D)
    return (y_prompt, y_sample)
```

```python
import bisect
import contextlib
import numpy as np
import concourse.bass as bass
import concourse.mybir as mybir
from concourse.bass_utils import run_bass_kernel_spmd

F32 = mybir.dt.float32
BF16 = mybir.dt.bfloat16
AF = mybir.ActivationFunctionType
ALU = mybir.AluOpType

D = 1024
DFF = 2816
G = 512
NCORES = 8
ALPHA = 2.0 ** 0.25
LN_EPS = 1e-5
RMS_EPS = 1e-6
NEG = -30000.0
IN_COLS = 4256
COMPUTE = ("pe", "act", "dve", "pool")


class Op:
    __slots__ = ("eng", "fn", "deps", "dma_key", "dma_cnt", "needs_inc", "inc_cnt", "idx", "wdeps", "gpos", "throttle")


class Prog:
    def __init__(self, nc):
        self.nc = nc
        self.ops = []
        self.last_w = {}
        self.readers = {}
        self.dma_total = {}
        self.eng_ops = {e: [] for e in ("pe", "act", "dve", "pool", "sp")}
        self.bar = {}
        self.phase = 0

    def barrier(self):
        bar = {}
        for e, lst in self.eng_ops.items():
            for o in reversed(lst):
                if o.dma_key is None:
                    bar[o.idx] = True
                    break
        last_dma = {}
        for o in self.ops:
            if o.dma_key is not None:
                last_dma[o.dma_key] = o.idx
        for i in last_dma.values():
            bar[i] = True
        self.bar = bar
        self.last_w = {}
        self.readers = {}
        self.phase += 1

    def op(self, eng, fn, reads=(), writes=(), dma_key=None, queue=None, group=False):
        o = Op()
        o.idx = len(self.ops)
        if dma_key is not None:
            dma_key = (self.phase, dma_key)
        o.eng = eng if dma_key is None else (queue or "sp")
        o.fn = fn
        o.dma_key = dma_key
        o.needs_inc = False
        o.inc_cnt = 0
        o.gpos = 0
        o.throttle = 0
        deps = dict(self.bar)
        for r in reads:
            w = self.last_w.get(r)
            if w is not None:
                deps[w] = True
        o.wdeps = {}
        for r in writes:
            w = self.last_w.get(r)
            dr = {}
            if w is not None:
                ow = self.ops[w]
                if group and dma_key is not None and ow.dma_key == dma_key and r in ow.wdeps:
                    dr.update(ow.wdeps[r])
                    o.gpos = ow.gpos + 1
                    if o.gpos >= 4:
                        o.throttle = self.dma_total[dma_key] - 16 * 4
                else:
                    dr[w] = False
            for rd in self.readers.get(r, ()):
                dr.setdefault(rd, False)
            o.wdeps[r] = dr
            for k, v in dr.items():
                deps.setdefault(k, v)
        o.deps = deps
        for r in reads:
            self.readers.setdefault(r, []).append(o.idx)
        for r in writes:
            self.last_w[r] = o.idx
            self.readers[r] = []
        if dma_key is not None:
            self.dma_total[dma_key] = self.dma_total.get(dma_key, 0) + 16
            o.dma_cnt = self.dma_total[dma_key]
        else:
            o.dma_cnt = 0
        self.ops.append(o)
        self.eng_ops[o.eng].append(o)
        return o

    def emit(self):
        nc = self.nc
        ops = self.ops
        for o in ops:
            real = []
            for d, is_raw in o.deps.items():
                od = ops[d]
                if od.dma_key is None:
                    if od.eng == o.eng and o.dma_key is None:
                        if o.eng == "pe" or not is_raw:
                            continue
                    od.needs_inc = True
                real.append(d)
            o.deps = real
        cnt = {e: 0 for e in self.eng_ops}
        for e, lst in self.eng_ops.items():
            for o in lst:
                if o.dma_key is None and o.needs_inc:
                    cnt[e] += 1
                o.inc_cnt = cnt[e]
        key_hist = {}
        for o in ops:
            if o.dma_key is not None:
                key_hist.setdefault(o.dma_key, []).append((o.idx, o.dma_cnt))
        key_idx = {k: [a for a, _ in v] for k, v in key_hist.items()}

        with contextlib.ExitStack() as st:
            esem = {e: st.enter_context(nc.semaphore("s_" + e)) for e in COMPUTE}
            dsem = {}
            for i, k in enumerate(key_hist):
                dsem[k] = st.enter_context(nc.semaphore("d%d" % i))
            block = st.enter_context(nc.Block())

            def run(ename, handle):
                seen = {}
                for o in self.eng_ops[ename]:
                    waits = {}
                    for d in o.deps:
                        od = ops[d]
                        if od.dma_key is not None:
                            k = od.dma_key
                            pos = bisect.bisect_left(key_idx[k], o.idx) - 1
                            c = key_hist[k][pos][1]
                            sk = ("d", k)
                        else:
                            c = od.inc_cnt
                            sk = ("e", od.eng)
                        if c > waits.get(sk, 0):
                            waits[sk] = c
                    if o.throttle > 0:
                        sk = ("d", o.dma_key)
                        if o.throttle > waits.get(sk, 0):
                            waits[sk] = o.throttle
                    for sk, c in waits.items():
                        if seen.get(sk, 0) >= c:
                            continue
                        seen[sk] = c
                        sem = dsem[sk[1]] if sk[0] == "d" else esem[sk[1]]
                        handle.wait_ge(sem, c)
                    ins = o.fn(handle)
                    if o.dma_key is not None:
                        ins.then_inc(dsem[o.dma_key], 16)
                    elif o.needs_inc:
                        ins.then_inc(esem[o.eng], 1)
                if ename == "sp":
                    for k, tot in self.dma_total.items():
                        handle.wait_ge(dsem[k], tot)
                    for e in COMPUTE:
                        if cnt[e] > 0:
                            handle.wait_ge(esem[e], cnt[e])

            @block.sync
            def _(e):
                run("sp", e)

            @block.tensor
            def _(e):
                run("pe", e)

            @block.scalar
            def _(e):
                run("act", e)

            @block.vector
            def _(e):
                run("dve", e)

            @block.gpsimd
            def _(e):
                run("pool", e)


class Arena:
    def __init__(self, nc):
        self.nc = nc
        self.base = (nc.sbuf_base + 63) // 64 * 64
        self.limit = nc.sbuf_top
        self.cur = self.base
        self.n = 0

    def pin(self):
        self.base = self.cur

    def reset(self):
        self.cur = self.base

    def alloc(self, name, shape, dt):
        esz = 4 if dt == F32 else 2
        nb = esz
        for s in shape[1:]:
            nb *= s
        off = self.cur
        self.cur += (nb + 63) // 64 * 64
        assert self.cur <= self.limit, (name, self.cur, self.limit)
        self.n += 1
        return self.nc.alloc_sbuf_tensor_at("%s_%d" % (name, self.n), list(shape), dt, offset=off)


class Cfg:
    def __init__(self, SP=16384, SS=4096, NS=2):
        self.SP = SP
        self.SS = SS
        self.NS = NS
        self.QP = SP // 4
        self.NPG = SP // G
        self.NOG = self.QP // G
        self.NSG = SS // G
        self.RP = self.QP // 64
        self.RS = SS // 64
        self.TOWN = self.QP + NS * SS
        self.TALL = SP + NS * SS


def build_program(cfg, debug=False, phases=("P1", "P2", "P3", "P4", "P5", "P6")):
    nc = bass.Bass("TRN2", target_bir_lowering=False)
    c = cfg
    SP, SS, NS, QP = c.SP, c.SS, c.NS, c.QP
    TOWN, TALL = c.TOWN, c.TALL

    def din(name, shape, dt=F32):
        return nc.dram_tensor(name, list(shape), dt, kind="ExternalInput").ap()

    def dout(name, shape, dt=F32):
        return nc.dram_tensor(name, list(shape), dt, kind="ExternalOutput").ap()

    def dscr(name, shape, dt):
        if debug:
            return nc.dram_tensor(name, list(shape), dt, kind="ExternalOutput").ap()
        return nc.dram_tensor(name, list(shape), dt).ap()

    xp = din("xp", [SP, D])
    xs = din("xs", [NS * SS, D])
    w_ffn1_in = din("ffn1_w_in", [D, 2 * DFF])
    w_ffn1_out = din("ffn1_w_out", [DFF, D])
    w_ffn2_in = din("ffn2_w_in", [D, 2 * DFF])
    w_ffn2_out = din("ffn2_w_out", [DFF, D])
    lnp = {k: din(k, [1, D]) for k in ("ln1_g", "ln1_b", "ln2_g", "ln2_b", "ln3_g", "ln3_b")}
    w_in = din("w_in", [D, IN_COLS])
    b_gate = din("b_gate", [2 * D])
    q_norm_g = din("q_norm_g", [384])
    kv_norm_g = din("kv_norm_g", [256])
    w_uq = din("w_uq", [384, 768])
    w_ukv = din("w_ukv", [256, 1024])
    w_na_o = din("w_na_o", [512, D])
    w_mla_o = din("w_mla_o", [512, D])
    w_out = din("w_out", [D, D])
    tz = din("tz", [8, 3, 64, 7, 64])
    mint = din("mint", [128, 5, 128])
    mcol = din("mcol", [128, 128])
    rmask = din("rmask", [2, 3 * 4 * 7 * 128 + 128])
    ropep = din("ropep", [2, 32, SP])
    ropes = din("ropes", [2, 32, SS])

    yp = dout("yp", [QP, D])
    ys = dout("ys", [NS * SS, D])

    x1_d = dscr("x1_d", [TALL, D], F32)
    x2_d = dscr("x2_d", [TOWN, D], F32)
    NAKP = QP + 512
    naq_d = dscr("naq_d", [128, 4, TOWN], BF16)
    nakp_d = dscr("nakp_d", [128, 4, NAKP], BF16)
    naks_d = dscr("naks_d", [128, 4, NS * SS], BF16)
    navp_d = dscr("navp_d", [NAKP, 512], BF16)
    navs_d = dscr("navs_d", [NS * SS, 512], BF16)
    qt_d = dscr("qt_d", [8, 96, TOWN], BF16)
    kt_d = dscr("kt_d", [8, 64, TALL], BF16)
    kr_d = dscr("kr_d", [32, TALL], BF16)
    v_d = dscr("v_d", [TALL, 512], BF16)
    gt_d = dscr("gt_d", [128, 16, TOWN], BF16)
    nao_d = dscr("nao_d", [128, 4, TOWN], BF16)
    mlao_d = dscr("mlao_d", [128, 4, TOWN], BF16)

    P = Prog(nc)
    A = Arena(nc)
    ps = nc.alloc_psum_tensor("ps", [128, 8, 512], F32)
    psb = ps.bitcast(BF16)

    def bank(i):
        return ("B", i)

    ident = A.alloc("ident", [128, 128], F32)
    ident_b = A.alloc("identb", [128, 128], BF16)
    ones_f = A.alloc("onesf", [128, 128], F32)
    ones_b = A.alloc("onesb", [128, 128], BF16)
    epsln = A.alloc("epsln", [128, 1], F32)
    A.pin()
    P.op("pool", lambda e: e.memset(ident[:], 0.0), writes=["ident"])
    P.op("pool", lambda e: e.affine_select(out=ident[:], in_=ident[:], pattern=[[-1, 128]], compare_op=ALU.not_equal,
                                           fill=1.0, base=0, channel_multiplier=1), reads=["ident"], writes=["ident"])
    P.op("pool", lambda e: e.tensor_copy(out=ident_b[:], in_=ident[:]), reads=["ident"], writes=["identb"])
    P.op("pool", lambda e: e.memset(ones_f[:], 1.0), writes=["onesf"])
    P.op("pool", lambda e: e.memset(ones_b[:], 1.0), writes=["onesb"])
    CONSTS = ["ident", "identb", "onesf", "onesb"]

    all_groups = []
    for g in range(c.NPG):
        all_groups.append(("p", g))
    for g in range(NS * c.NSG):
        all_groups.append(("s", g))

    def x_src(kind, g):
        return (xp if kind == "p" else xs)[g * G:(g + 1) * G, :]

    def tall_off(kind, g):
        return g * G if kind == "p" else SP + g * G

    def own_off(kind, g):
        if kind == "p":
            return g * G if g < c.NOG else None
        return QP + g * G


    def load_weight_cast(dst, src, key, nsplit=1):
        K = dst.shape[1]
        v = src.rearrange("(k p) n -> p k n", p=128)
        step = (K + nsplit - 1) // nsplit
        for k0 in range(0, K, step):
            k1 = min(K, k0 + step)
            P.op("pool", lambda e, k0=k0, k1=k1: e.dma_start(out=dst[:, k0:k1, :], in_=v[:, k0:k1, :]),
                 writes=[key], dma_key=key + "_ld", queue="pool", group=True)

    xin_ctr = [0]

    def load_transposed(src_rows, xin, xT, slot, bank0, xbf=None):
        nx = xin.shape[1]
        for t in range(4):
            s = xin_ctr[0] % nx
            xin_ctr[0] += 1
            P.op("sp", lambda e, t=t, s=s: e.dma_start(out=xin[:, s, :], in_=src_rows[t * 128:(t + 1) * 128, :]),
                 writes=[("xin", s)], dma_key="xin%d" % s)
            for half in range(2):
                b = bank0 + (t % 2) * 2 + half
                for j in range(4):
                    k = half * 4 + j
                    P.op("pe", lambda e, s=s, k=k, b=b, j=j: e.transpose(out=ps[:, b, j * 128:(j + 1) * 128],
                                                                         in_=xin[:, s, k * 128:(k + 1) * 128], identity=ident[:]),
                         reads=[("xin", s), "ident"], writes=[bank(b)])
                eng = "act" if half == 0 else "dve"
                dst = xT[:, slot, half * 4:(half + 1) * 4, t * 128:(t + 1) * 128]
                src = ps[:, b, :].rearrange("p (j n) -> p j n", n=128)
                if eng == "act":
                    P.op("act", lambda e, dst=dst, src=src: e.activation(out=dst, in_=src, func=AF.Copy),
                         reads=[bank(b)], writes=[("xT", slot)])
                else:
                    P.op("dve", lambda e, dst=dst, src=src: e.tensor_copy(out=dst, in_=src),
                         reads=[bank(b)], writes=[("xT", slot)])

    def layer_norm_tile(r, slot, gb, st6, mv, key):
        rv = r[:, slot, :]
        for cc in range(2):
            P.op("dve", lambda e, cc=cc: e.bn_stats(out=st6[:, slot, cc, :], in_=r[:, slot, cc * 512:(cc + 1) * 512]),
                 reads=[key], writes=[("st6", slot)])
        P.op("dve", lambda e: e.bn_aggr(out=mv[:, slot, 0:2], in_=st6[:, slot, :, :]), reads=[("st6", slot)], writes=[("mv", slot)])
        P.op("dve", lambda e: e.tensor_scalar(out=mv[:, slot, 2:3], in0=mv[:, slot, 1:2], scalar1=LN_EPS, scalar2=None, op0=ALU.add),
             reads=[("mv", slot)], writes=[("mv", slot)])
        P.op("act", lambda e: e.activation(out=mv[:, slot, 3:4], in_=mv[:, slot, 2:3], func=AF.Sqrt),
             reads=[("mv", slot)], writes=[("mv", slot)])
        P.op("dve", lambda e: e.reciprocal(out=mv[:, slot, 4:5], in_=mv[:, slot, 3:4]), reads=[("mv", slot)], writes=[("mv", slot)])
        P.op("dve", lambda e: e.tensor_scalar(out=mv[:, slot, 5:6], in0=mv[:, slot, 0:1], scalar1=mv[:, slot, 4:5], scalar2=-1.0,
                                              op0=ALU.mult, op1=ALU.mult), reads=[("mv", slot)], writes=[("mv", slot)])
        P.op("act", lambda e: e.activation(out=rv, in_=rv, func=AF.Identity, scale=mv[:, slot, 4:5], bias=mv[:, slot, 5:6]),
             reads=[key, ("mv", slot)], writes=[key])
        P.op("pool", lambda e: e.tensor_tensor(out=rv, in0=rv, in1=gb[:, 0, :], op=ALU.mult), reads=[key, "gb"], writes=[key])
        P.op("pool", lambda e: e.tensor_tensor(out=rv, in0=rv, in1=gb[:, 1, :], op=ALU.add), reads=[key, "gb"], writes=[key])

    def load_gb(gb, gname, bname):
        P.op("sp", lambda e: e.dma_start(out=gb[:, 0, :], in_=lnp[gname].partition_broadcast(128)), writes=["gb"], dma_key="gb")
        P.op("sp", lambda e: e.dma_start(out=gb[:, 1, :], in_=lnp[bname].partition_broadcast(128)), writes=["gb"], dma_key="gb")

    def ffn_phase(groups, w1_d, w2_d, gname, bname):
        A.reset()
        W1 = A.alloc("W1", [128, 8, 2 * DFF], BF16)
        W2 = A.alloc("W2", [128, 22, D], BF16)
        xin = A.alloc("xin", [128, 3, D], F32)
        xT = A.alloc("xT", [128, 2, 8, G], BF16)
        hT = A.alloc("hT", [128, 22, G], BF16)
        sil = A.alloc("sil", [128, 2, G], F32)
        rr = A.alloc("rr", [128, 2, D], F32)
        gb = A.alloc("gb", [128, 2, D], F32)
        st6 = A.alloc("st6", [128, 2, 2, 6], F32)
        mv = A.alloc("mv", [128, 2, 8], F32)
        load_weight_cast(W1, w1_d, "W1", nsplit=8)
        load_weight_cast(W2, w2_d, "W2", nsplit=4)
        load_gb(gb, gname, bname)
        ng = len(groups)

        def stage_T(g):
            load_transposed(groups[g][0], xin, xT, g % 2, 0)

        def stage_A(g):
            slot = g % 2
            for hc in range(22):
                ba = (hc % 2) * 2
                bu = ba + 1
                for k in range(8):
                    P.op("pe", lambda e, k=k, hc=hc, ba=ba: e.matmul(ps[:, ba, :], lhsT=W1[:, k, hc * 128:(hc + 1) * 128],
                                                                     rhs=xT[:, slot, k, :], start=(k == 0), stop=(k == 7)),
                         reads=["W1", ("xT", slot)], writes=[bank(ba)])
                for k in range(8):
                    P.op("pe", lambda e, k=k, hc=hc, bu=bu: e.matmul(ps[:, bu, :], lhsT=W1[:, k, DFF + hc * 128:DFF + (hc + 1) * 128],
                                                                     rhs=xT[:, slot, k, :], start=(k == 0), stop=(k == 7)),
                         reads=["W1", ("xT", slot)], writes=[bank(bu)])
                ss = hc % 2
                P.op("act", lambda e, ss=ss, ba=ba: e.activation(out=sil[:, ss, :], in_=ps[:, ba, :], func=AF.Silu),
                     reads=[bank(ba)], writes=[("sil", ss)])
                P.op("dve", lambda e, ss=ss, bu=bu, hc=hc: e.scalar_tensor_tensor(out=hT[:, hc, :], in0=sil[:, ss, :], scalar=0.5,
                                                                                  in1=ps[:, bu, :], op0=ALU.mult, op1=ALU.mult),
                     reads=[("sil", ss), bank(bu)], writes=["hT"])

        def stage_B(g):
            src, dst = groups[g]
            for t in range(4):
                rs = t % 2
                b0 = 4 + (t % 2) * 2
                P.op("sp", lambda e, t=t, rs=rs: e.dma_start(out=rr[:, rs, :], in_=src[t * 128:(t + 1) * 128, :]),
                     writes=[("rr", rs)], dma_key="rr%d" % rs)
                for half in range(2):
                    for k in range(22):
                        P.op("pe", lambda e, k=k, t=t, half=half, b0=b0: e.matmul(ps[:, b0 + half, :], lhsT=hT[:, k, t * 128:(t + 1) * 128],
                                                                                  rhs=W2[:, k, half * 512:(half + 1) * 512],
                                                                                  start=(k == 0), stop=(k == 21)),
                             reads=["hT", "W2"], writes=[bank(b0 + half)])
                psy = ps[:, b0:b0 + 2, :].rearrange("p a b -> p (a b)")
                P.op("dve", lambda e, rs=rs, psy=psy: e.scalar_tensor_tensor(out=rr[:, rs, :], in0=rr[:, rs, :], scalar=ALPHA, in1=psy,
                                                                            op0=ALU.mult, op1=ALU.add),
                     reads=[("rr", rs), bank(b0), bank(b0 + 1)], writes=[("rr", rs)])
                layer_norm_tile(rr, rs, gb, st6, mv, ("rr", rs))
                P.op("pool", lambda e, t=t, rs=rs: e.dma_start(out=dst[t * 128:(t + 1) * 128, :], in_=rr[:, rs, :]),
                     reads=[("rr", rs)], dma_key="rrst%d" % rs, queue="pool")

        stage_T(0)
        for g in range(ng):
            stage_A(g)
            if g + 1 < ng:
                stage_T(g + 1)
            stage_B(g)
        P.barrier()


    def proj_phase():
        A.reset()
        Wi = A.alloc("Wi", [128, 8, IN_COLS], BF16)
        wuq = A.alloc("wuq", [128, 3, 768], BF16)
        wuq_rh = A.alloc("wuqrh", [128, 3, 256], BF16)
        wuk = A.alloc("wuk", [128, 2, 512], BF16)
        wuv = A.alloc("wuv", [128, 2, 512], BF16)
        wkr_rh = A.alloc("wkrrh", [128, 8, 32], BF16)
        qg = A.alloc("qg", [128, 4], F32)
        kvg = A.alloc("kvg", [128, 2], F32)
        bg = A.alloc("bg", [128, 16], F32)
        xin = A.alloc("xin", [128, 4, D], F32)
        epsc = A.alloc("epsc", [128, 1], F32)
        xT = A.alloc("xT", [128, 2, 8, G], BF16)
        cq = A.alloc("cq", [128, 3, G], F32)
        sq = A.alloc("sq", [128, 3, G], F32)
        ckr = A.alloc("ckr", [128, 2, G], F32)
        skv = A.alloc("skv", [128, 2, G], F32)
        cqn = A.alloc("cqn", [128, 2, 3, G], BF16)
        ckvn = A.alloc("ckvn", [128, 2, 2, G], BF16)
        rstd = A.alloc("rstd", [128, 2, G], F32)
        cs = A.alloc("cs", [128, 2, 2, G], F32)
        t1 = A.alloc("t1", [128, 2, G], F32)
        t2 = A.alloc("t2", [128, 2, G], F32)
        naqs = A.alloc("naqs", [128, 4, G], BF16)
        naks = A.alloc("naks", [128, 4, G], BF16)
        navs = A.alloc("navs", [128, 4, 512], BF16)
        off_qst = A.cur
        qst = A.alloc("qst", [128, 8, G], BF16)
        kst = A.alloc("kst", [128, 4, G], BF16)
        vst = A.alloc("vst", [128, 4, 512], BF16)
        krs = A.alloc("krs", [128, G], BF16)
        off_gst = A.cur
        gst = A.alloc("gst", [128, 16, G], BF16)
        stg_q = nc.alloc_sbuf_tensor_at("stgq_alias", [128, 3, 768], F32, offset=off_gst)
        stg_kv = nc.alloc_sbuf_tensor_at("stgkv_alias", [128, 2, 1024], F32, offset=off_qst)

        def col_load(dst, src1d, n, key):
            for k in range(n):
                P.op("sp", lambda e, k=k: e.dma_start(out=dst[:, k:k + 1], in_=src1d[k * 128:(k + 1) * 128].rearrange("(p o) -> p o", o=1)),
                     writes=[key], dma_key=key + "_ld")

        P.op("pool", lambda e: e.memset(epsc[:], RMS_EPS), writes=["epsc"])
        load_weight_cast(Wi, w_in, "Wi", nsplit=8)
        col_load(qg, q_norm_g, 3, "qg")
        col_load(kvg, kv_norm_g, 2, "kvg")
        col_load(bg, b_gate, 16, "bg")
        P.op("sp", lambda e: e.dma_start(out=stg_q[:], in_=w_uq.rearrange("(k p) n -> p k n", p=128)), writes=["gst"], dma_key="stgq")
        P.op("sp", lambda e: e.dma_start(out=stg_kv[:], in_=w_ukv.rearrange("(k p) n -> p k n", p=128)), writes=["qst"], dma_key="stgkv")
        for k in range(3):
            P.op("dve", lambda e, k=k: e.tensor_scalar(out=wuq[:, k, :], in0=stg_q[:, k, :], scalar1=qg[:, k:k + 1], scalar2=None, op0=ALU.mult),
                 reads=["gst", "qg"], writes=["wuq"])
            v = wuq[:, k, :].rearrange("p (h t) -> p h t", t=96)
            o = wuq_rh[:, k, :].rearrange("p (h t) -> p h t", t=32)
            P.op("act", lambda e, v=v, o=o: e.mul(out=o[:, :, 0:16], in_=v[:, :, 80:96], mul=-1.0), reads=["wuq"], writes=["wuqrh"])
            P.op("act", lambda e, v=v, o=o: e.copy(out=o[:, :, 16:32], in_=v[:, :, 64:80]), reads=["wuq"], writes=["wuqrh"])
        for k in range(2):
            sv = stg_kv[:, k, :].rearrange("p (h t) -> p h t", t=128)
            P.op("dve", lambda e, k=k, sv=sv: e.tensor_scalar(out=wuk[:, k, :].rearrange("p (h d) -> p h d", d=64), in0=sv[:, :, 0:64],
                                                              scalar1=kvg[:, k:k + 1], scalar2=None, op0=ALU.mult),
                 reads=["qst", "kvg"], writes=["wuk"])
            P.op("dve", lambda e, k=k, sv=sv: e.tensor_scalar(out=wuv[:, k, :].rearrange("p (h d) -> p h d", d=64), in0=sv[:, :, 64:128],
                                                              scalar1=kvg[:, k:k + 1], scalar2=None, op0=ALU.mult),
                 reads=["qst", "kvg"], writes=["wuv"])
        KR0 = 2176
        P.op("act", lambda e: e.mul(out=wkr_rh[:, :, 0:16], in_=Wi[:, :, KR0 + 16:KR0 + 32], mul=-1.0), reads=["Wi"], writes=["wkrrh"])
        P.op("act", lambda e: e.copy(out=wkr_rh[:, :, 16:32], in_=Wi[:, :, KR0:KR0 + 16]), reads=["Wi"], writes=["wkrrh"])

        rot = [4]

        def nb():
            b = rot[0]
            rot[0] = 4 + (rot[0] - 3) % 4
            return b

        evt = [0]

        def evac(dst, src, rd, wr, scale=None, eng=None):
            if eng is None:
                eng = "act" if evt[0] % 2 == 0 else "dve"
                evt[0] += 1
            if eng == "act":
                if scale is None:
                    P.op("act", lambda e: e.activation(out=dst, in_=src, func=AF.Copy), reads=rd, writes=wr)
                else:
                    P.op("act", lambda e: e.activation(out=dst, in_=src, func=AF.Copy, scale=scale), reads=rd, writes=wr)
            else:
                assert scale is None
                P.op("dve", lambda e: e.tensor_copy(out=dst, in_=src), reads=rd, writes=wr)

        def mm8(out, c0, c1, slot, b):
            for k in range(8):
                P.op("pe", lambda e, k=k: e.matmul(out, lhsT=Wi[:, k, c0:c1], rhs=xT[:, slot, k, :], start=(k == 0), stop=(k == 7)),
                     reads=["Wi", ("xT", slot)], writes=[bank(b)])

        def rms_mm(c0, nch, slot, raw, sqb, rkey, skey):
            for ci in range(nch):
                b = nb()
                mm8(ps[:, b, :], c0 + ci * 128, c0 + (ci + 1) * 128, slot, b)
                P.op("act", lambda e, ci=ci, b=b: e.activation(out=raw[:, ci, :], in_=ps[:, b, :], func=AF.Copy), reads=[bank(b)], writes=[rkey])
                P.op("act", lambda e, ci=ci, b=b: e.activation(out=sqb[:, ci, :], in_=ps[:, b, :], func=AF.Square), reads=[bank(b)], writes=[skey])

        def rms_fin(nch, dim, rslot, raw, sqb, rkey, skey, dstn, dkey):
            b = nb()
            for ci in range(nch):
                P.op("pe", lambda e, ci=ci, b=b: e.matmul(ps[:, b, :], lhsT=ones_f[:], rhs=sqb[:, ci, :], start=(ci == 0), stop=(ci == nch - 1)),
                     reads=["onesf", skey], writes=[bank(b)])
            rk = ("rstd", rslot)
            P.op("act", lambda e, b=b: e.activation(out=rstd[:, rslot, :], in_=ps[:, b, :], func=AF.Ln, scale=1.0 / dim, bias=epsc[:, 0:1]),
                 reads=[bank(b), "epsc"], writes=[rk])
            P.op("act", lambda e: e.activation(out=rstd[:, rslot, :], in_=rstd[:, rslot, :], func=AF.Exp, scale=-0.5), reads=[rk], writes=[rk])
            for ci in range(nch):
                P.op("dve" if ci % 2 == 0 else "pool", lambda e, ci=ci: e.tensor_tensor(out=dstn[:, ci, :], in0=raw[:, ci, :], in1=rstd[:, rslot, :], op=ALU.mult),
                     reads=[rkey, rk], writes=[dkey])

        def st(dst, src, rd, key):
            P.op("pool", lambda e: e.dma_start(out=dst, in_=src), reads=rd, dma_key=key, queue="pool")

        def info(gi):
            kind, g = all_groups[gi]
            return kind, g, own_off(kind, g), tall_off(kind, g), gi % 2

        def stage_T(gi):
            kind, g, own, toff, sl = info(gi)
            load_transposed(x1_d[toff:toff + G, :], xin, xT, sl, 0)
            rope_src = ropep[:, :, g * G:(g + 1) * G] if kind == "p" else ropes[:, :, (g % c.NSG) * G:(g % c.NSG + 1) * G]
            for i in range(2):
                P.op("sp", lambda e, i=i: e.dma_start(out=cs[64:96, sl, i, :], in_=rope_src[i]), writes=[("cs", sl)], dma_key="cs%d" % sl)

        def stage_Amm(gi):
            kind, g, own, toff, sl = info(gi)
            if own is not None:
                rms_mm(1536, 3, sl, cq, sq, "cq", "sq")
            rms_mm(1920, 2, sl, ckr, skv, "ckr", "skv")
            b1 = nb()
            mm8(ps[64:96, b1, :], KR0, KR0 + 32, sl, b1)
            b2 = nb()
            for k in range(8):
                P.op("pe", lambda e, k=k, b2=b2: e.matmul(ps[64:96, b2, :], lhsT=wkr_rh[:, k, :], rhs=xT[:, sl, k, :], start=(k == 0), stop=(k == 7)),
                     reads=["wkrrh", ("xT", sl)], writes=[bank(b2)])
            P.op("dve", lambda e, b1=b1: e.tensor_tensor(out=t1[64:96, 0, :], in0=ps[64:96, b1, :], in1=cs[64:96, sl, 0, :], op=ALU.mult),
                 reads=[bank(b1), ("cs", sl)], writes=[("t1", 0)])
            P.op("dve", lambda e, b2=b2: e.tensor_tensor(out=t2[64:96, 0, :], in0=ps[64:96, b2, :], in1=cs[64:96, sl, 1, :], op=ALU.mult),
                 reads=[bank(b2), ("cs", sl)], writes=[("t2", 0)])
            P.op("pool", lambda e: e.tensor_tensor(out=krs[64:96, :], in0=t1[64:96, 0, :], in1=t2[64:96, 0, :], op=ALU.add),
                 reads=[("t1", 0), ("t2", 0)], writes=["krs"])
            st(kr_d[:, toff:toff + G], krs[64:96, :], ["krs"], "krs_st")

        def stage_Afin(gi):
            kind, g, own, toff, sl = info(gi)
            if own is not None:
                rms_fin(3, 384.0, 0, cq, sq, "cq", "sq", cqn[:, sl], ("cqn", sl))
            rms_fin(2, 256.0, 1, ckr, skv, "ckr", "skv", ckvn[:, sl], ("ckvn", sl))

        def stage_B1(gi):
            kind, g, own, toff, sl = info(gi)
            halo_a = kind == "p" and g == c.NOG
            halo_b = kind == "p" and g == c.NPG - 1
            if own is not None:
                for ci in range(4):
                    b = nb()
                    mm8(ps[:, b, :], ci * 128, (ci + 1) * 128, sl, b)
                    evac(naqs[:, ci, :], ps[:, b, :], [bank(b)], ["naqs"], scale=0.125, eng="act")
                st(naq_d[:, :, own:own + G], naqs[:], ["naqs"], "naqs_st")
            if own is not None or halo_a or halo_b:
                for ci in range(4):
                    b = nb()
                    mm8(ps[:, b, :], 512 + ci * 128, 512 + (ci + 1) * 128, sl, b)
                    evac(naks[:, ci, :], ps[:, b, :], [bank(b)], ["naks"])
                for t in range(4):
                    b = nb()
                    for k in range(8):
                        P.op("pe", lambda e, k=k, t=t, b=b: e.matmul(ps[:, b, :], lhsT=xT[:, sl, k, t * 128:(t + 1) * 128], rhs=Wi[:, k, 1024:1536],
                                                                     start=(k == 0), stop=(k == 7)),
                             reads=["Wi", ("xT", sl)], writes=[bank(b)])
                    evac(navs[:, t, :], ps[:, b, :], [bank(b)], ["navs"])
                if kind == "s":
                    st(naks_d[:, :, g * G:(g + 1) * G], naks[:], ["naks"], "naks_st")
                    st(navs_d[g * G:(g + 1) * G, :].rearrange("(t p) d -> p t d", p=128), navs[:], ["navs"], "navs_st")
                elif own is not None:
                    st(nakp_d[:, :, 256 + own:256 + own + G], naks[:], ["naks"], "naks_st")
                    st(navp_d[256 + own:256 + own + G, :].rearrange("(t p) d -> p t d", p=128), navs[:], ["navs"], "navs_st")
                elif halo_a:
                    st(nakp_d[:, :, 256 + QP:256 + QP + 256], naks[:, :, 0:256], ["naks"], "naks_st")
                    st(navp_d[256 + QP:256 + QP + 256, :].rearrange("(t p) d -> p t d", p=128), navs[:, 0:2, :], ["navs"], "navs_st")
                else:
                    st(nakp_d[:, :, 0:256], naks[:, :, 256:512], ["naks"], "naks_st")
                    st(navp_d[0:256, :].rearrange("(t p) d -> p t d", p=128), navs[:, 2:4, :], ["navs"], "navs_st")
            for hp in range(4):
                b = nb()
                for hh in range(2):
                    h = 2 * hp + hh
                    for k in range(2):
                        P.op("pe", lambda e, k=k, h=h, hh=hh, b=b: e.matmul(ps[hh * 64:(hh + 1) * 64, b, :], lhsT=wuk[:, k, h * 64:(h + 1) * 64],
                                                                            rhs=ckvn[:, sl, k, :], start=(k == 0), stop=(k == 1)),
                             reads=["wuk", ("ckvn", sl)], writes=[bank(b)])
                evac(kst[:, hp, :], ps[:, b, :], [bank(b)], ["kst"])
            st(kt_d[:, :, toff:toff + G].rearrange("(c a) d t -> (a d) c t", a=2), kst[:], ["kst"], "kst_st")

        def stage_B2(gi):
            kind, g, own, toff, sl = info(gi)
            for t in range(4):
                b = nb()
                for k in range(2):
                    P.op("pe", lambda e, k=k, t=t, b=b: e.matmul(ps[:, b, :], lhsT=ckvn[:, sl, k, t * 128:(t + 1) * 128], rhs=wuv[:, k, :],
                                                                 start=(k == 0), stop=(k == 1)),
                         reads=["wuv", ("ckvn", sl)], writes=[bank(b)])
                evac(vst[:, t, :], ps[:, b, :], [bank(b)], ["vst"])
            st(v_d[toff:toff + G, :].rearrange("(t p) d -> p t d", p=128), vst[:], ["vst"], "vst_st")
            if own is None:
                return
            for h in range(8):
                b = nb()
                for k in range(3):
                    P.op("pe", lambda e, k=k, h=h, b=b: e.matmul(ps[0:96, b, :], lhsT=wuq[:, k, h * 96:(h + 1) * 96], rhs=cqn[:, sl, k, :],
                                                                 start=(k == 0), stop=(k == 2)),
                         reads=["wuq", ("cqn", sl)], writes=[bank(b)])
                b2 = nb()
                for k in range(3):
                    P.op("pe", lambda e, k=k, h=h, b2=b2: e.matmul(ps[64:96, b2, :], lhsT=wuq_rh[:, k, h * 32:(h + 1) * 32], rhs=cqn[:, sl, k, :],
                                                                   start=(k == 0), stop=(k == 2)),
                         reads=["wuqrh", ("cqn", sl)], writes=[bank(b2)])
                ts = h % 2
                P.op("act", lambda e, h=h, b=b: e.activation(out=qst[0:64, h, :], in_=ps[0:64, b, :], func=AF.Copy), reads=[bank(b)], writes=["qst"])
                P.op("dve", lambda e, b=b, ts=ts: e.tensor_tensor(out=t1[64:96, ts, :], in0=ps[64:96, b, :], in1=cs[64:96, sl, 0, :], op=ALU.mult),
                     reads=[bank(b), ("cs", sl)], writes=[("t1", ts)])
                P.op("dve", lambda e, b2=b2, ts=ts: e.tensor_tensor(out=t2[64:96, ts, :], in0=ps[64:96, b2, :], in1=cs[64:96, sl, 1, :], op=ALU.mult),
                     reads=[bank(b2), ("cs", sl)], writes=[("t2", ts)])
                P.op("pool", lambda e, h=h, ts=ts: e.tensor_tensor(out=qst[64:96, h, :], in0=t1[64:96, ts, :], in1=t2[64:96, ts, :], op=ALU.add),
                     reads=[("t1", ts), ("t2", ts)], writes=["qst"])
            st(qt_d[:, :, own:own + G].rearrange("h d t -> d h t"), qst[0:96, :, :], ["qst"], "qst_st")
            for ci in range(16):
                b = nb()
                mm8(ps[:, b, :], 2208 + ci * 128, 2208 + (ci + 1) * 128, sl, b)
                P.op("act", lambda e, ci=ci, b=b: e.activation(out=gst[:, ci, :], in_=ps[:, b, :], func=AF.Sigmoid, bias=bg[:, ci:ci + 1]),
                     reads=[bank(b), "bg"], writes=["gst"])
            st(gt_d[:, :, own:own + G], gst[:], ["gst"], "gst_st")

        ng = len(all_groups)
        stage_T(0)
        stage_Amm(0)
        stage_Afin(0)
        for gi in range(ng):
            if gi + 1 < ng:
                stage_T(gi + 1)
            stage_B1(gi)
            if gi + 1 < ng:
                stage_Amm(gi + 1)
            stage_B2(gi)
            if gi + 1 < ng:
                stage_Afin(gi + 1)
        P.barrier()

    def na_phase():
        A.reset()
        NKMAX = max(QP + 512, SS)
        NQMAX = max(QP, SS)
        TTi = A.alloc("TTi", [128, 8, 5, 128], BF16)
        TTg = A.alloc("TTg", [128, 8, 7, 128], BF16)
        tzs = A.alloc("tzs", [128, 2, 7, 128], F32)
        mi = A.alloc("mi", [128, 5, 128], F32)
        mc = A.alloc("mc", [128, 128], F32)
        NRM = 3 * 4 * 7 * 128
        rm = A.alloc("rm", [2, NRM + 128], BF16)
        nak = A.alloc("nak", [128, 4, NKMAX], BF16)
        nav1 = A.alloc("nav1", [128, NKMAX // 128, 8, 128], BF16)
        naq = A.alloc("naq", [128, 4, NQMAX], BF16)
        nao = A.alloc("nao", [128, 2, 4, 128], BF16)
        PTn = A.alloc("PTn", [128, 2, 896], BF16)
        recn = A.alloc("recn", [128, 2, 128], F32)
        nv5 = nav1[:].rearrange("p c (hp two) d -> p c hp two d", two=2)
        P.op("pool", lambda e: e.memset(nv5[:, :, :, 0, 64:128], 1.0), writes=["nav1"])
        P.op("pool", lambda e: e.memset(nv5[:, :, :, 1, 0:64], 1.0), writes=["nav1"])
        P.op("sp", lambda e: e.dma_start(out=mi[:], in_=mint[:, :, :]), writes=["mi"], dma_key="mi")
        P.op("sp", lambda e: e.dma_start(out=mc[:], in_=mcol[:, :]), writes=["mc"], dma_key="mc")
        P.op("pool", lambda e: e.dma_start(out=rm[:], in_=rmask[:, :]), writes=["rm"], dma_key="rm", queue="pool")
        for h in range(8):
            sl = h % 2
            for a in range(2):
                for b in range(2):
                    dr0 = 1 + a - b
                    P.op("sp", lambda e, h=h, a=a, b=b, dr0=dr0, sl=sl: e.dma_start(
                        out=tzs[a * 64:(a + 1) * 64, sl, :, b * 64:(b + 1) * 64], in_=tz[h, dr0, :, :, :]),
                        writes=[("tzs", sl)], dma_key="tzs%d" % sl, group=True)
            for dd in range(7):
                P.op("dve", lambda e, h=h, dd=dd, sl=sl: e.tensor_tensor(out=TTg[:, h, dd, :], in0=tzs[:, sl, dd, :], in1=mc[:], op=ALU.add),
                     reads=[("tzs", sl), "mc"], writes=["TTg"])
            for di in range(5):
                P.op("dve", lambda e, h=h, di=di, sl=sl: e.tensor_tensor(out=TTi[:, h, di, :], in0=tzs[:, sl, di + 1, :], in1=mi[:, di, :], op=ALU.add),
                     reads=[("tzs", sl), "mi"], writes=["TTi"])

        blocks = [(0, "p", c.RP, 0, QP, nakp_d, navp_d, 0, QP + 512, 2)]
        for i in range(NS):
            blocks.append((1 + i, "s", c.RS, QP + i * SS, SS, naks_d, navs_d, i * SS, SS, 0))
        pcount = [0]
        for (blk, kind, R, q_off, NQ, kd, vd, koff, NK, joff) in blocks:
            NP = R // 2
            for ci in range(4):
                P.op("sp", lambda e, ci=ci, kd=kd, koff=koff, NK=NK: e.dma_start(out=nak[:, ci, 0:NK], in_=kd[:, ci, koff:koff + NK]),
                     writes=["nak"], dma_key="nak")
                P.op("sp", lambda e, ci=ci, q_off=q_off, NQ=NQ: e.dma_start(out=naq[:, ci, 0:NQ], in_=naq_d[:, ci, q_off:q_off + NQ]),
                     writes=["naq"], dma_key="naq")
            nch = NK // 128
            vsrc = vd[koff:koff + NK, :].rearrange("(c p) (hp two d) -> p c hp two d", p=128, two=2, d=64)
            for c0 in range(nch):
                for two in range(2):
                    P.op("sp", lambda e, c0=c0, two=two, vsrc=vsrc: e.dma_start(
                        out=nv5[:, c0, :, two, two * 64:two * 64 + 64], in_=vsrc[:, c0, :, two, :]),
                        writes=["nav1"], dma_key="nav", group=True)
            eps_ = edge_pairs(NP)

            def do_pair(p, blk, kind, R, joff, eps_, q_off):
                edge = p in eps_
                dls = delta_list(kind, p, R) if edge else [-2, -1, 0, 1, 2]
                n = len(dls)
                assert dls == list(range(dls[0], dls[0] + n))
                pi = eps_.index(p) if edge else -1
                osl = pcount[0] % 2
                pcount[0] += 1
                pieces = [(0, min(n, 4))] + ([(4, n)] if n > 4 else [])

                def S(h):
                    sb = (h % 2) * 2
                    hp = h % 2
                    for (i0, i1) in pieces:
                        out = ps[:, sb + i0 // 4, 0:(i1 - i0) * 128]
                        wr = [bank(sb + i0 // 4)]
                        if edge:
                            d0 = dls[0] + 3 + i0
                            rhs = TTg[:, h, d0:d0 + (i1 - i0), :].rearrange("p a b -> p (a b)")
                            P.op("pe", lambda e, out=out, rhs=rhs: e.matmul(out, lhsT=ident_b[:], rhs=rhs, start=True, stop=False),
                                 reads=["identb", "TTg"], writes=wr)
                            idx = ((blk * 4 + pi) * 7 + d0) * 128
                            P.op("pe", lambda e, out=out, idx=idx, i0=i0, i1=i1: e.matmul(out, lhsT=rm[0:2, NRM:NRM + 128],
                                                                                         rhs=rm[0:2, idx:idx + (i1 - i0) * 128], start=False, stop=False),
                                 reads=["rm"], writes=wr)
                        else:
                            d0 = dls[0] + 2 + i0
                            rhs = TTi[:, h, d0:d0 + (i1 - i0), :].rearrange("p a b -> p (a b)")
                            P.op("pe", lambda e, out=out, rhs=rhs: e.matmul(out, lhsT=ident_b[:], rhs=rhs, start=True, stop=False),
                                 reads=["identb", "TTi"], writes=wr)
                    for i, dl in enumerate(dls):
                        tok0 = (p + dl + joff) * 128
                        out = ps[:, sb + i // 4, (i % 4) * 128:(i % 4 + 1) * 128]
                        P.op("pe", lambda e, out=out, tok0=tok0, i=i: e.matmul(
                            out, lhsT=nak[hp * 64:(hp + 1) * 64, h // 2, tok0:tok0 + 128],
                            rhs=naq[hp * 64:(hp + 1) * 64, h // 2, p * 128:(p + 1) * 128], start=False, stop=(i == n - 1 or i == 3)),
                            reads=["nak", "naq"], writes=[bank(sb + i // 4)])
                    if n <= 4:
                        src = ps[:, sb, 0:n * 128]
                        rd = [bank(sb)]
                    else:
                        src = ps[:, sb:sb + 2, :].rearrange("p a b -> p (a b)")[:, 0:n * 128]
                        rd = [bank(sb), bank(sb + 1)]
                    P.op("act", lambda e, src=src: e.activation(out=PTn[:, h % 2, 0:n * 128], in_=src, func=AF.Exp),
                         reads=rd, writes=[("PTn", h % 2)])

                def PV(h):
                    ob = 4 + h % 4
                    oslot = (h // 4) + 2 * (p % 2)
                    cols = slice(oslot * 128, (oslot + 1) * 128)
                    okey = bank(ob)
                    nh = slice((h % 2) * 64, (h % 2) * 64 + 64)
                    dh = slice((1 - h % 2) * 64, (1 - h % 2) * 64 + 64)
                    for i, dl in enumerate(dls):
                        ch = p + dl + joff
                        P.op("pe", lambda e, i=i, ch=ch: e.matmul(ps[:, ob, cols], lhsT=nav1[:, ch, h, :],
                                                                 rhs=PTn[:, h % 2, i * 128:(i + 1) * 128], start=(i == 0), stop=(i == n - 1)),
                             reads=["nav1", ("PTn", h % 2)], writes=[okey])
                    if h % 2 == 0:
                        P.op("act", lambda e: e.activation(out=recn[nh, h % 2, :], in_=ps[dh, ob, cols], func=AF.Ln), reads=[okey], writes=[("recn", h % 2)])
                        P.op("act", lambda e: e.activation(out=recn[nh, h % 2, :], in_=recn[nh, h % 2, :], func=AF.Exp, scale=-1.0),
                             reads=[("recn", h % 2)], writes=[("recn", h % 2)])
                    else:
                        P.op("dve", lambda e: e.reciprocal(out=recn[nh, h % 2, :], in_=ps[dh, ob, cols]), reads=[okey], writes=[("recn", h % 2)])
                    P.op("dve", lambda e: e.tensor_tensor(out=nao[nh, osl, h // 2, :], in0=ps[nh, ob, cols], in1=recn[nh, h % 2, :],
                                                          op=ALU.mult), reads=[okey, ("recn", h % 2)], writes=[("nao", osl)])

                S(0)
                for h in range(8):
                    if h + 1 < 8:
                        S(h + 1)
                    PV(h)
                P.op("pool", lambda e: e.dma_start(out=nao_d[:, :, q_off + p * 128:q_off + (p + 1) * 128], in_=nao[:, osl, :, :]),
                     reads=[("nao", osl)], dma_key="nao_st%d" % osl, queue="pool")

            for p in range(NP):
                do_pair(p, blk, kind, R, joff, eps_, q_off)
        P.barrier()

    def mla_phase():
        A.reset()
        NKMAX = max(SP, SS)
        NQMAX = max(QP, SS)
        KT = A.alloc("KT", [128, 2, NKMAX], BF16)
        V1 = A.alloc("V1", [128, 2, NKMAX // 128, 128], BF16)
        QT = A.alloc("QT", [128, 2, G], BF16)
        PT = A.alloc("PT", [128, 3, 2 * G], BF16)
        mlao = A.alloc("mlao", [128, 4, NQMAX], BF16)
        rec = A.alloc("rec", [128, 2, G], F32)
        P.op("pool", lambda e: e.memset(V1[:, 0, :, 64:128], 1.0), writes=[("V1", 0)])
        P.op("pool", lambda e: e.memset(V1[:, 1, :, 0:64], 1.0), writes=[("V1", 1)])
        seqs = [(0, QP, 0, SP)] + [(QP + i * SS, SS, SP + i * SS, SS) for i in range(NS)]
        scale = 96.0 ** -0.5
        cnt = [0]
        for (q_off, NQ, k_off, NK) in seqs:
            nkc = NK // 128
            for sl in range(2):
                P.op("sp", lambda e, sl=sl, k_off=k_off, NK=NK: e.dma_start(out=KT[64:96, sl, 0:NK], in_=kr_d[:, k_off:k_off + NK]),
                     writes=[("KTr", sl)], dma_key="KTr%d" % sl)
            for h in range(8):
                sl = h % 2
                vc = 0 if sl == 0 else 64
                nsp = 4 if NK >= 2048 else 1
                stp = NK // nsp
                for i in range(nsp):
                    P.op("sp", lambda e, i=i, h=h, sl=sl, stp=stp, k_off=k_off: e.dma_start(
                        out=KT[0:64, sl, i * stp:(i + 1) * stp], in_=kt_d[h, :, k_off + i * stp:k_off + (i + 1) * stp]),
                        writes=[("KT", sl)], dma_key="KT%d" % sl)
                    P.op("sp", lambda e, i=i, h=h, sl=sl, stp=stp, k_off=k_off, vc=vc: e.dma_start(
                        out=V1[:, sl, i * stp // 128:(i + 1) * stp // 128, vc:vc + 64],
                        in_=v_d[k_off + i * stp:k_off + (i + 1) * stp, h * 64:(h + 1) * 64].rearrange("(c p) d -> p c d", p=128)),
                        writes=[("V1", sl)], dma_key="V1%d" % sl)
                def do_qg(h, sl, qg, q_off, nkc):
                    nh = slice(sl * 64, sl * 64 + 64)
                    dh = slice((1 - sl) * 64, (1 - sl) * 64 + 64)
                    qs = cnt[0] % 2
                    ob = 6 + cnt[0] % 2
                    cnt[0] += 1
                    P.op("sp", lambda e, h=h, qs=qs, qg=qg, q_off=q_off: e.dma_start(
                        out=QT[0:96, qs, :], in_=qt_d[h, :, q_off + qg * G:q_off + (qg + 1) * G]), writes=[("QT", qs)], dma_key="QT%d" % qs)

                    def QK2(kp):
                        sbp = (kp % 3) * 2
                        for i in range(2):
                            kc = 2 * kp + i
                            P.op("pe", lambda e, kc=kc, i=i: e.matmul(ps[:, sbp + i, :], lhsT=KT[0:96, sl, kc * 128:(kc + 1) * 128], rhs=QT[0:96, qs, :],
                                                                      start=True, stop=True),
                                 reads=[("KT", sl), ("KTr", sl), ("QT", qs)], writes=[bank(sbp + i)])
                        P.op("act", lambda e: e.activation(out=PT[:, kp % 3, :], in_=ps[:, sbp:sbp + 2, :].rearrange("p a b -> p (a b)"),
                                                           func=AF.Exp, scale=scale),
                             reads=[bank(sbp), bank(sbp + 1)], writes=[("PT", kp % 3)])

                    def PV2(kp):
                        for i in range(2):
                            kc = 2 * kp + i
                            P.op("pe", lambda e, kc=kc, i=i: e.matmul(ps[:, ob, :], lhsT=V1[:, sl, kc, :], rhs=PT[:, kp % 3, i * G:(i + 1) * G],
                                                                      start=(kc == 0), stop=(kc == nkc - 1)),
                                 reads=[("V1", sl), ("PT", kp % 3)], writes=[bank(ob)])

                    nkp = nkc // 2
                    QK2(0)
                    if nkp > 1:
                        QK2(1)
                    for kp in range(nkp):
                        if kp + 2 < nkp:
                            QK2(kp + 2)
                        PV2(kp)
                    rs = qs
                    P.op("dve", lambda e, rs=rs, ob=ob: e.reciprocal(out=rec[nh, rs, :], in_=ps[dh, ob, :]), reads=[bank(ob)], writes=[("rec", rs)])
                    P.op("dve", lambda e, rs=rs, ob=ob, h=h, qg=qg: e.tensor_tensor(out=mlao[nh, h // 2, qg * G:(qg + 1) * G], in0=ps[nh, ob, :],
                                                                                   in1=rec[nh, rs, :], op=ALU.mult),
                         reads=[bank(ob), ("rec", rs)], writes=["mlao"])

                for qg in range(NQ // G):
                    do_qg(h, sl, qg, q_off, nkc)
            P.op("pool", lambda e, q_off=q_off, NQ=NQ: e.dma_start(out=mlao_d[:, :, q_off:q_off + NQ], in_=mlao[:, :, 0:NQ]),
                 reads=["mlao"], dma_key="mlao_st", queue="pool")
        P.barrier()

    def mix_phase():
        A.reset()
        wna = A.alloc("wna", [128, 4, D], BF16)
        wml = A.alloc("wml", [128, 4, D], BF16)
        wo = A.alloc("wo", [128, 8, D], BF16)
        gb = A.alloc("gb", [128, 2, D], F32)
        nat = A.alloc("nat", [128, 2, 4, G], BF16)
        mlt = A.alloc("mlt", [128, 2, 4, G], BF16)
        gt = A.alloc("gt", [128, 2, 16, G], BF16)
        mT = A.alloc("mT", [128, 2, 8, G], BF16)
        ta = A.alloc("ta", [128, 2, G], F32)
        tb = A.alloc("tb", [128, 2, G], F32)
        rr = A.alloc("rr", [128, 4, D], F32)
        st6 = A.alloc("st6", [128, 4, 2, 6], F32)
        mv = A.alloc("mv", [128, 4, 8], F32)
        load_weight_cast(wna, w_na_o, "wna")
        load_weight_cast(wml, w_mla_o, "wml")
        load_weight_cast(wo, w_out, "wo", nsplit=2)
        load_gb(gb, "ln2_g", "ln2_b")
        own_groups = [(kind, g) for (kind, g) in all_groups if own_off(kind, g) is not None]

        def stage_C(i):
            kind, g = own_groups[i]
            o = own_off(kind, g)
            sg = i % 2
            P.op("sp", lambda e, o=o, sg=sg: e.dma_start(out=nat[:, sg], in_=nao_d[:, :, o:o + G]), writes=[("nat", sg)], dma_key="nat%d" % sg)
            P.op("sp", lambda e, o=o, sg=sg: e.dma_start(out=mlt[:, sg], in_=mlao_d[:, :, o:o + G]), writes=[("mlt", sg)], dma_key="mlt%d" % sg)
            P.op("sp", lambda e, o=o, sg=sg: e.dma_start(out=gt[:, sg], in_=gt_d[:, :, o:o + G]), writes=[("gt", sg)], dma_key="gt%d" % sg)
            for c8 in range(8):
                ba = (c8 % 2) * 2
                bb = ba + 1
                s2 = c8 % 2
                for k in range(4):
                    P.op("pe", lambda e, k=k, c8=c8, ba=ba, sg=sg: e.matmul(ps[:, ba, :], lhsT=wna[:, k, c8 * 128:(c8 + 1) * 128], rhs=nat[:, sg, k, :],
                                                                            start=(k == 0), stop=(k == 3)), reads=["wna", ("nat", sg)], writes=[bank(ba)])
                for k in range(4):
                    P.op("pe", lambda e, k=k, c8=c8, bb=bb, sg=sg: e.matmul(ps[:, bb, :], lhsT=wml[:, k, c8 * 128:(c8 + 1) * 128], rhs=mlt[:, sg, k, :],
                                                                            start=(k == 0), stop=(k == 3)), reads=["wml", ("mlt", sg)], writes=[bank(bb)])
                P.op("dve", lambda e, c8=c8, ba=ba, s2=s2, sg=sg: e.tensor_tensor(out=ta[:, s2, :], in0=ps[:, ba, :], in1=gt[:, sg, c8, :], op=ALU.mult),
                     reads=[bank(ba), ("gt", sg)], writes=[("ta", s2)])
                P.op("dve", lambda e, c8=c8, bb=bb, s2=s2, sg=sg: e.tensor_tensor(out=tb[:, s2, :], in0=ps[:, bb, :], in1=gt[:, sg, 8 + c8, :], op=ALU.mult),
                     reads=[bank(bb), ("gt", sg)], writes=[("tb", s2)])
                P.op("dve", lambda e, c8=c8, s2=s2, sg=sg: e.tensor_tensor(out=mT[:, sg, c8, :], in0=ta[:, s2, :], in1=tb[:, s2, :], op=ALU.add),
                     reads=[("ta", s2), ("tb", s2)], writes=[("mT", sg)])

        def stage_O(i):
            kind, g = own_groups[i]
            o = own_off(kind, g)
            xo = tall_off(kind, g)
            sg = i % 2
            for t in range(4):
                rs = t
                b0 = 4 + (t % 2) * 2
                P.op("sp", lambda e, t=t, rs=rs, xo=xo: e.dma_start(out=rr[:, rs, :], in_=x1_d[xo + t * 128:xo + (t + 1) * 128, :]),
                     writes=[("rr", rs)], dma_key="rr%d" % rs)
                for half in range(2):
                    for k in range(8):
                        P.op("pe", lambda e, k=k, t=t, half=half, b0=b0, sg=sg: e.matmul(ps[:, b0 + half, :], lhsT=mT[:, sg, k, t * 128:(t + 1) * 128],
                                                                                         rhs=wo[:, k, half * 512:(half + 1) * 512],
                                                                                         start=(k == 0), stop=(k == 7)),
                             reads=[("mT", sg), "wo"], writes=[bank(b0 + half)])
                psy = ps[:, b0:b0 + 2, :].rearrange("p a b -> p (a b)")
                P.op("dve", lambda e, rs=rs, psy=psy: e.scalar_tensor_tensor(out=rr[:, rs, :], in0=rr[:, rs, :], scalar=ALPHA, in1=psy,
                                                                            op0=ALU.mult, op1=ALU.add),
                     reads=[("rr", rs), bank(b0), bank(b0 + 1)], writes=[("rr", rs)])
                layer_norm_tile(rr, rs, gb, st6, mv, ("rr", rs))
                P.op("pool", lambda e, t=t, rs=rs, o=o: e.dma_start(out=x2_d[o + t * 128:o + (t + 1) * 128, :], in_=rr[:, rs, :]),
                     reads=[("rr", rs)], dma_key="rrst%d" % rs, queue="pool")

        nog = len(own_groups)
        stage_C(0)
        for i in range(nog):
            if i + 1 < nog:
                stage_C(i + 1)
            stage_O(i)
        P.barrier()

    if "P1" in phases:
        groups = [(x_src(k, g), x1_d[tall_off(k, g):tall_off(k, g) + G, :]) for (k, g) in all_groups]
        ffn_phase(groups, w_ffn1_in, w_ffn1_out, "ln1_g", "ln1_b")
    if "P2" in phases:
        proj_phase()
    if "P3" in phases:
        na_phase()
    if "P4" in phases:
        mla_phase()
    if "P5" in phases:
        mix_phase()
    if "P6" in phases:
        groups = []
        for (k, g) in all_groups:
            o = own_off(k, g)
            if o is None:
                continue
            dst = yp[o:o + G, :] if k == "p" else ys[o - QP:o - QP + G, :]
            groups.append((x2_d[o:o + G, :], dst))
        ffn_phase(groups, w_ffn2_in, w_ffn2_out, "ln3_g", "ln3_b")

    P.emit()
    return nc


def window_valid(qr, kr, R, top, bottom):
    ws = qr - 4
    if top:
        ws = max(ws, 0)
    if bottom:
        ws = min(ws, R - 8)
    return ws <= kr < ws + 8


def edge_pairs(NP):
    return [0, 1, NP - 2, NP - 1]


def delta_list(kind, p, R):
    NP = R // 2
    if kind == "p":
        variants = [(False, False), (True, False), (False, True)]
        jmin, jmax = -2, NP + 1
    else:
        variants = [(True, True)]
        jmin, jmax = 0, NP - 1
    out = []
    for dl in range(-3, 4):
        j = p + dl
        if j < jmin or j > jmax:
            continue
        ok = False
        for (top, bottom) in variants:
            for a in range(2):
                for b in range(2):
                    kr, qr = 2 * j + a, 2 * p + b
                    if top and kr < 0:
                        continue
                    if bottom and kr >= R:
                        continue
                    if window_valid(qr, kr, R, top, bottom):
                        ok = True
        if ok:
            out.append(dl)
    return out


def host_constants(cfg, core):
    c = cfg
    qtr = core % 4
    kc = np.arange(64)[:, None]
    cc = np.arange(64)[None, :]
    ws = np.clip(cc - 8, 0, 48)
    colv = (kc >= ws) & (kc < ws + 16)
    mcol64 = np.where(colv, 0.0, NEG).astype(np.float32)
    mcol = np.tile(mcol64, (2, 2))
    mint = np.zeros((128, 5, 128), np.float32)
    for di in range(5):
        dl = di - 2
        for a in range(2):
            for b in range(2):
                rv = -4 <= 2 * dl + a - b <= 3
                blk = mcol64 if rv else np.full((64, 64), NEG, np.float32)
                mint[a * 64:(a + 1) * 64, di, b * 64:(b + 1) * 64] = blk
    rm = np.zeros((2, 3, 4, 7, 128), np.float32)
    for blk in range(3):
        if blk == 0:
            R, top, bottom, kind = c.RP, qtr == 0, qtr == 3, "p"
        else:
            R, top, bottom, kind = c.RS, True, True, "s"
        NP = R // 2
        for pi, p in enumerate(edge_pairs(NP)):
            for di in range(7):
                dl = di - 3
                for a in range(2):
                    for b in range(2):
                        kr, qr = 2 * (p + dl) + a, 2 * p + b
                        v = window_valid(qr, kr, R, top, bottom)
                        rm[a, blk, pi, di, b * 64:(b + 1) * 64] = 0.0 if v else NEG
    inv = (1.0 / (10000.0 ** (np.arange(0, 32, 2, dtype=np.float32) / np.float32(32)))).astype(np.float32)

    def rope(pos):
        ang = pos.astype(np.float32)[:, None] * inv[None, :]
        cs = np.cos(ang).astype(np.float32).T
        sn = np.sin(ang).astype(np.float32).T
        return np.stack([np.concatenate([cs, cs], 0), np.concatenate([sn, sn], 0)], 0)

    posp = (np.arange(c.SP) + qtr * c.QP) % c.SP
    return {
        "mcol": mcol, "mint": mint,
        "rmask": np.ascontiguousarray(np.concatenate([rm.reshape(2, -1), np.kron(np.eye(2, dtype=np.float32), np.ones((1, 64), np.float32))], 1)),
        "ropep": np.ascontiguousarray(rope(posp)), "ropes": np.ascontiguousarray(rope(np.arange(c.SS))),
    }


def make_in_maps(inputs, cfg, used=None):
    c = cfg
    x_prompt = np.asarray(inputs["x_prompt"], np.float32)
    x_sample = np.asarray(inputs["x_sample"], np.float32)
    rpb = np.asarray(inputs["na_rpb"], np.float32)[0]
    kc = np.arange(64)[:, None]
    cc = np.arange(64)[None, :]
    tz15 = rpb[:, :, np.clip(kc - cc + 15, 0, 30)]
    tz = np.ascontiguousarray(np.stack([tz15[:, m:m + 13:2] for m in range(3)], axis=1).transpose(0, 1, 3, 2, 4))
    shared = {
        "ffn1_w_in": inputs["ffn1_w_in"][0], "ffn1_w_out": inputs["ffn1_w_out"][0],
        "ffn2_w_in": inputs["ffn2_w_in"][0], "ffn2_w_out": inputs["ffn2_w_out"][0],
        "ln1_g": inputs["ln1_g"], "ln1_b": inputs["ln1_b"], "ln2_g": inputs["ln2_g"], "ln2_b": inputs["ln2_b"],
        "ln3_g": inputs["ln3_g"], "ln3_b": inputs["ln3_b"],
        "w_in": inputs["w_in"][0], "b_gate": inputs["b_gate"][0], "q_norm_g": inputs["q_norm_g"][0],
        "kv_norm_g": inputs["kv_norm_g"][0], "w_uq": inputs["w_uq"][0], "w_ukv": inputs["w_ukv"][0],
        "w_na_o": inputs["w_na_o"][0], "w_mla_o": inputs["w_mla_o"][0], "w_out": inputs["w_out"][0], "tz": tz,
    }
    shared = {k: np.ascontiguousarray(np.asarray(v, np.float32)) for k, v in shared.items()}
    maps = []
    for core in range(NCORES):
        b, qtr = core // 4, core % 4
        m = dict(shared)
        m["xp"] = np.ascontiguousarray(np.roll(x_prompt[b], -qtr * c.QP, axis=0))
        m["xs"] = np.ascontiguousarray(x_sample[c.NS * core:c.NS * (core + 1)].reshape(c.NS * c.SS, D))
        m.update(host_constants(c, core))
        if used is not None:
            m = {k: v for k, v in m.items() if k in used}
        maps.append(m)
    return maps


_CACHE = {}


def kernel(**inputs):
    cfg = Cfg()
    if "nc" not in _CACHE:
        _CACHE["nc"] = build_program(cfg)
    nc = _CACHE["nc"]
    maps = make_in_maps(inputs, cfg)
    res = run_bass_kernel_spmd(nc, maps, core_ids=list(range(NCORES)))
    y_prompt = np.empty((2, cfg.SP, D), np.float32)
    y_sample = np.empty((NCORES * cfg.NS, cfg.SS, D), np.float32)
    for core in range(NCORES):
        b, qtr = core // 4, core % 4
        r = res.results[core]
        y_prompt[b, qtr * cfg.QP:(qtr + 1) * cfg.QP] = np.asarray(r["yp"], np.float32)
        y_sample[cfg.NS * core:cfg.NS * (core + 1)] = np.asarray(r["ys"], np.float32).reshape(cfg.NS, cfg.SS, D)
    return (y_prompt, y_sample)
```
